# Optimizing a Trainium2 kernel written in Bass

```python
import jax, jax.numpy as jnp
from jax import lax
import numpy as np

D_MODEL = 2048
BATCH = 4
SEQ = 2048
DEPTH = 2

CTX_LEN = 256
GRID_W = 64
HEAD_DIM = 128
D_MIX = D_MODEL
N_NA_HEADS = 12
NA_WIDTH = N_NA_HEADS * HEAD_DIM
N_FOUR_GROUPS = 4
FOUR_DIM = 128
FOUR_WIDTH = N_FOUR_GROUPS * FOUR_DIM
IN_WIDTH = 3 * NA_WIDTH + FOUR_WIDTH
WIN_ROWS = 8
WIN_COLS = 16
D_FF = 5632
N_MOD = 9
EPS = 1e-6

kernel_name = "hybrid_na_fnet_macaron_dit_block"


def _rms(x, w):
    xf = x.astype(jnp.float32)
    y = xf * lax.rsqrt(jnp.mean(xf * xf, axis=-1, keepdims=True) + EPS)
    return (y * w.astype(jnp.float32)).astype(x.dtype)


def _modulate(xn, shift, scale):
    return xn * (1 + scale) + shift


def _swiglu(h, wi, wo):
    g, u = jnp.split(h @ wi, 2, axis=-1)
    return (jax.nn.silu(g) * u) @ wo


def _heads(t, gain=None):
    B, L, _ = t.shape
    t = t.reshape(B, L, N_NA_HEADS, HEAD_DIM)
    return t if gain is None else _rms(t, gain)


def _fourier(f, w_four):
    B, L, _ = f.shape
    fg = f.reshape(B, L, N_FOUR_GROUPS, FOUR_DIM).transpose(0, 2, 1, 3).astype(jnp.float32)
    mixed = jnp.fft.fft2(fg, axes=(-2, -1), norm="ortho").real.astype(f.dtype)
    out = jnp.einsum('bglc,gcd->blgd', mixed, w_four)
    return out.reshape(B, L, FOUR_WIDTH)


def _ctx_attn(qc, kc, vc):
    B, L, H, Dh = qc.shape
    s = jnp.einsum('bqhd,bkhd->bhqk', qc, kc).astype(jnp.float32) * (Dh ** -0.5)
    p = jax.nn.softmax(s, axis=-1).astype(vc.dtype)
    return jnp.einsum('bhqk,bkhd->bqhd', p, vc).reshape(B, L, H * Dh)


def _na_latent(q, k, v, kc, vc, rpb, rows):
    B, N, H, Dh = q.shape
    kh = min(WIN_ROWS, rows)
    kw = WIN_COLS

    def grid(t):
        return t.reshape(B, rows, GRID_W, H, Dh).transpose(0, 3, 1, 2, 4)

    qg, kg, vg = grid(q), grid(k), grid(v)
    r = jnp.arange(rows)
    row_start = jnp.clip(r - kh // 2, 0, rows - kh)
    row_idx = row_start[:, None] + jnp.arange(kh)
    k_rows = jnp.take(kg, row_idx, axis=2)
    v_rows = jnp.take(vg, row_idx, axis=2)

    col = jnp.arange(GRID_W)
    col_start = jnp.clip(col - kw // 2, 0, GRID_W - kw)
    col_in = (col[None, :] >= col_start[:, None]) & (col[None, :] < col_start[:, None] + kw)
    dr = row_idx - r[:, None] + (WIN_ROWS - 1)
    dc = jnp.clip(col[None, :] - col[:, None], -(kw - 1), kw - 1) + (WIN_COLS - 1)
    bias = rpb[:, dr[:, None, :, None], dc[None, :, None, :]]

    scale = Dh ** -0.5
    s_win = jnp.einsum('bhrqd,bhrikd->bhrqik', qg, k_rows).astype(jnp.float32) * scale \
        + bias.astype(jnp.float32)
    s_win = jnp.where(col_in[:, None, :], s_win, -jnp.inf)
    s_ctx = jnp.einsum('bhrqd,bchd->bhrqc', qg, kc).astype(jnp.float32) * scale
    s = jnp.concatenate([s_win.reshape(B, H, rows, GRID_W, kh * GRID_W), s_ctx], axis=-1)
    p = jax.nn.softmax(s, axis=-1).astype(v.dtype)
    p_win = p[..., :kh * GRID_W].reshape(B, H, rows, GRID_W, kh, GRID_W)
    p_ctx = p[..., kh * GRID_W:]
    o = jnp.einsum('bhrqik,bhrikd->bhrqd', p_win, v_rows) \
        + jnp.einsum('bhrqc,bchd->bhrqd', p_ctx, vc)
    return o.transpose(0, 2, 3, 1, 4).reshape(B, N, H * Dh)


def setup_inputs(seed: int = 0) -> dict:
    key = jax.random.key(seed)
    ks = jax.random.split(key, 17)

    def nrm(k, shape, s):
        return jax.random.normal(k, shape, jnp.float32) * s

    return {
        "x": nrm(ks[0], (BATCH, SEQ, D_MODEL), 1.0),
        "c": nrm(ks[1], (BATCH, D_MODEL), 1.0),
        "ctx": nrm(ks[2], (BATCH, CTX_LEN, D_MODEL), 1.0),
        "c_ctx": nrm(ks[3], (D_MODEL,), 1.0),
        "w_mod": nrm(ks[4], (DEPTH, D_MODEL, N_MOD * D_MODEL), 0.5 * D_MODEL ** -0.5),
        "b_mod": nrm(ks[5], (DEPTH, N_MOD * D_MODEL), 0.02),
        "norm_w": 1.0 + nrm(ks[6], (DEPTH, 3, D_MODEL), 0.1),
        "ffn1_wi": nrm(ks[7], (DEPTH, D_MODEL, 2 * D_FF), D_MODEL ** -0.5),
        "ffn1_wo": nrm(ks[8], (DEPTH, D_FF, D_MODEL), D_FF ** -0.5),
        "w_in": nrm(ks[9], (DEPTH, D_MODEL, IN_WIDTH), D_MODEL ** -0.5),
        "q_norm_w": 1.0 + nrm(ks[10], (DEPTH, HEAD_DIM), 0.1),
        "k_norm_w": 1.0 + nrm(ks[11], (DEPTH, HEAD_DIM), 0.1),
        "rpb": nrm(ks[12], (DEPTH, N_NA_HEADS, 2 * WIN_ROWS - 1, 2 * WIN_COLS - 1), 0.5),
        "w_four": nrm(ks[13], (DEPTH, N_FOUR_GROUPS, FOUR_DIM, FOUR_DIM), FOUR_DIM ** -0.5),
        "w_out": nrm(ks[14], (DEPTH, D_MIX, D_MODEL), D_MIX ** -0.5),
        "ffn2_wi": nrm(ks[15], (DEPTH, D_MODEL, 2 * D_FF), D_MODEL ** -0.5),
        "ffn2_wo": nrm(ks[16], (DEPTH, D_FF, D_MODEL), D_FF ** -0.5),
    }


def reference(x, c, ctx, c_ctx, w_mod, b_mod, norm_w, ffn1_wi, ffn1_wo, w_in,
              q_norm_w, k_norm_w, rpb, w_four, w_out, ffn2_wi, ffn2_wo):
    B, N, D = x.shape
    rows = N // GRID_W
    xc = ctx
    for l in range(DEPTH):
        last = l == DEPTH - 1
        m = [t[:, None, :] for t in jnp.split(jax.nn.silu(c) @ w_mod[l] + b_mod[l], N_MOD, axis=-1)]
        mc = jnp.split(jax.nn.silu(c_ctx) @ w_mod[l] + b_mod[l], N_MOD, axis=-1)

        x = x + 0.5 * m[2] * _swiglu(_modulate(_rms(x, norm_w[l, 0]), m[0], m[1]), ffn1_wi[l], ffn1_wo[l])
        xc = xc + 0.5 * mc[2] * _swiglu(_modulate(_rms(xc, norm_w[l, 0]), mc[0], mc[1]), ffn1_wi[l], ffn1_wo[l])

        h = _modulate(_rms(x, norm_w[l, 1]), m[3], m[4]) @ w_in[l]
        q, k, v, f = jnp.split(h, [NA_WIDTH, 2 * NA_WIDTH, 3 * NA_WIDTH], axis=-1)
        hcn = _modulate(_rms(xc, norm_w[l, 1]), mc[3], mc[4])
        if last:
            kc, vc = jnp.split(hcn @ w_in[l][:, NA_WIDTH:3 * NA_WIDTH], 2, axis=-1)
        else:
            qc, kc, vc, fc = jnp.split(hcn @ w_in[l], [NA_WIDTH, 2 * NA_WIDTH, 3 * NA_WIDTH], axis=-1)
        kc_h = _heads(kc, k_norm_w[l])
        vc_h = _heads(vc)

        na = _na_latent(_heads(q, q_norm_w[l]), _heads(k, k_norm_w[l]), _heads(v), kc_h, vc_h, rpb[l], rows)
        y = jnp.concatenate([na, _fourier(f, w_four[l])], axis=-1) @ w_out[l]
        x = x + m[5] * y

        if not last:
            nac = _ctx_attn(_heads(qc, q_norm_w[l]), kc_h, vc_h)
            yc = jnp.concatenate([nac, _fourier(fc, w_four[l])], axis=-1) @ w_out[l]
            xc = xc + mc[5] * yc
            xc = xc + 0.5 * mc[8] * _swiglu(_modulate(_rms(xc, norm_w[l, 2]), mc[6], mc[7]), ffn2_wi[l], ffn2_wo[l])

        x = x + 0.5 * m[8] * _swiglu(_modulate(_rms(x, norm_w[l, 2]), m[6], m[7]), ffn2_wi[l], ffn2_wo[l])
    return x
```

```python
import numpy as np
import ml_dtypes
import concourse.bass as bass
import concourse.mybir as mybir
from concourse.bass_utils import run_bass_kernel_spmd
from contextlib import ExitStack

F32, BF16 = mybir.dt.float32, mybir.dt.bfloat16
AF = mybir.ActivationFunctionType
ALU = mybir.AluOpType
NPBF = ml_dtypes.bfloat16

CFG = dict(D=2048, DFF=5632, H=12, G=4, NQ=4, DEPTH=2)
SEQ, CTX, GRID_W, ROWS = 2048, 256, 64, 32
WIN_ROWS, WIN_COLS = 8, 16
NLAT, NCTX = 1024, 128
NT = NLAT + NCTX
TT = [(0, 384), (384, 768), (768, 1152)]
SEGS = [[(0, 384, 0)], [(384, 768, 0)], [(768, 1024, 0), (1024, 1152, 1)]]
EPS = 1e-6
NEG = -30000.0
SLOT = 2048
NSLOT = 4


class Op:
    __slots__ = ("eng", "fn", "deps", "idx", "needs_inc", "chan", "cnt", "is_dma")


class Prog:
    def __init__(self):
        self.ops = []
        self.lastw = {}
        self.readers = {}
        self.chan_cnt = {}

    def add(self, eng, fn, reads=(), writes=(), stream=None):
        op = Op()
        op.eng, op.fn, op.idx = eng, fn, len(self.ops)
        op.needs_inc = False
        op.is_dma = stream is not None
        op.chan = ("dma", stream) if stream is not None else ("eng", eng)
        deps = set()
        for k in reads:
            w = self.lastw.get(k)
            if w is not None:
                deps.add(w)
        for k in writes:
            w = self.lastw.get(k)
            if w is not None:
                deps.add(w)
            for r in self.readers.get(k, {}).values():
                deps.add(r)
        for k in reads:
            self.readers.setdefault(k, {})[op.chan] = op.idx
        for k in writes:
            self.lastw[k] = op.idx
            self.readers[k] = {}
        deps.discard(op.idx)
        op.deps = deps
        self.ops.append(op)
        return op.idx

    def emit(self, nc, es):
        ops = self.ops
        for b in ops:
            keep = set()
            for ai in b.deps:
                a = ops[ai]
                if (not a.is_dma) and (not b.is_dma) and a.eng == "pe" and b.eng == "pe":
                    continue
                keep.add(ai)
                a.needs_inc = True
            b.deps = keep
        for o in ops:
            if o.needs_inc:
                self.chan_cnt[o.chan] = self.chan_cnt.get(o.chan, 0) + 1
                o.cnt = self.chan_cnt[o.chan]
            else:
                o.cnt = None
        sems = {}

        def sem_for(chan, cnt):
            unit = 16 if chan[0] == "dma" else 1
            cap = 1800 if chan[0] == "dma" else 30000
            ep = (cnt - 1) // cap
            key = (chan, ep)
            if key not in sems:
                sems[key] = es.enter_context(nc.semaphore(f"s{len(sems)}"))
            return sems[key], ((cnt - 1) % cap + 1) * unit, unit

        block = es.enter_context(nc.Block())
        engmap = {"pe": block.tensor, "act": block.scalar, "dve": block.vector,
                  "pool": block.gpsimd, "sp": block.sync}
        for ename, deco in engmap.items():
            eops = [o for o in ops if o.eng == ename]

            def section(e, eops=eops):
                waited = {}
                for o in eops:
                    need = {}
                    for ai in o.deps:
                        a = ops[ai]
                        s_, v, _ = sem_for(a.chan, a.cnt)
                        if need.get(id(s_), (None, -1))[1] < v:
                            need[id(s_)] = (s_, v)
                    for sid, (s_, v) in need.items():
                        if waited.get(sid, -1) >= v:
                            continue
                        e.wait_ge(s_, v)
                        waited[sid] = v
                    if o.fn is None:
                        continue
                    ins = o.fn(e)
                    if o.needs_inc:
                        s_, v, unit = sem_for(o.chan, o.cnt)
                        ins.then_inc(s_, unit)

            deco(section)


class Builder:
    def __init__(self, stages, cfg):
        self.cfg = cfg
        self.stages = stages
        self.D, self.DFF, self.H, self.G, self.NQ = cfg["D"], cfg["DFF"], cfg["H"], cfg["G"], cfg["NQ"]
        self.KC = self.D // 128
        self.FC = self.DFF // 128
        self.FQ = self.FC // self.NQ
        self.NM = 9 * self.KC
        self.nc = bass.Bass("TRN2", target_bir_lowering=False)
        self.p = Prog()
        self.dram_in = {}
        self.dram_out = {}
        self.ring_i = 0
        self.uid = 0
        self._aps = {}
        self._sbs = {}
        self.fused = (len(stages) > 0 and stages[0][0] == "FUSED")

    def din(self, name, shape, dt=F32):
        if name in self._aps:
            return self._aps[name]
        t = self.nc.dram_tensor(name, list(shape), dt, kind="ExternalInput").ap()
        self.dram_in[name] = (tuple(shape), dt)
        self._aps[name] = t
        return t

    def dout(self, name, shape, dt=F32):
        if name in self._aps:
            return self._aps[name]
        t = self.nc.dram_tensor(name, list(shape), dt, kind="ExternalOutput").ap()
        self.dram_out[name] = (tuple(shape), dt)
        self._aps[name] = t
        return t

    def dscr(self, name, shape, dt=F32):
        if name in self._aps:
            return self._aps[name]
        t = self.nc.dram_tensor(name, list(shape), dt, kind="Internal").ap()
        self._aps[name] = t
        return t

    def sb(self, name, shape, dt):
        if name in self._sbs:
            return self._sbs[name]
        t = self.es.enter_context(self.nc.sbuf_tensor(name, list(shape), dt))
        self._sbs[name] = t
        return t

    def dma(self, q, out, in_, reads, writes, stream, **kw):
        return self.p.add(q, lambda e: e.dma_start(out=out, in_=in_, **kw), reads, list(writes) + [("strm", stream)], stream=stream)

    def wload(self, src_ap, ncols, tag):
        s = self.ring_i % NSLOT
        self.ring_i += 1
        dst = self.wring[:, s, 0:ncols]
        self.dma("pool", dst, src_ap, [], [("ring", s)], f"ring{s}", max_dma_last_dim=8192)
        return s

    def mm(self, out, lhsT, rhs, start, stop, reads, writes):
        return self.p.add("pe", lambda e: e.matmul(out, lhsT=lhsT, rhs=rhs, start=start, stop=stop), reads, writes)

    def act(self, out, in_, func, reads, writes, **kw):
        return self.p.add("act", lambda e: e.activation(out=out, in_=in_, func=func, **kw), reads, writes)

    def dve(self, fn, reads, writes):
        return self.p.add("dve", fn, reads, writes)

    def build(self):
        nc, cfg = self.nc, self.cfg
        KC, H, G, FC, NM = self.KC, self.H, self.G, self.FC, self.NM
        with ExitStack() as es:
            self.es = es
            self.xT = self.sb("xT", [128, KC, NT], F32)
            self.xm = self.sb("xm", [128, KC, NT], BF16)
            self.ab = self.sb("ab", [128, self.FQ, NT], BF16)
            self.wring = self.sb("wring", [128, NSLOT, SLOT], BF16)
            self.ones = self.sb("ones", [128, 128], BF16)
            self.modT_l = [self.sb(f"modTsb{i}", [128, NM, 2], F32) for i in range(2)]
            self.gs_l = [self.sb(f"gssb{i}", [128, 3, KC, 2], F32) for i in range(2)]
            self.gate_l = [self.sb(f"gatesb{i}", [128, 3, KC, 2], F32) for i in range(2)]
            self.nw_l = [self.sb(f"nwsb{i}", [128, 3, KC, 2], F32) for i in range(2)]
            self.set_par(0)
            self.rt = self.sb("rt", [128, 1, 384], F32)
            self.rstd = self.sb("rstd", [128, 1, 384], F32)
            self.sg = self.sb("sg", [128, 3, 384], F32)
            self.ps = [es.enter_context(nc.psum_tensor(f"ps{i}", [128, 512], F32)) for i in range(8)]
            self.p.add("dve", lambda e: e.memset(self.ones[:], 1.0), [], ["ones"])
            if self.fused:
                self.build_fused()
                self.p.emit(nc, es)
                return nc
            first = True
            for (kind, l) in self.stages:
                last_layer = (l == cfg["DEPTH"] - 1)
                if kind == "A":
                    self.stage_A(l, load_x=first)
                else:
                    self.stage_B(l, load_x=first, final=last_layer)
                first = False
            kind, l = self.stages[-1]
            if kind == "A":
                xo = self.dout("xT_out", [128, KC, NT])
                mo = self.dout("modT_out", [128, NM, 2])
                i1 = self.dma("sp", xo[:, :, :], self.xT[:], [("x", c, t) for c in range(KC) for t in range(3)], ["xo"], "fin0")
                i2 = self.dma("sp", mo[:, :, :], self.modT[:], [("modT", self.par)], ["mo"], "fin1")
                fin = [i1, i2] + self.out_dmas
            else:
                yo = self.dout("y_out", [128, KC, NLAT])
                i1 = self.dma("sp", yo[:, :, :], self.xT[:, :, 0:NLAT], [("x", c, t) for c in range(KC) for t in range(3)], ["yo"], "fin0")
                fin = [i1]
            self.p.add("sp", None, ["xo", "mo", "yo", "qkvf_out"], [])
            self.p.emit(nc, es)
        return nc

    def set_par(self, par):
        self.par = par
        self.modT, self.gs, self.gate, self.nw = self.modT_l[par], self.gs_l[par], self.gate_l[par], self.nw_l[par]

    def build_fused(self):
        KC, DEPTH = self.KC, self.cfg["DEPTH"]
        xin = self.din("xT_in", [2, 128, KC, NT])
        yo = self.dout("y_out", [2, 128, KC, NLAT])
        xs = self.dscr("xs", [2, 128, KC, NT])
        self.epsb = self.sb("epsb", [128, 1], F32)
        self.p.add("dve", lambda e: e.memset(self.epsb[:], EPS), [], ["epsb"])
        XK = [("x", c, t) for c in range(KC) for t in range(3)]

        def xload(src, key):
            for c in range(KC):
                self.dma("sp", self.xT[:, c, :], src[:, c, :], [key], [("x", c, t) for t in range(3)], f"xin{c % 4}")

        def xstore(dst, key):
            self.dma("sp", dst, self.xT[:], XK, [key], "xst")

        KC3 = 3 * KC
        self.mod_prep()
        for l in range(DEPTH):
            last = l == DEPTH - 1
            self.set_par(l % 2)
            if l == 0:
                xload(xin[0], "xin0")
                for it in self.mod_items(0, 0, KC3):
                    it()
                self.mod_finish(0, 0, KC3)
                self.derive_mod(0, subs=(0,))
                rest = self.mod_items(0, KC3, self.NM)
                self.stage_norm(0)
                self.stage_ffn(0, 1, 0, extra=rest)
                self.mod_finish(0, KC3, self.NM)
                self.derive_mod(0, subs=(1, 2), load_nw=False)
                self.stage_norm(1)
                self.stage_inproj(0, 0)
            else:
                self.stage_A_body(l, 0)
            xstore(xs[0], ("xs", 0))
            if l == 0:
                xload(xin[1], "xin1")
            else:
                xload(xs[1], ("xs", 1))
            self.stage_A_body(l, 1)
            self.stage_B(l, load_x=False, final=last, h=1)
            if last:
                self.dma("sp", yo[1], self.xT[:, :, 0:NLAT], XK, ["yo"], "fin0")
            else:
                xstore(xs[1], ("xs", 1))
            xload(xs[0], ("xs", 0))
            nxt = None if last else self.mod_items(l + 1, 0, self.NM)
            self.stage_B(l, load_x=False, final=last, h=0, ffn_extra=nxt)
            if last:
                self.dma("sp", yo[0], self.xT[:, :, 0:NLAT], XK, ["yo"], "fin1")
            else:
                self.set_par((l + 1) % 2)
                self.mod_finish(l + 1, 0, self.NM)
                self.derive_mod(l + 1)
        self.p.add("sp", None, ["yo"], [])

    def stage_A_body(self, l, h):
        self.stage_norm(0)
        self.stage_ffn(l, 1, 0)
        self.stage_norm(1)
        self.stage_inproj(l, h)

    def load_x(self):
        KC = self.KC
        xin = self.din("xT_in", [128, KC, NT])
        for c in range(KC):
            self.dma("sp", self.xT[:, c, :], xin[:, c, :], [], [("x", c, t) for t in range(3)], f"xin{c % 4}")

    def derive_mod(self, l, subs=(0, 1, 2), load_nw=True):
        KC = self.KC
        par = self.par
        modT, gs, gate, nw = self.modT, self.gs, self.gate, self.nw
        if load_nw:
            nwd = self.din(f"nw{l}", [128, 3, KC, 2])
            self.dma("sp", nw[:], nwd[:, :, :, :], [], [("nw", par)], "small")
        for sub in subs:
            sc = modT[:, (3 * sub + 1) * KC:(3 * sub + 2) * KC, :]
            gt = modT[:, (3 * sub + 2) * KC:(3 * sub + 3) * KC, :]
            self.dve(lambda e, sc=sc, sub=sub, gs=gs, nw=nw: e.scalar_tensor_tensor(
                out=gs[:, sub], in0=sc, scalar=1.0, in1=nw[:, sub], op0=ALU.add, op1=ALU.mult),
                [("modT", par), ("nw", par)], [("gs", par, sub)])
            f = 1.0 if sub == 1 else 0.5
            self.dve(lambda e, gt=gt, sub=sub, f=f, gate=gate: e.tensor_scalar(
                out=gate[:, sub], in0=gt, scalar1=f, scalar2=None, op0=ALU.mult),
                [("modT", par)], [("gate", par, sub)])

    def mod_prep(self):
        KC, NM = self.KC, self.NM
        cT = self.din("cT0", [128, KC, 2])
        cs = self.sb("cs", [128, KC, 2], F32)
        self.scT = self.sb("scT", [128, KC, 4], BF16)
        s32 = self.sb("s32", [128, KC, 2], F32)
        scT = self.scT
        self.dma("sp", cs[:], cT[:, :, :], [], ["cs"], "small")
        self.act(s32[:], cs[:], AF.Silu, ["cs"], ["s32"])
        self.act(scT[:, :, 0:2], s32[:], AF.Copy, ["s32"], ["scT"])
        self.dve(lambda e: e.tensor_tensor(out=scT[:, :, 2:4], in0=s32[:], in1=scT[:, :, 0:2], op=ALU.subtract),
                 ["s32", "scT"], ["scT"])

    def mod_items(self, l, j0, j1):
        KC, NM = self.KC, self.NM
        NH = NM // 2
        wm = self.din(f"wmod{l}", [NM, 128, KC * 128])
        scT = self.scT

        def item(j):
            s_ = self.wload(wm[j], KC * 128, "wmod")
            pbi = 7 if j < NH else 6
            jj = j if j < NH else j - NH
            for kc in range(KC):
                self.mm(self.ps[pbi][:, 4 * jj:4 * jj + 4], self.wring[:, s_, kc * 128:(kc + 1) * 128],
                        scT[:, kc, :], kc == 0, kc == KC - 1, [("ring", s_), "scT"], [("ps", pbi)])
        return [(lambda j=j: item(j)) for j in range(j0, j1)]

    def mod_finish(self, l, j0, j1):
        KC, NM = self.KC, self.NM
        NH = NM // 2
        par = self.par
        modT = self.modT
        bm = self.din(f"bmod{l}", [128, NM, 2])
        bms = self.sb("bms", [128, NM, 2], F32)
        self.dma("sp", bms[:], bm[:, :, :], [], ["bms"], "small2")
        for (a, b, pbi, off) in ((j0, min(j1, NH), 7, 0), (max(j0, NH), j1, 6, NH)):
            if b <= a:
                continue
            pv = self.ps[pbi][:, 4 * (a - off):4 * (b - off)].rearrange("p (a b) -> p a b", b=4)
            mo = modT[:, a:b, :]
            self.dve(lambda e, pv=pv, mo=mo, a=a, b=b: e.tensor_tensor(out=mo, in0=pv[:, :, 0:2], in1=bms[:, a:b, :], op=ALU.add),
                     [("ps", pbi), "bms"], [("modT", par)])
            self.dve(lambda e, pv=pv, mo=mo: e.tensor_tensor(out=mo, in0=pv[:, :, 2:4], in1=mo, op=ALU.add),
                     [("ps", pbi), ("modT", par)], [("modT", par)])

    def stage_mod(self, l):
        if not hasattr(self, "scT"):
            self.mod_prep()
        for it in self.mod_items(l, 0, self.NM):
            it()
        self.mod_finish(l, 0, self.NM)

    def stage_norm(self, sub):
        KC = self.KC
        D = self.D
        for t, (t0, t1) in enumerate(TT):
            n = t1 - t0
            xk = [("x", c, t) for c in range(KC)]
            mk = [("xm", c, t) for c in range(KC)]
            self.act(self.xm[:, :, t0:t1], self.xT[:, :, t0:t1], AF.Square, xk, mk)
            pst = self.ps[6]
            for c in range(KC):
                self.mm(pst[:, 0:n], self.ones[:], self.xm[:, c, t0:t1], c == 0, c == KC - 1,
                        ["ones", ("xm", c, t)], [("ps", 6)])
            self.act(self.rt[:, 0, 0:n], pst[:, 0:n], AF.Ln, [("ps", 6)], [("rt", 0)], scale=1.0 / D, bias=self.epsb[:])
            self.act(self.rstd[:, 0, 0:n], self.rt[:, 0, 0:n], AF.Exp, [("rt", 0)], [("rstd", 0)], scale=-0.5)
            for (s0, s1, r) in SEGS[t]:
                for c in range(KC):
                    b = c % 2
                    self.dve(lambda e, c=c, b=b, s0=s0, s1=s1, t0=t0: e.tensor_tensor(
                        out=self.sg[:, b, 0:s1 - s0], in0=self.xT[:, c, s0:s1], in1=self.rstd[:, 0, s0 - t0:s1 - t0], op=ALU.mult),
                        [("x", c, t), ("rstd", 0)], [("sg", b)])
                    sh = self.modT[:, 3 * sub * KC + c, r:r + 1]
                    self.act(self.xm[:, c, s0:s1], self.sg[:, b, 0:s1 - s0], AF.Identity,
                             [("sg", b), ("modT", self.par), ("gs", self.par, sub)], [("xm", c, t)],
                             scale=self.gs[:, sub, c, r:r + 1], bias=sh)

    def resid(self, bi, i, t, sub):
        psb = self.ps[bi]
        t0 = TT[t][0]
        gate, par = self.gate, self.par
        for (s0, s1, r) in SEGS[t]:
            self.dve(lambda e, s0=s0, s1=s1, r=r, gate=gate: e.scalar_tensor_tensor(
                out=self.xT[:, i, s0:s1], in0=psb[:, s0 - t0:s1 - t0], scalar=gate[:, sub, i, r:r + 1],
                in1=self.xT[:, i, s0:s1], op0=ALU.mult, op1=ALU.add),
                [("ps", bi), ("gate", par, sub), ("x", i, t)], [("x", i, t)])

    def stage_ffn(self, l, which, sub, extra=None):
        extra = list(extra) if extra else []
        nslots = 2 * self.FC
        per_slot = -(-len(extra) // nslots) if extra else 0
        KC, FQ, NQ = self.KC, self.FQ, self.NQ
        wi = self.din(f"wi{which}_{l}", [self.FC * 2, 128, KC * 128])
        wo = self.din(f"wo{which}_{l}", [NQ * KC, 128, FQ * 128])
        for qd in range(NQ):
            for jj in range(FQ):
                j = qd * FQ + jj
                for half in range(2):
                    s = self.wload(wi[2 * j + half], KC * 128, "wi")
                    for kc in range(KC):
                        for t, (t0, t1) in enumerate(TT):
                            b = half * 3 + t
                            self.mm(self.ps[b][:, 0:t1 - t0], self.wring[:, s, kc * 128:(kc + 1) * 128],
                                    self.xm[:, kc, t0:t1], kc == 0, kc == KC - 1, [("ring", s), ("xm", kc, t)], [("ps", b)])
                    for _ in range(per_slot):
                        if extra:
                            extra.pop(0)()
                for t, (t0, t1) in enumerate(TT):
                    n = t1 - t0
                    self.act(self.sg[:, t, 0:n], self.ps[t][:, 0:n], AF.Silu, [("ps", t)], [("sg", t)])
                    self.dve(lambda e, t=t, n=n, jj=jj, t0=t0, t1=t1: e.tensor_tensor(
                        out=self.ab[:, jj, t0:t1], in0=self.ps[3 + t][:, 0:n], in1=self.sg[:, t, 0:n], op=ALU.mult),
                        [("ps", 3 + t), ("sg", t)], [("ab", jj, t)])
            for i in range(KC):
                s = self.wload(wo[qd * KC + i], FQ * 128, "wo")
                bb = 3 * (i % 2)
                for kc in range(FQ):
                    for t, (t0, t1) in enumerate(TT):
                        self.mm(self.ps[bb + t][:, 0:t1 - t0], self.wring[:, s, kc * 128:(kc + 1) * 128],
                                self.ab[:, kc, t0:t1], kc == 0, kc == FQ - 1, [("ring", s), ("ab", kc, t)], [("ps", bb + t)])
                for t in range(3):
                    self.resid(bb + t, i, t, sub)
        while extra:
            extra.pop(0)()

    def stage_inproj(self, l, h=None):
        KC, H, G = self.KC, self.H, self.G
        nqkf = 2 * H + G
        w1 = self.din(f"winqkf{l}", [nqkf, 128, KC * 128])
        nvg = H
        vw = 128
        w2 = self.din(f"winv{l}", [nvg, 128, KC * vw])
        qkg = self.din(f"qkg{l}", [128, 2])
        if self.fused:
            qo = self.dscr(f"q{l}_{h}", [128, H, NT], BF16)
            ko = self.dscr(f"k{l}_{h}", [128, H, NT], BF16)
            fo = self.dscr(f"f{l}_{h}", [128, G, NT], F32)
            vo = self.dscr(f"v{l}_{h}", [NT // 128, 128, H * 128], BF16)
            OK = ("qkvf", l, h)
        else:
            qo = self.dout(f"q{l}", [128, H, NT], BF16)
            ko = self.dout(f"k{l}", [128, H, NT], BF16)
            fo = self.dout(f"f{l}", [128, G, NT], F32)
            vo = self.dout(f"v{l}", [NT // 128, 128, H * 128], BF16)
            OK = "qkvf_out"
        qkgs = self.sb("qkgs", [128, 2], F32)
        self.dma("sp", qkgs[:], qkg[:, :], [], ["qkg"], "small")
        stg = self.sb("stg", [128, 2, NT], BF16)
        sq = self.sb("sqq", [128, 1, 384], BF16)
        vst = self.sb("vst", [128, 2, vw], BF16)
        self.out_dmas = []
        def main(ch):
            s = self.wload(w1[ch], KC * 128, "winqkf")
            pb0 = 4 * (ch % 2)
            for kc in range(KC):
                for t, (t0, t1) in enumerate(TT):
                    self.mm(self.ps[pb0 + t][:, 0:t1 - t0], self.wring[:, s, kc * 128:(kc + 1) * 128], self.xm[:, kc, t0:t1],
                            kc == 0, kc == KC - 1, [("ring", s), ("xm", kc, t)], [("ps", pb0 + t)])

        def post(ch):
            sb_ = ch % 2
            pb0 = 4 * sb_
            if ch < 2 * H:
                which = 0 if ch < H else 1
                for t, (t0, t1) in enumerate(TT):
                    n = t1 - t0
                    tb = 0
                    self.act(sq[:, tb, 0:n], self.ps[pb0 + t][:, 0:n], AF.Square, [("ps", pb0 + t)], [("sqq", tb)])
                    self.mm(self.ps[pb0 + 3][:, 0:n], self.ones[:], sq[:, tb, 0:n], True, True, ["ones", ("sqq", tb)], [("ps", pb0 + 3)])
                    self.act(self.rt[:, 0, 0:n], self.ps[pb0 + 3][:, 0:n], AF.Ln, [("ps", pb0 + 3)], [("rt", 0)], scale=1.0 / 128, bias=self.epsb[:])
                    self.act(self.rstd[:, 0, 0:n], self.rt[:, 0, 0:n], AF.Exp, [("rt", 0)], [("rstd", 0)], scale=-0.5)
                    self.dve(lambda e, t=t, n=n, t0=t0, t1=t1, sb_=sb_, which=which, pb0=pb0: e.scalar_tensor_tensor(
                        out=stg[:, sb_, t0:t1], in0=self.ps[pb0 + t][:, 0:n], scalar=qkgs[:, which:which + 1],
                        in1=self.rstd[:, 0, 0:n], op0=ALU.mult, op1=ALU.mult),
                        [("ps", pb0 + t), ("rstd", 0), "qkg"], [("stg", sb_)])
                dst = qo[:, ch, :] if ch < H else ko[:, ch - H, :]
                self.dma("sp", dst, stg[:, sb_, :], [("stg", sb_)], [OK], f"qkf{sb_}")
            else:
                for t, (t0, t1) in enumerate(TT):
                    n = t1 - t0
                    self.act(self.sg[:, t, 0:n], self.ps[pb0 + t][:, 0:n], AF.Copy, [("ps", pb0 + t)], [("sg", t)])
                self.dma("sp", fo[:, ch - 2 * H, :].rearrange("p (t n) -> p t n", t=3), self.sg[:], [("sg", 0), ("sg", 1), ("sg", 2)], [OK], "qkff")

        for ch in range(nqkf):
            main(ch)
            if ch > 0:
                post(ch - 1)
        post(nqkf - 1)
        for cg in range(nvg):
            s = self.wload(w2[cg], KC * vw, "winv")
            for tc in range(NT // 128):
                pb = 4 + (tc % 2)
                for kc in range(KC):
                    self.mm(self.ps[pb][:, 0:vw], self.xm[:, kc, tc * 128:(tc + 1) * 128], self.wring[:, s, kc * vw:(kc + 1) * vw],
                            kc == 0, kc == KC - 1, [("ring", s), ("xm", kc, tc // 3)], [("ps", pb)])
                vb = tc % 2
                self.act(vst[:, vb, :], self.ps[pb][:, 0:vw], AF.Copy, [("ps", pb)], [("vst", vb)])
                self.dma("sp", vo[tc, :, cg * vw:(cg + 1) * vw], vst[:, vb, :], [("vst", vb)], [OK], f"vst{vb}")

    def stage_A(self, l, load_x):
        if not hasattr(self, "epsb"):
            self.epsb = self.sb("epsb", [128, 1], F32)
            self.p.add("dve", lambda e: e.memset(self.epsb[:], EPS), [], ["epsb"])
        if load_x:
            self.load_x()
        self.stage_mod(l)
        self.derive_mod(l)
        self.stage_norm(0)
        self.stage_ffn(l, 1, 0)
        self.stage_norm(1)
        self.stage_inproj(l)

    def stage_B(self, l, load_x, final, h=None, ffn_extra=None):
        KC, H, G = self.KC, self.H, self.G
        nc = self.nc
        if not hasattr(self, "epsb"):
            self.epsb = self.sb("epsb", [128, 1], F32)
            self.p.add("dve", lambda e: e.memset(self.epsb[:], EPS), [], ["epsb"])
        if load_x:
            self.load_x()
            mi = self.din("modT_in", [128, self.NM, 2])
            self.dma("sp", self.modT[:], mi[:, :, :], [], [("modT", self.par)], "small")
            self.derive_mod(l)
        if self.fused:
            ksc = [self.dscr(f"k{l}_{r}", [128, H, NT], BF16) for r in range(2)]
            vsc = [self.dscr(f"v{l}_{r}", [NT // 128, 128, H * 128], BF16) for r in range(2)]
            fsc = [self.dscr(f"f{l}_{r}", [128, G, NT], F32) for r in range(2)]
            qd = self.dscr(f"q{l}_{h}", [128, H, NT], BF16)
            kown, vown = ksc[h], vsc[h]
            kp, vp, fp = ksc, vsc, fsc
            bias = self.din(f"bias{l}_{h}", [H, 5, 128, 6 * 128])
            RK = [("qkvf", l, 0), ("qkvf", l, 1)]
        else:
            qd = self.din(f"qown{l}", [128, H, NT], BF16)
            kown = self.din(f"kown{l}", [128, H, NT], BF16)
            vown = self.din(f"vown{l}", [NT // 128, 128, H * 128], BF16)
            kp = self.din(f"kpair{l}", [2, 128, H, NT], BF16)
            vp = self.din(f"vpair{l}", [2, NT // 128, 128, H * 128], BF16)
            fp = self.din(f"fpair{l}", [2, 128, G, NT], F32)
            bias = self.din(f"bias{l}", [H, 5, 128, 6 * 128])
            kp = [kp[0], kp[1]]
            vp = [vp[0], vp[1]]
            fp = [fp[0], fp[1]]
            RK = []
        self.RK = RK
        self.bt = None
        Kt = self.sb("Kt", [128, 2, 18, 128], BF16)
        self.Kt, self.Qt = Kt, None
        Vt = self.sb("Vt", [128, 2, 14, 128], BF16)
        Qt = self.sb("Qt", [128, 2, NT], BF16)
        self.Qt = Qt
        bt = self.sb("bt", [128, 2, 6 * 128], F32)
        self.bt = bt
        tS = self.sb("tS", [128, 6 * 128], F32)
        Pm = self.sb("Pm", [128, 2, 8, 128], BF16)
        rD = self.sb("rD", [128, 128], F32)
        rL = self.sb("rL", [128, 128], F32)
        scale = 128.0 ** -0.5
        units = [(hh, jl) for hh in range(H) for jl in range(9)]
        state = {}

        def emit_S(ui):
            hh, jl = units[ui]
            hb = hh % 2
            kk, vk, qk = ("Kt", hb), ("Vt", hb), ("Qt", hb)
            if jl == 0:
                st = f"att{hb}"
                self.dma("sp", Kt[:, hb, 0:2, :], kp[0][:, hh, 768:1024].rearrange("p (c k) -> p c k", k=128), RK, [kk], st + "a")
                self.dma("sp", Kt[:, hb, 2:10, :], kown[:, hh, 0:1024].rearrange("p (c k) -> p c k", k=128), RK, [kk], st + "b")
                self.dma("sp", Kt[:, hb, 10:12, :], kp[1][:, hh, 0:256].rearrange("p (c k) -> p c k", k=128), RK, [kk], st + "c")
                self.dma("sp", Kt[:, hb, 12, :], kp[0][:, hh, 1024:1152], RK, [kk], st + "d")
                self.dma("sp", Kt[:, hb, 13, :], kp[1][:, hh, 1024:1152], RK, [kk], st + "e")
                hs = slice(hh * 128, (hh + 1) * 128)
                self.dma("sp", Vt[:, hb, 0:2, :], vp[0][6:8, :, hs].rearrange("c p d -> p c d"), RK, [vk], st + "f")
                self.dma("sp", Vt[:, hb, 2:10, :], vown[0:8, :, hs].rearrange("c p d -> p c d"), RK, [vk], st + "g")
                self.dma("sp", Vt[:, hb, 10:12, :], vp[1][0:2, :, hs].rearrange("c p d -> p c d"), RK, [vk], st + "h")
                self.dma("sp", Vt[:, hb, 12, :], vp[0][8, :, hs], RK, [vk], st + "i")
                self.dma("sp", Vt[:, hb, 13, :], vp[1][8, :, hs], RK, [vk], st + "j")
                self.dma("sp", Qt[:, hb, :], qd[:, hh, :], RK, [qk], st + "k")
            ib = ui % 2
            q0 = jl * 128
            if jl < 8:
                wb0, nw = _win(jl)
                slots = [wb0 + i for i in range(nw)] + [12, 13]
                bslot = 0 if jl == 0 else 1 if jl == 1 else 3 if jl == 6 else 4 if jl == 7 else 2
                self.dma("sp", bt[:, ib, :], bias[hh, bslot, :, :], [], [("bt", ib)], f"bt{ib}")
            else:
                slots = [12, 13]
            state[ui] = slots
            bA, bB = self.ps[4 * ib], self.ps[4 * ib + 1]
            iA, iB = 4 * ib, 4 * ib + 1
            for i, sl in enumerate(slots):
                pb, pi, off = (bA, iA, i) if i < 4 else (bB, iB, i - 4)
                self.mm(pb[:, off * 128:(off + 1) * 128], Kt[:, hb, sl, :], Qt[:, hb, q0:q0 + 128], True, True,
                        [kk, qk], [("ps", pi)])
            pk = ("Pm", ib)
            if jl < 8:
                self.dve(lambda e, ib=ib, bA=bA: e.scalar_tensor_tensor(
                    out=tS[:, 0:512], in0=bA[:, 0:512], scalar=scale, in1=bt[:, ib, 0:512], op0=ALU.mult, op1=ALU.add),
                    [("ps", iA), ("bt", ib)], ["tS0"])
                nB = nw - 4
                self.dve(lambda e, ib=ib, bB=bB, nB=nB: e.scalar_tensor_tensor(
                    out=tS[:, 512:512 + nB * 128], in0=bB[:, 0:nB * 128], scalar=scale, in1=bt[:, ib, 512:512 + nB * 128],
                    op0=ALU.mult, op1=ALU.add),
                    [("ps", iB), ("bt", ib)], ["tS1"])
                self.act(Pm[:, ib, 0:nw, :].rearrange("p a b -> p (a b)"), tS[:, 0:nw * 128], AF.Exp, ["tS0", "tS1"], [pk])
                self.act(Pm[:, ib, nw:nw + 2, :].rearrange("p a b -> p (a b)"), bB[:, nB * 128:(nB + 2) * 128], AF.Exp,
                         [("ps", iB)], [pk], scale=scale)
            else:
                self.act(Pm[:, ib, 0:2, :].rearrange("p a b -> p (a b)"), bA[:, 0:256], AF.Exp, [("ps", iA)], [pk], scale=scale)

        def emit_PV(ui):
            hh, jl = units[ui]
            hb = hh % 2
            vk = ("Vt", hb)
            ib = ui % 2
            q0 = jl * 128
            slots = state.pop(ui)
            ns = len(slots)
            bO, bD = self.ps[4 * ib + 2], self.ps[4 * ib + 3]
            iO, iD = 4 * ib + 2, 4 * ib + 3
            pk = ("Pm", ib)
            for i, sl in enumerate(slots):
                self.mm(bO[:, 0:128], Vt[:, hb, sl, :], Pm[:, ib, i, :], i == 0, i == ns - 1, [vk, pk], [("ps", iO)])
            for i, sl in enumerate(slots):
                self.mm(bD[:, 0:128], self.ones[:], Pm[:, ib, i, :], i == 0, i == ns - 1, ["ones", pk], [("ps", iD)])
            self.act(rL[:], bD[:, 0:128], AF.Ln, [("ps", iD)], ["rL"])
            self.act(rD[:], rL[:], AF.Exp, ["rL"], ["rD"], scale=-1.0)
            self.dve(lambda e, bO=bO, hh=hh, q0=q0: e.tensor_tensor(
                out=self.xm[:, hh, q0:q0 + 128], in0=bO[:, 0:128], in1=rD[:], op=ALU.mult),
                [("ps", iO), "rD"], [("xm", hh, q0 // 384)])

        for ui in range(len(units)):
            emit_S(ui)
            if ui > 0:
                emit_PV(ui - 1)
        emit_PV(len(units) - 1)
        self.stage_fourier(l, fp, h)
        wout = self.din(f"wout{l}", [KC, 128, KC * 128])
        for i in range(KC):
            s = self.wload(wout[i], KC * 128, "wout")
            bb = 3 * (i % 2)
            for kc in range(KC):
                for t, (t0, t1) in enumerate(TT):
                    self.mm(self.ps[bb + t][:, 0:t1 - t0], self.wring[:, s, kc * 128:(kc + 1) * 128], self.xm[:, kc, t0:t1],
                            kc == 0, kc == KC - 1, [("ring", s), ("xm", kc, t)], [("ps", bb + t)])
            for t in range(3):
                self.resid(bb + t, i, t, 1)
        self.stage_norm(2)
        self.stage_ffn(l, 2, 2, extra=ffn_extra)

    def stage_fourier(self, l, fp, h=None):
        KC, H, G = self.KC, self.H, self.G
        wf = self.din(f"wfour{l}", [G, 128, 128])
        ccs = self.din("dftc", [128, 2, 128])
        sfx = f"_{h}" if self.fused else ""
        dl = self.din("dftl" + sfx, [2, 4, 128, 4 * NLAT], BF16)
        dcx = self.din("dftx" + sfx, [2, 128, 2 * NCTX], BF16)
        RK = self.RK
        cc = self.sb("cc", [128, 2, 128], F32)
        wfs = self.sb("wfs", [128, G, 128], F32)
        AB = self.sb("AB", [128, 2, 128], F32)
        ff = self.bt[:].rearrange("p a n -> p (a n)")
        Y = self.Kt
        ALK = [("Kt", 0), ("Kt", 1)]
        ALQ = [("bt", 0), ("bt", 1)]
        dx = self.sb("dx", [128, 2, 2 * NCTX], BF16)
        self.dma("sp", cc[:], ccs[:, :, :], [], ["cc"], "small")
        self.dma("sp", wfs[:], wf.rearrange("g c d -> c g d"), [], ["wfs"], "small2")
        self.dma("sp", dx[:, 0, :], dcx[0], [], ["dx"], "dx0")
        self.dma("sp", dx[:, 1, :], dcx[1], [], ["dx"], "dx1")
        for g in range(G):
            for ab in range(2):
                self.mm(self.ps[6][:, ab * 128:(ab + 1) * 128], cc[:, ab, :], wfs[:, g, :], True, True, ["cc", "wfs"], [("ps", 6)])
            self.act(AB[:].rearrange("p a b -> p (a b)"), self.ps[6][:, 0:256], AF.Copy, [("ps", 6)], ["AB"])
            for r in range(2):
                self.dma("sp", ff[:, 0:NT], fp[r][:, g, :], RK, ["ff"] + ALQ, "ffa")
                for lc in range(9):
                    ci = r * 8 + lc if lc < 8 else 16 + r
                    c0 = lc * 128
                    pb = 4 + lc % 2
                    for ab in range(2):
                        self.mm(self.ps[pb][:, ab * 128:(ab + 1) * 128], ff[:, c0:c0 + 128], AB[:, ab, :], True, True, ["ff", "AB"], [("ps", pb)])
                    self.act(Y[:, :, ci, :], self.ps[pb][:, 0:256].rearrange("p (a b) -> p a b", a=2), AF.Copy, [("ps", pb)], [("Y", ci)] + ALK)
            LT = [(0, 384), (384, 768), (768, 1024)]
            for t, (t0, t1) in enumerate(LT):
                n = t1 - t0
                nmm = 0
                for cs in range(2):
                    for qtr in range(4):
                        s = self.ring_i % NSLOT
                        self.ring_i += 1
                        dst = self.wring[:, s, 0:4 * n].rearrange("p (k n) -> p k n", k=4)
                        src = dl[cs, qtr, :, :].rearrange("p (k n) -> p k n", k=4)[:, :, t0:t1]
                        self.dma("sp", dst, src, [], [("ring", s)], f"ring{s}")
                        for k4 in range(4):
                            kc = qtr * 4 + k4
                            self.mm(self.ps[t][:, 0:n], Y[:, cs, kc, :], self.wring[:, s, k4 * n:(k4 + 1) * n],
                                    nmm == 0, nmm == 31, [("Y", kc), ("ring", s)], [("ps", t)])
                            nmm += 1
                self.act(self.xm[:, H + g, t0:t1], self.ps[t][:, 0:n], AF.Copy, [("ps", t)], [("xm", H + g, t)])
            nmm = 0
            for cs in range(2):
                for k2 in range(2):
                    self.mm(self.ps[3][:, 0:128], Y[:, cs, 16 + k2, :], dx[:, cs, k2 * 128:(k2 + 1) * 128],
                            nmm == 0, nmm == 3, [("Y", 16 + k2), "dx"], [("ps", 3)])
                    nmm += 1
            self.act(self.xm[:, H + g, NLAT:NT], self.ps[3][:, 0:128], AF.Copy, [("ps", 3)], [("xm", H + g, 2)])


def _fm(a, kc):
    T = a.shape[0]
    return np.ascontiguousarray(a.reshape(T, kc, 128).transpose(2, 1, 0))


def _vec_fm(v, kc):
    return np.ascontiguousarray(v.reshape(kc, 128).T)


def _wtile(w, cols, kc):
    K, N = w.shape
    g = N // cols
    return np.ascontiguousarray(w.reshape(kc, 128, g, cols).transpose(2, 1, 0, 3)).reshape(g, 128, kc * cols)


def _win(jl):
    if jl == 0:
        return 0, 6
    if jl == 7:
        return 6, 6
    return jl, 5


def _bias_tables(rpb_l, h, H):
    out = np.full((H, 5, 128, 6, 128), NEG, np.float32)
    jl_of_slot = [0, 1, 2, 6, 7]
    kk = np.arange(128)
    qq = np.arange(128)
    for sl, jl in enumerate(jl_of_slot):
        J = 8 * h + jl
        r = 2 * J + qq // 64
        cq = qq % 64
        rs = np.clip(r - WIN_ROWS // 2, 0, ROWS - WIN_ROWS)
        cstart = np.clip(cq - WIN_COLS // 2, 0, GRID_W - WIN_COLS)
        wb0, nw = _win(jl)
        for i in range(nw):
            gc = wb0 + i + 8 * h - 2
            if gc < 0 or gc > 15:
                continue
            kr = 2 * gc + kk // 64
            ck = kk % 64
            valid = (kr[:, None] >= rs[None, :]) & (kr[:, None] < rs[None, :] + WIN_ROWS) & \
                    (ck[:, None] >= cstart[None, :]) & (ck[:, None] < cstart[None, :] + WIN_COLS)
            dr = np.clip(kr[:, None] - r[None, :] + WIN_ROWS - 1, 0, 2 * WIN_ROWS - 2)
            dc = np.clip(ck[:, None] - cq[None, :], -(WIN_COLS - 1), WIN_COLS - 1) + WIN_COLS - 1
            vals = rpb_l[:, dr, dc]
            out[:, sl, :, i, :] = np.where(valid[None], vals, NEG)
    return out.reshape(H, 5, 128, 6 * 128)


def _dft_consts(h):
    c = np.arange(128)
    ang = 2 * np.pi * np.outer(c, c) / 128
    dftc = (np.stack([np.cos(ang), -np.sin(ang)], 1) / np.sqrt(128)).astype(np.float32)
    lin = np.arange(SEQ)
    lout = np.arange(h * NLAT, (h + 1) * NLAT)
    ang = 2 * np.pi * (np.outer(lin, lout) % SEQ) / SEQ
    tabs = (np.stack([np.cos(ang), np.sin(ang)]) / np.sqrt(SEQ)).astype(np.float32)
    dftl = np.ascontiguousarray(tabs.reshape(2, 4, 4, 128, NLAT).transpose(0, 1, 3, 2, 4)).reshape(2, 4, 128, 4 * NLAT).astype(NPBF)
    lin = np.arange(CTX)
    lout = np.arange(h * NCTX, (h + 1) * NCTX)
    ang = 2 * np.pi * (np.outer(lin, lout) % CTX) / CTX
    tx = (np.stack([np.cos(ang), np.sin(ang)]) / np.sqrt(CTX)).astype(np.float32)
    dftx = np.ascontiguousarray(tx.reshape(2, 2, 128, NCTX).transpose(0, 2, 1, 3)).reshape(2, 128, 2 * NCTX).astype(NPBF)
    return dftc, dftl, dftx


_PROG_CACHE = {}


def _get_prog(stages, cfg):
    key = (tuple(stages), tuple(sorted(cfg.items())))
    if key not in _PROG_CACHE:
        b = Builder(list(stages), dict(cfg))
        nc = b.build()
        _PROG_CACHE[key] = (nc, b)
    return _PROG_CACHE[key]


def _run(stages, cfg, in_maps):
    nc, b = _get_prog(stages, cfg)
    maps = []
    for m in in_maps:
        mm = {}
        for name, (shape, dt) in b.dram_in.items():
            a = m[name]
            want = NPBF if dt == BF16 else np.float32
            a = np.ascontiguousarray(a)
            assert a.shape == shape, (name, a.shape, shape)
            assert a.dtype == want, (name, a.dtype)
            mm[name] = a
        maps.append(mm)
    res = run_bass_kernel_spmd(nc, maps, core_ids=list(range(8)))
    return res.results


def _layer_inputs_A(l, W, cfg, cores):
    KC, H, G = cfg["D"] // 128, cfg["H"], cfg["G"]
    D, DFF, NQ = cfg["D"], cfg["DFF"], cfg["NQ"]
    FQ = DFF // 128 // NQ
    sh = {}
    sh[f"wmod{l}"] = _wtile(W["w_mod"][l], 128, KC)
    sh[f"bmod{l}"] = np.ascontiguousarray(np.repeat(_vec_fm(W["b_mod"][l], 9 * KC)[:, :, None], 2, 2))
    sh.update(_ffn_w(l, 1, W, cfg))
    w_in = W["w_in"][l]
    NA = H * 128
    qkf = np.concatenate([w_in[:, :2 * NA], w_in[:, 3 * NA:]], 1)
    sh[f"winqkf{l}"] = _wtile(qkf, 128, KC)
    vw = 128
    sh[f"winv{l}"] = _wtile(w_in[:, 2 * NA:3 * NA], vw, KC)
    sh[f"qkg{l}"] = np.ascontiguousarray(np.stack([W["q_norm_w"][l], W["k_norm_w"][l]], 1))
    sh.update(_norm_w(l, W, cfg))
    per = []
    for (b, h) in cores:
        m = dict(sh)
        cc = np.stack([W["c"][b], W["c_ctx"]], 1)
        m[f"cT{l}"] = np.ascontiguousarray(cc.reshape(KC, 128, 2).transpose(1, 0, 2))
        per.append(m)
    return per


def _norm_w(l, W, cfg):
    KC = cfg["D"] // 128
    nw = np.stack([_vec_fm(W["norm_w"][l, s], KC) for s in range(3)], 1)
    return {f"nw{l}": np.ascontiguousarray(np.repeat(nw[:, :, :, None], 2, 3))}


def _ffn_w(l, which, W, cfg):
    KC, DFF, NQ = cfg["D"] // 128, cfg["DFF"], cfg["NQ"]
    FC = DFF // 128
    FQ = FC // NQ
    wi = W[f"ffn{which}_wi"][l]
    g = wi[:, :DFF].reshape(KC, 128, FC, 128)
    u = wi[:, DFF:].reshape(KC, 128, FC, 128)
    gu = np.stack([g, u], 3)
    wi_r = np.ascontiguousarray(gu.transpose(2, 3, 1, 0, 4)).reshape(FC * 2, 128, KC * 128)
    wo = W[f"ffn{which}_wo"][l]
    wo_r = np.ascontiguousarray(wo.reshape(NQ, FQ, 128, KC, 128).transpose(0, 3, 2, 1, 4)).reshape(NQ * KC, 128, FQ * 128)
    return {f"wi{which}_{l}": wi_r, f"wo{which}_{l}": wo_r}


def _layer_inputs_B(l, W, cfg, cores, dev):
    KC, H, G = cfg["D"] // 128, cfg["H"], cfg["G"]
    sh = {}
    sh[f"wfour{l}"] = np.ascontiguousarray(W["w_four"][l])
    sh[f"wout{l}"] = _wtile(W["w_out"][l], 128, KC)
    sh.update(_ffn_w(l, 2, W, cfg))
    per = []
    for ci, (b, h) in enumerate(cores):
        m = dict(sh)
        c0, c1 = 2 * b, 2 * b + 1
        m[f"qown{l}"] = dev[ci][f"q{l}"]
        m[f"kown{l}"] = dev[ci][f"k{l}"]
        m[f"vown{l}"] = dev[ci][f"v{l}"]
        m[f"kpair{l}"] = np.stack([dev[c0][f"k{l}"], dev[c1][f"k{l}"]])
        m[f"vpair{l}"] = np.stack([dev[c0][f"v{l}"], dev[c1][f"v{l}"]])
        m[f"fpair{l}"] = np.stack([dev[c0][f"f{l}"], dev[c1][f"f{l}"]])
        m[f"bias{l}"] = _bias_tables(W["rpb"][l], h, H)
        dftc, dftl, dftx = _dft_consts(h)
        m["dftc"], m["dftl"], m["dftx"] = dftc, dftl, dftx
        per.append(m)
    return per


def _run_n(stages, cfg, in_maps, n):
    nc, b = _get_prog(stages, cfg)
    maps = []
    for m in in_maps:
        mm = {}
        for name, (shape, dt) in b.dram_in.items():
            a = np.ascontiguousarray(m[name])
            want = NPBF if dt == BF16 else np.float32
            assert a.shape == shape, (name, a.shape, shape)
            assert a.dtype == want, (name, a.dtype)
            mm[name] = a
        maps.append(mm)
    res = run_bass_kernel_spmd(nc, maps, core_ids=list(range(n)))
    return res.results


def kernel(**inputs):
    cfg = dict(CFG)
    W = {k: np.asarray(v, np.float32) for k, v in inputs.items()}
    D, H, DEPTH = cfg["D"], cfg["H"], cfg["DEPTH"]
    KC = D // 128
    sh = {}
    for l in range(DEPTH):
        a = _layer_inputs_A(l, W, cfg, [(0, 0)])[0]
        ct = a.pop(f"cT{l}")
        sh.update(a)
        sh[f"wfour{l}"] = np.ascontiguousarray(W["w_four"][l])
        sh[f"wout{l}"] = _wtile(W["w_out"][l], 128, KC)
        sh.update(_ffn_w(l, 2, W, cfg))
        for h in range(2):
            sh[f"bias{l}_{h}"] = _bias_tables(W["rpb"][l], h, H)
    for h in range(2):
        dftc, dftl, dftx = _dft_consts(h)
        sh["dftc"], sh[f"dftl_{h}"], sh[f"dftx_{h}"] = dftc, dftl, dftx
    per = []
    for b in range(4):
        m = dict(sh)
        cc = np.stack([W["c"][b], W["c_ctx"]], 1)
        for l in range(DEPTH):
            m[f"cT{l}"] = np.ascontiguousarray(cc.reshape(KC, 128, 2).transpose(1, 0, 2))
        halves = []
        for h in range(2):
            tok = np.concatenate([W["x"][b, h * NLAT:(h + 1) * NLAT], W["ctx"][b, h * NCTX:(h + 1) * NCTX]], 0)
            halves.append(_fm(tok, KC))
        m["xT_in"] = np.stack(halves)
        per.append(m)
    real = [0, 1, 2, 3]
    r = _run_n((("FUSED", 0),), cfg, per, 4)
    out = np.zeros((4, SEQ, D), np.float32)
    for b in range(4):
        y = r[real[b]]["y_out"]
        for h in range(2):
            out[b, h * NLAT:(h + 1) * NLAT] = y[h].transpose(2, 1, 0).reshape(NLAT, D)
    return out


def kernel_unfused(**inputs):
    cfg = dict(CFG)
    W = {k: np.asarray(v, np.float32) for k, v in inputs.items()}
    D = cfg["D"]
    KC = D // 128
    cores = [(b, h) for b in range(4) for h in range(2)]
    per = _layer_inputs_A(0, W, cfg, cores)
    for ci, (b, h) in enumerate(cores):
        tok = np.concatenate([W["x"][b, h * NLAT:(h + 1) * NLAT], W["ctx"][b, h * NCTX:(h + 1) * NCTX]], 0)
        per[ci]["xT_in"] = _fm(tok, KC)
    r0 = _run((("A", 0),), cfg, per)
    perB = _layer_inputs_B(0, W, cfg, cores, r0)
    perA = _layer_inputs_A(1, W, cfg, cores)
    nw0 = _norm_w(0, W, cfg)
    for ci in range(8):
        perB[ci].update(perA[ci])
        perB[ci].update(nw0)
        perB[ci]["xT_in"] = r0[ci]["xT_out"]
        perB[ci]["modT_in"] = r0[ci]["modT_out"]
    r1 = _run((("B", 0), ("A", 1)), cfg, perB)
    del perB, perA
    perB = _layer_inputs_B(1, W, cfg, cores, r1)
    nw1 = _norm_w(1, W, cfg)
    for ci in range(8):
        perB[ci].update(nw1)
        perB[ci]["xT_in"] = r1[ci]["xT_out"]
        perB[ci]["modT_in"] = r1[ci]["modT_out"]
    r2 = _run((("B", 1),), cfg, perB)
    out = np.zeros((4, SEQ, D), np.float32)
    for ci, (b, h) in enumerate(cores):
        y = r2[ci]["y_out"]
        out[b, h * NLAT:(h + 1) * NLAT] = y.transpose(2, 1, 0).reshape(NLAT, D)
    return out
```

```python
import numpy as np
import ml_dtypes
import concourse.bass as bass
import concourse.mybir as mybir
from concourse.bass_utils import run_bass_kernel_spmd
from contextlib import ExitStack

F32, BF16 = mybir.dt.float32, mybir.dt.bfloat16
AF = mybir.ActivationFunctionType
ALU = mybir.AluOpType
NPBF = ml_dtypes.bfloat16

CFG = dict(D=2048, DFF=5632, H=12, G=4, NQ=4, DEPTH=2)
SEQ, CTX, GRID_W, ROWS = 2048, 256, 64, 32
WIN_ROWS, WIN_COLS = 8, 16
NLAT, NCTX = 1024, 128
NT = NLAT + NCTX
TT = [(0, 384), (384, 768), (768, 1152)]
SEGS = [[(0, 384, 0)], [(384, 768, 0)], [(768, 1024, 0), (1024, 1152, 1)]]
EPS = 1e-6
NEG = -30000.0
SLOT = 2048
NSLOT = 4


class Op:
    __slots__ = ("eng", "fn", "deps", "idx", "needs_inc", "chan", "cnt", "is_dma")


class Prog:
    def __init__(self):
        self.ops = []
        self.lastw = {}
        self.readers = {}
        self.chan_cnt = {}

    def add(self, eng, fn, reads=(), writes=(), stream=None):
        op = Op()
        op.eng, op.fn, op.idx = eng, fn, len(self.ops)
        op.needs_inc = False
        op.is_dma = stream is not None
        op.chan = ("dma", stream) if stream is not None else ("eng", eng)
        deps = set()
        for k in reads:
            w = self.lastw.get(k)
            if w is not None:
                deps.add(w)
        for k in writes:
            w = self.lastw.get(k)
            if w is not None:
                deps.add(w)
            for r in self.readers.get(k, {}).values():
                deps.add(r)
        for k in reads:
            self.readers.setdefault(k, {})[op.chan] = op.idx
        for k in writes:
            self.lastw[k] = op.idx
            self.readers[k] = {}
        deps.discard(op.idx)
        op.deps = deps
        self.ops.append(op)
        return op.idx

    def emit(self, nc, es):
        ops = self.ops
        for b in ops:
            keep = set()
            for ai in b.deps:
                a = ops[ai]
                if (not a.is_dma) and (not b.is_dma) and a.eng == "pe" and b.eng == "pe":
                    continue
                keep.add(ai)
                a.needs_inc = True
            b.deps = keep
        for o in ops:
            if o.needs_inc:
                self.chan_cnt[o.chan] = self.chan_cnt.get(o.chan, 0) + 1
                o.cnt = self.chan_cnt[o.chan]
            else:
                o.cnt = None
        sems = {}

        def sem_for(chan, cnt):
            unit = 16 if chan[0] == "dma" else 1
            cap = 1800 if chan[0] == "dma" else 30000
            ep = (cnt - 1) // cap
            key = (chan, ep)
            if key not in sems:
                sems[key] = es.enter_context(nc.semaphore(f"s{len(sems)}"))
            return sems[key], ((cnt - 1) % cap + 1) * unit, unit

        block = es.enter_context(nc.Block())
        engmap = {"pe": block.tensor, "act": block.scalar, "dve": block.vector,
                  "pool": block.gpsimd, "sp": block.sync}
        for ename, deco in engmap.items():
            eops = [o for o in ops if o.eng == ename]

            def section(e, eops=eops):
                waited = {}
                for o in eops:
                    need = {}
                    for ai in o.deps:
                        a = ops[ai]
                        s_, v, _ = sem_for(a.chan, a.cnt)
                        if need.get(id(s_), (None, -1))[1] < v:
                            need[id(s_)] = (s_, v)
                    for sid, (s_, v) in need.items():
                        if waited.get(sid, -1) >= v:
                            continue
                        e.wait_ge(s_, v)
                        waited[sid] = v
                    if o.fn is None:
                        continue
                    ins = o.fn(e)
                    if o.needs_inc:
                        s_, v, unit = sem_for(o.chan, o.cnt)
                        ins.then_inc(s_, unit)

            deco(section)


class Builder:
    def __init__(self, stages, cfg):
        self.cfg = cfg
        self.stages = stages
        self.D, self.DFF, self.H, self.G, self.NQ = cfg["D"], cfg["DFF"], cfg["H"], cfg["G"], cfg["NQ"]
        self.KC = self.D // 128
        self.FC = self.DFF // 128
        self.FQ = self.FC // self.NQ
        self.NM = 9 * self.KC
        self.nc = bass.Bass("TRN2", target_bir_lowering=False)
        self.p = Prog()
        self.dram_in = {}
        self.dram_out = {}
        self.ring_i = 0
        self.uid = 0
        self._aps = {}
        self._sbs = {}
        self.fused = (len(stages) > 0 and stages[0][0] == "FUSED")

    def din(self, name, shape, dt=F32):
        if name in self._aps:
            return self._aps[name]
        t = self.nc.dram_tensor(name, list(shape), dt, kind="ExternalInput").ap()
        self.dram_in[name] = (tuple(shape), dt)
        self._aps[name] = t
        return t

    def dout(self, name, shape, dt=F32):
        if name in self._aps:
            return self._aps[name]
        t = self.nc.dram_tensor(name, list(shape), dt, kind="ExternalOutput").ap()
        self.dram_out[name] = (tuple(shape), dt)
        self._aps[name] = t
        return t

    def dscr(self, name, shape, dt=F32):
        if name in self._aps:
            return self._aps[name]
        t = self.nc.dram_tensor(name, list(shape), dt, kind="Internal").ap()
        self._aps[name] = t
        return t

    def sb(self, name, shape, dt):
        if name in self._sbs:
            return self._sbs[name]
        t = self.es.enter_context(self.nc.sbuf_tensor(name, list(shape), dt))
        self._sbs[name] = t
        return t

    def dma(self, q, out, in_, reads, writes, stream, **kw):
        return self.p.add(q, lambda e: e.dma_start(out=out, in_=in_, **kw), reads, list(writes) + [("strm", stream)], stream=stream)

    def wload(self, src_ap, ncols, tag):
        s = self.ring_i % NSLOT
        self.ring_i += 1
        dst = self.wring[:, s, 0:ncols]
        self.dma("pool", dst, src_ap, [], [("ring", s)], f"ring{s}", max_dma_last_dim=8192)
        return s

    def mm(self, out, lhsT, rhs, start, stop, reads, writes):
        return self.p.add("pe", lambda e: e.matmul(out, lhsT=lhsT, rhs=rhs, start=start, stop=stop), reads, writes)

    def act(self, out, in_, func, reads, writes, **kw):
        return self.p.add("act", lambda e: e.activation(out=out, in_=in_, func=func, **kw), reads, writes)

    def dve(self, fn, reads, writes):
        return self.p.add("dve", fn, reads, writes)

    def build(self):
        nc, cfg = self.nc, self.cfg
        KC, H, G, FC, NM = self.KC, self.H, self.G, self.FC, self.NM
        with ExitStack() as es:
            self.es = es
            self.xT = self.sb("xT", [128, KC, NT], F32)
            self.xm = self.sb("xm", [128, KC, NT], BF16)
            self.ab = self.sb("ab", [128, self.FQ, NT], BF16)
            self.wring = self.sb("wring", [128, NSLOT, SLOT], BF16)
            self.ones = self.sb("ones", [128, 128], BF16)
            self.modT_l = [self.sb(f"modTsb{i}", [128, NM, 2], F32) for i in range(2)]
            self.gs_l = [self.sb(f"gssb{i}", [128, 3, KC, 2], F32) for i in range(2)]
            self.gate_l = [self.sb(f"gatesb{i}", [128, 3, KC, 2], F32) for i in range(2)]
            self.nw_l = [self.sb(f"nwsb{i}", [128, 3, KC, 2], F32) for i in range(2)]
            self.set_par(0)
            self.rt = self.sb("rt", [128, 1, 384], F32)
            self.rstd = self.sb("rstd", [128, 1, 384], F32)
            self.sg = self.sb("sg", [128, 3, 384], F32)
            self.ps = [es.enter_context(nc.psum_tensor(f"ps{i}", [128, 512], F32)) for i in range(8)]
            self.p.add("dve", lambda e: e.memset(self.ones[:], 1.0), [], ["ones"])
            if self.fused:
                self.build_fused()
                self.p.emit(nc, es)
                return nc
            first = True
            for (kind, l) in self.stages:
                last_layer = (l == cfg["DEPTH"] - 1)
                if kind == "A":
                    self.stage_A(l, load_x=first)
                else:
                    self.stage_B(l, load_x=first, final=last_layer)
                first = False
            kind, l = self.stages[-1]
            if kind == "A":
                xo = self.dout("xT_out", [128, KC, NT])
                mo = self.dout("modT_out", [128, NM, 2])
                i1 = self.dma("sp", xo[:, :, :], self.xT[:], [("x", c, t) for c in range(KC) for t in range(3)], ["xo"], "fin0")
                i2 = self.dma("sp", mo[:, :, :], self.modT[:], [("modT", self.par)], ["mo"], "fin1")
                fin = [i1, i2] + self.out_dmas
            else:
                yo = self.dout("y_out", [128, KC, NLAT])
                i1 = self.dma("sp", yo[:, :, :], self.xT[:, :, 0:NLAT], [("x", c, t) for c in range(KC) for t in range(3)], ["yo"], "fin0")
                fin = [i1]
            self.p.add("sp", None, ["xo", "mo", "yo", "qkvf_out"], [])
            self.p.emit(nc, es)
        return nc

    def set_par(self, par):
        self.par = par
        self.modT, self.gs, self.gate, self.nw = self.modT_l[par], self.gs_l[par], self.gate_l[par], self.nw_l[par]

    def build_fused(self):
        KC, DEPTH = self.KC, self.cfg["DEPTH"]
        xin = self.din("xT_in", [2, 128, KC, NT])
        yo = self.dout("y_out", [2, 128, KC, NLAT])
        xs = self.dscr("xs", [2, 128, KC, NT])
        self.epsb = self.sb("epsb", [128, 1], F32)
        self.p.add("dve", lambda e: e.memset(self.epsb[:], EPS), [], ["epsb"])
        XK = [("x", c, t) for c in range(KC) for t in range(3)]

        def xload(src, key):
            for c in range(KC):
                self.dma("sp", self.xT[:, c, :], src[:, c, :], [key], [("x", c, t) for t in range(3)], f"xin{c % 4}")

        def xstore(dst, key):
            self.dma("sp", dst, self.xT[:], XK, [key], "xst")

        KC3 = 3 * KC
        self.mod_prep()
        for l in range(DEPTH):
            last = l == DEPTH - 1
            self.set_par(l % 2)
            if l == 0:
                xload(xin[0], "xin0")
                for it in self.mod_items(0, 0, KC3):
                    it()
                self.mod_finish(0, 0, KC3)
                self.derive_mod(0, subs=(0,))
                rest = self.mod_items(0, KC3, self.NM)
                self.stage_norm(0)
                self.stage_ffn(0, 1, 0, extra=rest)
                self.mod_finish(0, KC3, self.NM)
                self.derive_mod(0, subs=(1, 2), load_nw=False)
                self.stage_norm(1)
                self.stage_inproj(0, 0)
            else:
                self.stage_A_body(l, 0)
            xstore(xs[0], ("xs", 0))
            if l == 0:
                xload(xin[1], "xin1")
            else:
                xload(xs[1], ("xs", 1))
            self.stage_A_body(l, 1)
            self.stage_B(l, load_x=False, final=last, h=1)
            if last:
                self.dma("sp", yo[1], self.xT[:, :, 0:NLAT], XK, ["yo"], "fin0")
            else:
                xstore(xs[1], ("xs", 1))
            xload(xs[0], ("xs", 0))
            nxt = None if last else self.mod_items(l + 1, 0, self.NM)
            self.stage_B(l, load_x=False, final=last, h=0, ffn_extra=nxt)
            if last:
                self.dma("sp", yo[0], self.xT[:, :, 0:NLAT], XK, ["yo"], "fin1")
            else:
                self.set_par((l + 1) % 2)
                self.mod_finish(l + 1, 0, self.NM)
                self.derive_mod(l + 1)
        self.p.add("sp", None, ["yo"], [])

    def stage_A_body(self, l, h):
        self.stage_norm(0)
        self.stage_ffn(l, 1, 0)
        self.stage_norm(1)
        self.stage_inproj(l, h)

    def load_x(self):
        KC = self.KC
        xin = self.din("xT_in", [128, KC, NT])
        for c in range(KC):
            self.dma("sp", self.xT[:, c, :], xin[:, c, :], [], [("x", c, t) for t in range(3)], f"xin{c % 4}")

    def derive_mod(self, l, subs=(0, 1, 2), load_nw=True):
        KC = self.KC
        par = self.par
        modT, gs, gate, nw = self.modT, self.gs, self.gate, self.nw
        if load_nw:
            nwd = self.din(f"nw{l}", [128, 3, KC, 2])
            self.dma("sp", nw[:], nwd[:, :, :, :], [], [("nw", par)], "small")
        for sub in subs:
            sc = modT[:, (3 * sub + 1) * KC:(3 * sub + 2) * KC, :]
            gt = modT[:, (3 * sub + 2) * KC:(3 * sub + 3) * KC, :]
            self.dve(lambda e, sc=sc, sub=sub, gs=gs, nw=nw: e.scalar_tensor_tensor(
                out=gs[:, sub], in0=sc, scalar=1.0, in1=nw[:, sub], op0=ALU.add, op1=ALU.mult),
                [("modT", par), ("nw", par)], [("gs", par, sub)])
            f = 1.0 if sub == 1 else 0.5
            self.dve(lambda e, gt=gt, sub=sub, f=f, gate=gate: e.tensor_scalar(
                out=gate[:, sub], in0=gt, scalar1=f, scalar2=None, op0=ALU.mult),
                [("modT", par)], [("gate", par, sub)])

    def mod_prep(self):
        KC, NM = self.KC, self.NM
        cT = self.din("cT0", [128, KC, 2])
        cs = self.sb("cs", [128, KC, 2], F32)
        self.scT = self.sb("scT", [128, KC, 4], BF16)
        s32 = self.sb("s32", [128, KC, 2], F32)
        scT = self.scT
        self.dma("sp", cs[:], cT[:, :, :], [], ["cs"], "small")
        self.act(s32[:], cs[:], AF.Silu, ["cs"], ["s32"])
        self.act(scT[:, :, 0:2], s32[:], AF.Copy, ["s32"], ["scT"])
        self.dve(lambda e: e.tensor_tensor(out=scT[:, :, 2:4], in0=s32[:], in1=scT[:, :, 0:2], op=ALU.subtract),
                 ["s32", "scT"], ["scT"])

    def mod_items(self, l, j0, j1):
        KC, NM = self.KC, self.NM
        NH = NM // 2
        wm = self.din(f"wmod{l}", [NM, 128, KC * 128])
        scT = self.scT

        def item(j):
            s_ = self.wload(wm[j], KC * 128, "wmod")
            pbi = 7 if j < NH else 6
            jj = j if j < NH else j - NH
            for kc in range(KC):
                self.mm(self.ps[pbi][:, 4 * jj:4 * jj + 4], self.wring[:, s_, kc * 128:(kc + 1) * 128],
                        scT[:, kc, :], kc == 0, kc == KC - 1, [("ring", s_), "scT"], [("ps", pbi)])
        return [(lambda j=j: item(j)) for j in range(j0, j1)]

    def mod_finish(self, l, j0, j1):
        KC, NM = self.KC, self.NM
        NH = NM // 2
        par = self.par
        modT = self.modT
        bm = self.din(f"bmod{l}", [128, NM, 2])
        bms = self.sb("bms", [128, NM, 2], F32)
        self.dma("sp", bms[:], bm[:, :, :], [], ["bms"], "small2")
        for (a, b, pbi, off) in ((j0, min(j1, NH), 7, 0), (max(j0, NH), j1, 6, NH)):
            if b <= a:
                continue
            pv = self.ps[pbi][:, 4 * (a - off):4 * (b - off)].rearrange("p (a b) -> p a b", b=4)
            mo = modT[:, a:b, :]
            self.dve(lambda e, pv=pv, mo=mo, a=a, b=b: e.tensor_tensor(out=mo, in0=pv[:, :, 0:2], in1=bms[:, a:b, :], op=ALU.add),
                     [("ps", pbi), "bms"], [("modT", par)])
            self.dve(lambda e, pv=pv, mo=mo: e.tensor_tensor(out=mo, in0=pv[:, :, 2:4], in1=mo, op=ALU.add),
                     [("ps", pbi), ("modT", par)], [("modT", par)])

    def stage_mod(self, l):
        if not hasattr(self, "scT"):
            self.mod_prep()
        for it in self.mod_items(l, 0, self.NM):
            it()
        self.mod_finish(l, 0, self.NM)

    def stage_norm(self, sub):
        KC = self.KC
        D = self.D
        for t, (t0, t1) in enumerate(TT):
            n = t1 - t0
            xk = [("x", c, t) for c in range(KC)]
            mk = [("xm", c, t) for c in range(KC)]
            self.act(self.xm[:, :, t0:t1], self.xT[:, :, t0:t1], AF.Square, xk, mk)
            pst = self.ps[6]
            for c in range(KC):
                self.mm(pst[:, 0:n], self.ones[:], self.xm[:, c, t0:t1], c == 0, c == KC - 1,
                        ["ones", ("xm", c, t)], [("ps", 6)])
            self.act(self.rt[:, 0, 0:n], pst[:, 0:n], AF.Ln, [("ps", 6)], [("rt", 0)], scale=1.0 / D, bias=self.epsb[:])
            self.act(self.rstd[:, 0, 0:n], self.rt[:, 0, 0:n], AF.Exp, [("rt", 0)], [("rstd", 0)], scale=-0.5)
            for (s0, s1, r) in SEGS[t]:
                for c in range(KC):
                    b = c % 2
                    self.dve(lambda e, c=c, b=b, s0=s0, s1=s1, t0=t0: e.tensor_tensor(
                        out=self.sg[:, b, 0:s1 - s0], in0=self.xT[:, c, s0:s1], in1=self.rstd[:, 0, s0 - t0:s1 - t0], op=ALU.mult),
                        [("x", c, t), ("rstd", 0)], [("sg", b)])
                    sh = self.modT[:, 3 * sub * KC + c, r:r + 1]
                    self.act(self.xm[:, c, s0:s1], self.sg[:, b, 0:s1 - s0], AF.Identity,
                             [("sg", b), ("modT", self.par), ("gs", self.par, sub)], [("xm", c, t)],
                             scale=self.gs[:, sub, c, r:r + 1], bias=sh)

    def resid(self, bi, i, t, sub):
        psb = self.ps[bi]
        t0 = TT[t][0]
        gate, par = self.gate, self.par
        for (s0, s1, r) in SEGS[t]:
            self.dve(lambda e, s0=s0, s1=s1, r=r, gate=gate: e.scalar_tensor_tensor(
                out=self.xT[:, i, s0:s1], in0=psb[:, s0 - t0:s1 - t0], scalar=gate[:, sub, i, r:r + 1],
                in1=self.xT[:, i, s0:s1], op0=ALU.mult, op1=ALU.add),
                [("ps", bi), ("gate", par, sub), ("x", i, t)], [("x", i, t)])

    def stage_ffn(self, l, which, sub, extra=None):
        extra = list(extra) if extra else []
        nslots = 2 * self.FC
        per_slot = -(-len(extra) // nslots) if extra else 0
        KC, FQ, NQ = self.KC, self.FQ, self.NQ
        wi = self.din(f"wi{which}_{l}", [self.FC * 2, 128, KC * 128])
        wo = self.din(f"wo{which}_{l}", [NQ * KC, 128, FQ * 128])
        for qd in range(NQ):
            for jj in range(FQ):
                j = qd * FQ + jj
                for half in range(2):
                    s = self.wload(wi[2 * j + half], KC * 128, "wi")
                    for kc in range(KC):
                        for t, (t0, t1) in enumerate(TT):
                            b = half * 3 + t
                            self.mm(self.ps[b][:, 0:t1 - t0], self.wring[:, s, kc * 128:(kc + 1) * 128],
                                    self.xm[:, kc, t0:t1], kc == 0, kc == KC - 1, [("ring", s), ("xm", kc, t)], [("ps", b)])
                    for _ in range(per_slot):
                        if extra:
                            extra.pop(0)()
                for t, (t0, t1) in enumerate(TT):
                    n = t1 - t0
                    self.act(self.sg[:, t, 0:n], self.ps[t][:, 0:n], AF.Silu, [("ps", t)], [("sg", t)])
                    self.dve(lambda e, t=t, n=n, jj=jj, t0=t0, t1=t1: e.tensor_tensor(
                        out=self.ab[:, jj, t0:t1], in0=self.ps[3 + t][:, 0:n], in1=self.sg[:, t, 0:n], op=ALU.mult),
                        [("ps", 3 + t), ("sg", t)], [("ab", jj, t)])
            for i in range(KC):
                s = self.wload(wo[qd * KC + i], FQ * 128, "wo")
                bb = 3 * (i % 2)
                for kc in range(FQ):
                    for t, (t0, t1) in enumerate(TT):
                        self.mm(self.ps[bb + t][:, 0:t1 - t0], self.wring[:, s, kc * 128:(kc + 1) * 128],
                                self.ab[:, kc, t0:t1], kc == 0, kc == FQ - 1, [("ring", s), ("ab", kc, t)], [("ps", bb + t)])
                for t in range(3):
                    self.resid(bb + t, i, t, sub)
        while extra:
            extra.pop(0)()

    def stage_inproj(self, l, h=None):
        KC, H, G = self.KC, self.H, self.G
        nqkf = 2 * H + G
        w1 = self.din(f"winqkf{l}", [nqkf, 128, KC * 128])
        nvg = H
        vw = 128
        w2 = self.din(f"winv{l}", [nvg, 128, KC * vw])
        qkg = self.din(f"qkg{l}", [128, 2])
        if self.fused:
            qo = self.dscr(f"q{l}_{h}", [128, H, NT], BF16)
            ko = self.dscr(f"k{l}_{h}", [128, H, NT], BF16)
            fo = self.dscr(f"f{l}_{h}", [128, G, NT], F32)
            vo = self.dscr(f"v{l}_{h}", [NT // 128, 128, H * 128], BF16)
            OK = ("qkvf", l, h)
        else:
            qo = self.dout(f"q{l}", [128, H, NT], BF16)
            ko = self.dout(f"k{l}", [128, H, NT], BF16)
            fo = self.dout(f"f{l}", [128, G, NT], F32)
            vo = self.dout(f"v{l}", [NT // 128, 128, H * 128], BF16)
            OK = "qkvf_out"
        qkgs = self.sb("qkgs", [128, 2], F32)
        self.dma("sp", qkgs[:], qkg[:, :], [], ["qkg"], "small")
        stg = self.sb("stg", [128, 2, NT], BF16)
        sq = self.sb("sqq", [128, 1, 384], BF16)
        vst = self.sb("vst", [128, 2, vw], BF16)
        self.out_dmas = []
        def main(ch, hooks):
            s = self.wload(w1[ch], KC * 128, "winqkf")
            pb0 = 4 * (ch % 2)
            pts = {KC // 4: 0, KC // 2: 1, (3 * KC) // 4: 2} if KC >= 4 else {}
            for kc in range(KC):
                if kc in pts and hooks:
                    hooks[pts[kc]]()
                for t, (t0, t1) in enumerate(TT):
                    self.mm(self.ps[pb0 + t][:, 0:t1 - t0], self.wring[:, s, kc * 128:(kc + 1) * 128], self.xm[:, kc, t0:t1],
                            kc == 0, kc == KC - 1, [("ring", s), ("xm", kc, t)], [("ps", pb0 + t)])
            if hooks and not pts:
                for hk in hooks:
                    hk()

        def post_hooks(ch):
            sb_ = ch % 2
            pb0 = 4 * sb_
            if ch < 2 * H:
                which = 0 if ch < H else 1

                def sqr(t):
                    t0, t1 = TT[t]
                    n = t1 - t0
                    self.act(sq[:, 0, 0:n], self.ps[pb0 + t][:, 0:n], AF.Square, [("ps", pb0 + t)], [("sqq", 0)])

                def tile(t):
                    t0, t1 = TT[t]
                    n = t1 - t0
                    self.mm(self.ps[pb0 + 3][:, 0:n], self.ones[:], sq[:, 0, 0:n], True, True, ["ones", ("sqq", 0)], [("ps", pb0 + 3)])
                    self.act(self.rt[:, 0, 0:n], self.ps[pb0 + 3][:, 0:n], AF.Ln, [("ps", pb0 + 3)], [("rt", 0)], scale=1.0 / 128, bias=self.epsb[:])
                    self.act(self.rstd[:, 0, 0:n], self.rt[:, 0, 0:n], AF.Exp, [("rt", 0)], [("rstd", 0)], scale=-0.5)
                    self.dve(lambda e, t=t, n=n, t0=t0, t1=t1: e.scalar_tensor_tensor(
                        out=stg[:, sb_, t0:t1], in0=self.ps[pb0 + t][:, 0:n], scalar=qkgs[:, which:which + 1],
                        in1=self.rstd[:, 0, 0:n], op0=ALU.mult, op1=ALU.mult),
                        [("ps", pb0 + t), ("rstd", 0), "qkg"], [("stg", sb_)])
                    if t < 2:
                        sqr(t + 1)
                    else:
                        dst = qo[:, ch, :] if ch < H else ko[:, ch - H, :]
                        self.dma("sp", dst, stg[:, sb_, :], [("stg", sb_)], [OK], f"qkf{sb_}")
                sqr(0)
                return [lambda t=t: tile(t) for t in range(3)]
            else:
                def ftile(t):
                    t0, t1 = TT[t]
                    n = t1 - t0
                    self.act(self.sg[:, t, 0:n], self.ps[pb0 + t][:, 0:n], AF.Copy, [("ps", pb0 + t)], [("sg", t)])
                    if t == 2:
                        self.dma("sp", fo[:, ch - 2 * H, :].rearrange("p (t n) -> p t n", t=3), self.sg[:],
                                 [("sg", 0), ("sg", 1), ("sg", 2)], [OK], "qkff")
                return [lambda t=t: ftile(t) for t in range(3)]

        hooks = []
        for ch in range(nqkf):
            main(ch, hooks)
            hooks = post_hooks(ch)
        for hk in hooks:
            hk()
        for cg in range(nvg):
            s = self.wload(w2[cg], KC * vw, "winv")
            for tc in range(NT // 128):
                pb = 4 + (tc % 2)
                for kc in range(KC):
                    self.mm(self.ps[pb][:, 0:vw], self.xm[:, kc, tc * 128:(tc + 1) * 128], self.wring[:, s, kc * vw:(kc + 1) * vw],
                            kc == 0, kc == KC - 1, [("ring", s), ("xm", kc, tc // 3)], [("ps", pb)])
                vb = tc % 2
                self.act(vst[:, vb, :], self.ps[pb][:, 0:vw], AF.Copy, [("ps", pb)], [("vst", vb)])
                self.dma("sp", vo[tc, :, cg * vw:(cg + 1) * vw], vst[:, vb, :], [("vst", vb)], [OK], f"vst{vb}")

    def stage_A(self, l, load_x):
        if not hasattr(self, "epsb"):
            self.epsb = self.sb("epsb", [128, 1], F32)
            self.p.add("dve", lambda e: e.memset(self.epsb[:], EPS), [], ["epsb"])
        if load_x:
            self.load_x()
        self.stage_mod(l)
        self.derive_mod(l)
        self.stage_norm(0)
        self.stage_ffn(l, 1, 0)
        self.stage_norm(1)
        self.stage_inproj(l)

    def stage_B(self, l, load_x, final, h=None, ffn_extra=None):
        KC, H, G = self.KC, self.H, self.G
        nc = self.nc
        if not hasattr(self, "epsb"):
            self.epsb = self.sb("epsb", [128, 1], F32)
            self.p.add("dve", lambda e: e.memset(self.epsb[:], EPS), [], ["epsb"])
        if load_x:
            self.load_x()
            mi = self.din("modT_in", [128, self.NM, 2])
            self.dma("sp", self.modT[:], mi[:, :, :], [], [("modT", self.par)], "small")
            self.derive_mod(l)
        if self.fused:
            ksc = [self.dscr(f"k{l}_{r}", [128, H, NT], BF16) for r in range(2)]
            vsc = [self.dscr(f"v{l}_{r}", [NT // 128, 128, H * 128], BF16) for r in range(2)]
            fsc = [self.dscr(f"f{l}_{r}", [128, G, NT], F32) for r in range(2)]
            qd = self.dscr(f"q{l}_{h}", [128, H, NT], BF16)
            kown, vown = ksc[h], vsc[h]
            kp, vp, fp = ksc, vsc, fsc
            bias = self.din(f"bias{l}_{h}", [H, 5, 128, 6 * 128])
            RK = [("qkvf", l, 0), ("qkvf", l, 1)]
        else:
            qd = self.din(f"qown{l}", [128, H, NT], BF16)
            kown = self.din(f"kown{l}", [128, H, NT], BF16)
            vown = self.din(f"vown{l}", [NT // 128, 128, H * 128], BF16)
            kp = self.din(f"kpair{l}", [2, 128, H, NT], BF16)
            vp = self.din(f"vpair{l}", [2, NT // 128, 128, H * 128], BF16)
            fp = self.din(f"fpair{l}", [2, 128, G, NT], F32)
            bias = self.din(f"bias{l}", [H, 5, 128, 6 * 128])
            kp = [kp[0], kp[1]]
            vp = [vp[0], vp[1]]
            fp = [fp[0], fp[1]]
            RK = []
        self.RK = RK
        self.bt = None
        Kt = self.sb("Kt", [128, 2, 18, 128], BF16)
        self.Kt, self.Qt = Kt, None
        Vt = self.sb("Vt", [128, 2, 14, 128], BF16)
        Qt = self.sb("Qt", [128, 2, NT], BF16)
        self.Qt = Qt
        bt = self.sb("bt", [128, 2, 6 * 128], F32)
        self.bt = bt
        tS = self.sb("tS", [128, 6 * 128], F32)
        Pm = self.sb("Pm", [128, 2, 8, 128], BF16)
        rD = self.sb("rD", [128, 128], F32)
        rL = self.sb("rL", [128, 128], F32)
        scale = 128.0 ** -0.5
        units = [(hh, jl) for hh in range(H) for jl in range(9)]
        state = {}

        def emit_S(ui):
            hh, jl = units[ui]
            hb = hh % 2
            kk, vk, qk = ("Kt", hb), ("Vt", hb), ("Qt", hb)
            if jl == 0:
                st = f"att{hb}"
                self.dma("sp", Kt[:, hb, 0:2, :], kp[0][:, hh, 768:1024].rearrange("p (c k) -> p c k", k=128), RK, [kk], st + "a")
                self.dma("sp", Kt[:, hb, 2:10, :], kown[:, hh, 0:1024].rearrange("p (c k) -> p c k", k=128), RK, [kk], st + "b")
                self.dma("sp", Kt[:, hb, 10:12, :], kp[1][:, hh, 0:256].rearrange("p (c k) -> p c k", k=128), RK, [kk], st + "c")
                self.dma("sp", Kt[:, hb, 12, :], kp[0][:, hh, 1024:1152], RK, [kk], st + "d")
                self.dma("sp", Kt[:, hb, 13, :], kp[1][:, hh, 1024:1152], RK, [kk], st + "e")
                hs = slice(hh * 128, (hh + 1) * 128)
                self.dma("sp", Vt[:, hb, 0:2, :], vp[0][6:8, :, hs].rearrange("c p d -> p c d"), RK, [vk], st + "f")
                self.dma("sp", Vt[:, hb, 2:10, :], vown[0:8, :, hs].rearrange("c p d -> p c d"), RK, [vk], st + "g")
                self.dma("sp", Vt[:, hb, 10:12, :], vp[1][0:2, :, hs].rearrange("c p d -> p c d"), RK, [vk], st + "h")
                self.dma("sp", Vt[:, hb, 12, :], vp[0][8, :, hs], RK, [vk], st + "i")
                self.dma("sp", Vt[:, hb, 13, :], vp[1][8, :, hs], RK, [vk], st + "j")
                self.dma("sp", Qt[:, hb, :], qd[:, hh, :], RK, [qk], st + "k")
            ib = ui % 2
            q0 = jl * 128
            if jl < 8:
                wb0, nw = _win(jl)
                slots = [wb0 + i for i in range(nw)] + [12, 13]
                bslot = 0 if jl == 0 else 1 if jl == 1 else 3 if jl == 6 else 4 if jl == 7 else 2
                self.dma("sp", bt[:, ib, :], bias[hh, bslot, :, :], [], [("bt", ib)], f"bt{ib}")
            else:
                slots = [12, 13]
            state[ui] = slots
            bA, bB = self.ps[4 * ib], self.ps[4 * ib + 1]
            iA, iB = 4 * ib, 4 * ib + 1
            for i, sl in enumerate(slots):
                pb, pi, off = (bA, iA, i) if i < 4 else (bB, iB, i - 4)
                self.mm(pb[:, off * 128:(off + 1) * 128], Kt[:, hb, sl, :], Qt[:, hb, q0:q0 + 128], True, True,
                        [kk, qk], [("ps", pi)])
            pk = ("Pm", ib)
            if jl < 8:
                self.dve(lambda e, ib=ib, bA=bA: e.scalar_tensor_tensor(
                    out=tS[:, 0:512], in0=bA[:, 0:512], scalar=scale, in1=bt[:, ib, 0:512], op0=ALU.mult, op1=ALU.add),
                    [("ps", iA), ("bt", ib)], ["tS0"])
                nB = nw - 4
                self.dve(lambda e, ib=ib, bB=bB, nB=nB: e.scalar_tensor_tensor(
                    out=tS[:, 512:512 + nB * 128], in0=bB[:, 0:nB * 128], scalar=scale, in1=bt[:, ib, 512:512 + nB * 128],
                    op0=ALU.mult, op1=ALU.add),
                    [("ps", iB), ("bt", ib)], ["tS1"])
                self.act(Pm[:, ib, 0:nw, :].rearrange("p a b -> p (a b)"), tS[:, 0:nw * 128], AF.Exp, ["tS0", "tS1"], [pk])
                self.act(Pm[:, ib, nw:nw + 2, :].rearrange("p a b -> p (a b)"), bB[:, nB * 128:(nB + 2) * 128], AF.Exp,
                         [("ps", iB)], [pk], scale=scale)
            else:
                self.act(Pm[:, ib, 0:2, :].rearrange("p a b -> p (a b)"), bA[:, 0:256], AF.Exp, [("ps", iA)], [pk], scale=scale)

        def emit_PV(ui):
            hh, jl = units[ui]
            hb = hh % 2
            vk = ("Vt", hb)
            ib = ui % 2
            q0 = jl * 128
            slots = state.pop(ui)
            ns = len(slots)
            bO, bD = self.ps[4 * ib + 2], self.ps[4 * ib + 3]
            iO, iD = 4 * ib + 2, 4 * ib + 3
            pk = ("Pm", ib)
            for i, sl in enumerate(slots):
                self.mm(bO[:, 0:128], Vt[:, hb, sl, :], Pm[:, ib, i, :], i == 0, i == ns - 1, [vk, pk], [("ps", iO)])
            for i, sl in enumerate(slots):
                self.mm(bD[:, 0:128], self.ones[:], Pm[:, ib, i, :], i == 0, i == ns - 1, ["ones", pk], [("ps", iD)])
            self.act(rL[:], bD[:, 0:128], AF.Ln, [("ps", iD)], ["rL"])
            self.act(rD[:], rL[:], AF.Exp, ["rL"], ["rD"], scale=-1.0)
            self.dve(lambda e, bO=bO, hh=hh, q0=q0: e.tensor_tensor(
                out=self.xm[:, hh, q0:q0 + 128], in0=bO[:, 0:128], in1=rD[:], op=ALU.mult),
                [("ps", iO), "rD"], [("xm", hh, q0 // 384)])

        for ui in range(len(units)):
            emit_S(ui)
            if ui > 0:
                emit_PV(ui - 1)
        emit_PV(len(units) - 1)
        self.stage_fourier(l, fp, h)
        wout = self.din(f"wout{l}", [KC, 128, KC * 128])
        for i in range(KC):
            s = self.wload(wout[i], KC * 128, "wout")
            bb = 3 * (i % 2)
            for kc in range(KC):
                for t, (t0, t1) in enumerate(TT):
                    self.mm(self.ps[bb + t][:, 0:t1 - t0], self.wring[:, s, kc * 128:(kc + 1) * 128], self.xm[:, kc, t0:t1],
                            kc == 0, kc == KC - 1, [("ring", s), ("xm", kc, t)], [("ps", bb + t)])
            for t in range(3):
                self.resid(bb + t, i, t, 1)
        self.stage_norm(2)
        self.stage_ffn(l, 2, 2, extra=ffn_extra)

    def stage_fourier(self, l, fp, h=None):
        KC, H, G = self.KC, self.H, self.G
        wf = self.din(f"wfour{l}", [G, 128, 128])
        ccs = self.din("dftc", [128, 2, 128])
        sfx = f"_{h}" if self.fused else ""
        dl = self.din("dftl" + sfx, [2, 4, 128, 4 * NLAT], BF16)
        dcx = self.din("dftx" + sfx, [2, 128, 2 * NCTX], BF16)
        RK = self.RK
        cc = self.sb("cc", [128, 2, 128], F32)
        wfs = self.sb("wfs", [128, G, 128], F32)
        AB = self.sb("AB", [128, 2, 128], F32)
        ff = self.bt[:].rearrange("p a n -> p (a n)")
        Y = self.Kt
        ALK = [("Kt", 0), ("Kt", 1)]
        ALQ = [("bt", 0), ("bt", 1)]
        dx = self.sb("dx", [128, 2, 2 * NCTX], BF16)
        self.dma("sp", cc[:], ccs[:, :, :], [], ["cc"], "small")
        self.dma("sp", wfs[:], wf.rearrange("g c d -> c g d"), [], ["wfs"], "small2")
        self.dma("sp", dx[:, 0, :], dcx[0], [], ["dx"], "dx0")
        self.dma("sp", dx[:, 1, :], dcx[1], [], ["dx"], "dx1")
        for g in range(G):
            for ab in range(2):
                self.mm(self.ps[6][:, ab * 128:(ab + 1) * 128], cc[:, ab, :], wfs[:, g, :], True, True, ["cc", "wfs"], [("ps", 6)])
            self.act(AB[:].rearrange("p a b -> p (a b)"), self.ps[6][:, 0:256], AF.Copy, [("ps", 6)], ["AB"])
            for r in range(2):
                self.dma("sp", ff[:, 0:NT], fp[r][:, g, :], RK, ["ff"] + ALQ, "ffa")
                for lc in range(9):
                    ci = r * 8 + lc if lc < 8 else 16 + r
                    c0 = lc * 128
                    pb = 4 + lc % 2
                    for ab in range(2):
                        self.mm(self.ps[pb][:, ab * 128:(ab + 1) * 128], ff[:, c0:c0 + 128], AB[:, ab, :], True, True, ["ff", "AB"], [("ps", pb)])
                    self.act(Y[:, :, ci, :], self.ps[pb][:, 0:256].rearrange("p (a b) -> p a b", a=2), AF.Copy, [("ps", pb)], [("Y", ci)] + ALK)
            LT = [(0, 384), (384, 768), (768, 1024)]
            for t, (t0, t1) in enumerate(LT):
                n = t1 - t0
                nmm = 0
                for cs in range(2):
                    for qtr in range(4):
                        s = self.ring_i % NSLOT
                        self.ring_i += 1
                        dst = self.wring[:, s, 0:4 * n].rearrange("p (k n) -> p k n", k=4)
                        src = dl[cs, qtr, :, :].rearrange("p (k n) -> p k n", k=4)[:, :, t0:t1]
                        self.dma("sp", dst, src, [], [("ring", s)], f"ring{s}")
                        for k4 in range(4):
                            kc = qtr * 4 + k4
                            self.mm(self.ps[t][:, 0:n], Y[:, cs, kc, :], self.wring[:, s, k4 * n:(k4 + 1) * n],
                                    nmm == 0, nmm == 31, [("Y", kc), ("ring", s)], [("ps", t)])
                            nmm += 1
                self.act(self.xm[:, H + g, t0:t1], self.ps[t][:, 0:n], AF.Copy, [("ps", t)], [("xm", H + g, t)])
            nmm = 0
            for cs in range(2):
                for k2 in range(2):
                    self.mm(self.ps[3][:, 0:128], Y[:, cs, 16 + k2, :], dx[:, cs, k2 * 128:(k2 + 1) * 128],
                            nmm == 0, nmm == 3, [("Y", 16 + k2), "dx"], [("ps", 3)])
                    nmm += 1
            self.act(self.xm[:, H + g, NLAT:NT], self.ps[3][:, 0:128], AF.Copy, [("ps", 3)], [("xm", H + g, 2)])


def _fm(a, kc):
    T = a.shape[0]
    return np.ascontiguousarray(a.reshape(T, kc, 128).transpose(2, 1, 0))


def _vec_fm(v, kc):
    return np.ascontiguousarray(v.reshape(kc, 128).T)


def _wtile(w, cols, kc):
    K, N = w.shape
    g = N // cols
    return np.ascontiguousarray(w.reshape(kc, 128, g, cols).transpose(2, 1, 0, 3)).reshape(g, 128, kc * cols)


def _win(jl):
    if jl == 0:
        return 0, 6
    if jl == 7:
        return 6, 6
    return jl, 5


def _bias_tables(rpb_l, h, H):
    out = np.full((H, 5, 128, 6, 128), NEG, np.float32)
    jl_of_slot = [0, 1, 2, 6, 7]
    kk = np.arange(128)
    qq = np.arange(128)
    for sl, jl in enumerate(jl_of_slot):
        J = 8 * h + jl
        r = 2 * J + qq // 64
        cq = qq % 64
        rs = np.clip(r - WIN_ROWS // 2, 0, ROWS - WIN_ROWS)
        cstart = np.clip(cq - WIN_COLS // 2, 0, GRID_W - WIN_COLS)
        wb0, nw = _win(jl)
        for i in range(nw):
            gc = wb0 + i + 8 * h - 2
            if gc < 0 or gc > 15:
                continue
            kr = 2 * gc + kk // 64
            ck = kk % 64
            valid = (kr[:, None] >= rs[None, :]) & (kr[:, None] < rs[None, :] + WIN_ROWS) & \
                    (ck[:, None] >= cstart[None, :]) & (ck[:, None] < cstart[None, :] + WIN_COLS)
            dr = np.clip(kr[:, None] - r[None, :] + WIN_ROWS - 1, 0, 2 * WIN_ROWS - 2)
            dc = np.clip(ck[:, None] - cq[None, :], -(WIN_COLS - 1), WIN_COLS - 1) + WIN_COLS - 1
            vals = rpb_l[:, dr, dc]
            out[:, sl, :, i, :] = np.where(valid[None], vals, NEG)
    return out.reshape(H, 5, 128, 6 * 128)


def _dft_consts(h):
    c = np.arange(128)
    ang = 2 * np.pi * np.outer(c, c) / 128
    dftc = (np.stack([np.cos(ang), -np.sin(ang)], 1) / np.sqrt(128)).astype(np.float32)
    lin = np.arange(SEQ)
    lout = np.arange(h * NLAT, (h + 1) * NLAT)
    ang = 2 * np.pi * (np.outer(lin, lout) % SEQ) / SEQ
    tabs = (np.stack([np.cos(ang), np.sin(ang)]) / np.sqrt(SEQ)).astype(np.float32)
    dftl = np.ascontiguousarray(tabs.reshape(2, 4, 4, 128, NLAT).transpose(0, 1, 3, 2, 4)).reshape(2, 4, 128, 4 * NLAT).astype(NPBF)
    lin = np.arange(CTX)
    lout = np.arange(h * NCTX, (h + 1) * NCTX)
    ang = 2 * np.pi * (np.outer(lin, lout) % CTX) / CTX
    tx = (np.stack([np.cos(ang), np.sin(ang)]) / np.sqrt(CTX)).astype(np.float32)
    dftx = np.ascontiguousarray(tx.reshape(2, 2, 128, NCTX).transpose(0, 2, 1, 3)).reshape(2, 128, 2 * NCTX).astype(NPBF)
    return dftc, dftl, dftx


_PROG_CACHE = {}


def _get_prog(stages, cfg):
    key = (tuple(stages), tuple(sorted(cfg.items())))
    if key not in _PROG_CACHE:
        b = Builder(list(stages), dict(cfg))
        nc = b.build()
        _PROG_CACHE[key] = (nc, b)
    return _PROG_CACHE[key]


def _run(stages, cfg, in_maps):
    nc, b = _get_prog(stages, cfg)
    maps = []
    for m in in_maps:
        mm = {}
        for name, (shape, dt) in b.dram_in.items():
            a = m[name]
            want = NPBF if dt == BF16 else np.float32
            a = np.ascontiguousarray(a)
            assert a.shape == shape, (name, a.shape, shape)
            assert a.dtype == want, (name, a.dtype)
            mm[name] = a
        maps.append(mm)
    res = run_bass_kernel_spmd(nc, maps, core_ids=list(range(8)))
    return res.results


def _layer_inputs_A(l, W, cfg, cores):
    KC, H, G = cfg["D"] // 128, cfg["H"], cfg["G"]
    D, DFF, NQ = cfg["D"], cfg["DFF"], cfg["NQ"]
    FQ = DFF // 128 // NQ
    sh = {}
    sh[f"wmod{l}"] = _wtile(W["w_mod"][l], 128, KC)
    sh[f"bmod{l}"] = np.ascontiguousarray(np.repeat(_vec_fm(W["b_mod"][l], 9 * KC)[:, :, None], 2, 2))
    sh.update(_ffn_w(l, 1, W, cfg))
    w_in = W["w_in"][l]
    NA = H * 128
    qkf = np.concatenate([w_in[:, :2 * NA], w_in[:, 3 * NA:]], 1)
    sh[f"winqkf{l}"] = _wtile(qkf, 128, KC)
    vw = 128
    sh[f"winv{l}"] = _wtile(w_in[:, 2 * NA:3 * NA], vw, KC)
    sh[f"qkg{l}"] = np.ascontiguousarray(np.stack([W["q_norm_w"][l], W["k_norm_w"][l]], 1))
    sh.update(_norm_w(l, W, cfg))
    per = []
    for (b, h) in cores:
        m = dict(sh)
        cc = np.stack([W["c"][b], W["c_ctx"]], 1)
        m[f"cT{l}"] = np.ascontiguousarray(cc.reshape(KC, 128, 2).transpose(1, 0, 2))
        per.append(m)
    return per


def _norm_w(l, W, cfg):
    KC = cfg["D"] // 128
    nw = np.stack([_vec_fm(W["norm_w"][l, s], KC) for s in range(3)], 1)
    return {f"nw{l}": np.ascontiguousarray(np.repeat(nw[:, :, :, None], 2, 3))}


def _ffn_w(l, which, W, cfg):
    KC, DFF, NQ = cfg["D"] // 128, cfg["DFF"], cfg["NQ"]
    FC = DFF // 128
    FQ = FC // NQ
    wi = W[f"ffn{which}_wi"][l]
    g = wi[:, :DFF].reshape(KC, 128, FC, 128)
    u = wi[:, DFF:].reshape(KC, 128, FC, 128)
    gu = np.stack([g, u], 3)
    wi_r = np.ascontiguousarray(gu.transpose(2, 3, 1, 0, 4)).reshape(FC * 2, 128, KC * 128)
    wo = W[f"ffn{which}_wo"][l]
    wo_r = np.ascontiguousarray(wo.reshape(NQ, FQ, 128, KC, 128).transpose(0, 3, 2, 1, 4)).reshape(NQ * KC, 128, FQ * 128)
    return {f"wi{which}_{l}": wi_r, f"wo{which}_{l}": wo_r}


def _layer_inputs_B(l, W, cfg, cores, dev):
    KC, H, G = cfg["D"] // 128, cfg["H"], cfg["G"]
    sh = {}
    sh[f"wfour{l}"] = np.ascontiguousarray(W["w_four"][l])
    sh[f"wout{l}"] = _wtile(W["w_out"][l], 128, KC)
    sh.update(_ffn_w(l, 2, W, cfg))
    per = []
    for ci, (b, h) in enumerate(cores):
        m = dict(sh)
        c0, c1 = 2 * b, 2 * b + 1
        m[f"qown{l}"] = dev[ci][f"q{l}"]
        m[f"kown{l}"] = dev[ci][f"k{l}"]
        m[f"vown{l}"] = dev[ci][f"v{l}"]
        m[f"kpair{l}"] = np.stack([dev[c0][f"k{l}"], dev[c1][f"k{l}"]])
        m[f"vpair{l}"] = np.stack([dev[c0][f"v{l}"], dev[c1][f"v{l}"]])
        m[f"fpair{l}"] = np.stack([dev[c0][f"f{l}"], dev[c1][f"f{l}"]])
        m[f"bias{l}"] = _bias_tables(W["rpb"][l], h, H)
        dftc, dftl, dftx = _dft_consts(h)
        m["dftc"], m["dftl"], m["dftx"] = dftc, dftl, dftx
        per.append(m)
    return per


def _run_n(stages, cfg, in_maps, n):
    nc, b = _get_prog(stages, cfg)
    maps = []
    for m in in_maps:
        mm = {}
        for name, (shape, dt) in b.dram_in.items():
            a = np.ascontiguousarray(m[name])
            want = NPBF if dt == BF16 else np.float32
            assert a.shape == shape, (name, a.shape, shape)
            assert a.dtype == want, (name, a.dtype)
            mm[name] = a
        maps.append(mm)
    res = run_bass_kernel_spmd(nc, maps, core_ids=list(range(n)))
    return res.results


def kernel(**inputs):
    cfg = dict(CFG)
    W = {k: np.asarray(v, np.float32) for k, v in inputs.items()}
    D, H, DEPTH = cfg["D"], cfg["H"], cfg["DEPTH"]
    KC = D // 128
    sh = {}
    for l in range(DEPTH):
        a = _layer_inputs_A(l, W, cfg, [(0, 0)])[0]
        ct = a.pop(f"cT{l}")
        sh.update(a)
        sh[f"wfour{l}"] = np.ascontiguousarray(W["w_four"][l])
        sh[f"wout{l}"] = _wtile(W["w_out"][l], 128, KC)
        sh.update(_ffn_w(l, 2, W, cfg))
        for h in range(2):
            sh[f"bias{l}_{h}"] = _bias_tables(W["rpb"][l], h, H)
    for h in range(2):
        dftc, dftl, dftx = _dft_consts(h)
        sh["dftc"], sh[f"dftl_{h}"], sh[f"dftx_{h}"] = dftc, dftl, dftx
    per = []
    for b in range(4):
        m = dict(sh)
        cc = np.stack([W["c"][b], W["c_ctx"]], 1)
        for l in range(DEPTH):
            m[f"cT{l}"] = np.ascontiguousarray(cc.reshape(KC, 128, 2).transpose(1, 0, 2))
        halves = []
        for h in range(2):
            tok = np.concatenate([W["x"][b, h * NLAT:(h + 1) * NLAT], W["ctx"][b, h * NCTX:(h + 1) * NCTX]], 0)
            halves.append(_fm(tok, KC))
        m["xT_in"] = np.stack(halves)
        per.append(m)
    real = [0, 1, 2, 3]
    r = _run_n((("FUSED", 0),), cfg, per, 4)
    out = np.zeros((4, SEQ, D), np.float32)
    for b in range(4):
        y = r[real[b]]["y_out"]
        for h in range(2):
            out[b, h * NLAT:(h + 1) * NLAT] = y[h].transpose(2, 1, 0).reshape(NLAT, D)
    return out


def kernel_unfused(**inputs):
    cfg = dict(CFG)
    W = {k: np.asarray(v, np.float32) for k, v in inputs.items()}
    D = cfg["D"]
    KC = D // 128
    cores = [(b, h) for b in range(4) for h in range(2)]
    per = _layer_inputs_A(0, W, cfg, cores)
    for ci, (b, h) in enumerate(cores):
        tok = np.concatenate([W["x"][b, h * NLAT:(h + 1) * NLAT], W["ctx"][b, h * NCTX:(h + 1) * NCTX]], 0)
        per[ci]["xT_in"] = _fm(tok, KC)
    r0 = _run((("A", 0),), cfg, per)
    perB = _layer_inputs_B(0, W, cfg, cores, r0)
    perA = _layer_inputs_A(1, W, cfg, cores)
    nw0 = _norm_w(0, W, cfg)
    for ci in range(8):
        perB[ci].update(perA[ci])
        perB[ci].update(nw0)
        perB[ci]["xT_in"] = r0[ci]["xT_out"]
        perB[ci]["modT_in"] = r0[ci]["modT_out"]
    r1 = _run((("B", 0), ("A", 1)), cfg, perB)
    del perB, perA
    perB = _layer_inputs_B(1, W, cfg, cores, r1)
    nw1 = _norm_w(1, W, cfg)
    for ci in range(8):
        perB[ci].update(nw1)
        perB[ci]["xT_in"] = r1[ci]["xT_out"]
        perB[ci]["modT_in"] = r1[ci]["modT_out"]
    r2 = _run((("B", 1),), cfg, perB)
    out = np.zeros((4, SEQ, D), np.float32)
    for ci, (b, h) in enumerate(cores):
        y = r2[ci]["y_out"]
        out[b, h * NLAT:(h + 1) * NLAT] = y.transpose(2, 1, 0).reshape(NLAT, D)
    return out
```

```python
import numpy as np
import ml_dtypes
import concourse.bass as bass
import concourse.mybir as mybir
from concourse.bass_utils import run_bass_kernel_spmd
from contextlib import ExitStack

F32, BF16 = mybir.dt.float32, mybir.dt.bfloat16
AF = mybir.ActivationFunctionType
ALU = mybir.AluOpType
NPBF = ml_dtypes.bfloat16

CFG = dict(D=2048, DFF=5632, H=12, G=4, NQ=4, DEPTH=2)
SEQ, CTX, GRID_W, ROWS = 2048, 256, 64, 32
WIN_ROWS, WIN_COLS = 8, 16
NLAT, NCTX = 1024, 128
NT = NLAT + NCTX
TT = [(0, 384), (384, 768), (768, 1152)]
SEGS = [[(0, 384, 0)], [(384, 768, 0)], [(768, 1024, 0), (1024, 1152, 1)]]
EPS = 1e-6
NEG = -30000.0
SLOT = 2048
NSLOT = 4


class Op:
    __slots__ = ("eng", "fn", "deps", "idx", "needs_inc", "chan", "cnt", "is_dma")


class Prog:
    def __init__(self):
        self.ops = []
        self.lastw = {}
        self.readers = {}
        self.chan_cnt = {}

    def add(self, eng, fn, reads=(), writes=(), stream=None):
        op = Op()
        op.eng, op.fn, op.idx = eng, fn, len(self.ops)
        op.needs_inc = False
        op.is_dma = stream is not None
        op.chan = ("dma", stream) if stream is not None else ("eng", eng)
        deps = set()
        for k in reads:
            w = self.lastw.get(k)
            if w is not None:
                deps.add(w)
        for k in writes:
            w = self.lastw.get(k)
            if w is not None:
                deps.add(w)
            for r in self.readers.get(k, {}).values():
                deps.add(r)
        for k in reads:
            self.readers.setdefault(k, {})[op.chan] = op.idx
        for k in writes:
            self.lastw[k] = op.idx
            self.readers[k] = {}
        deps.discard(op.idx)
        op.deps = deps
        self.ops.append(op)
        return op.idx

    def emit(self, nc, es):
        ops = self.ops
        for b in ops:
            keep = set()
            for ai in b.deps:
                a = ops[ai]
                if (not a.is_dma) and (not b.is_dma) and a.eng == "pe" and b.eng == "pe":
                    continue
                keep.add(ai)
                a.needs_inc = True
            b.deps = keep
        for o in ops:
            if o.needs_inc:
                self.chan_cnt[o.chan] = self.chan_cnt.get(o.chan, 0) + 1
                o.cnt = self.chan_cnt[o.chan]
            else:
                o.cnt = None
        sems = {}

        def sem_for(chan, cnt):
            unit = 16 if chan[0] == "dma" else 1
            cap = 1800 if chan[0] == "dma" else 30000
            ep = (cnt - 1) // cap
            key = (chan, ep)
            if key not in sems:
                sems[key] = es.enter_context(nc.semaphore(f"s{len(sems)}"))
            return sems[key], ((cnt - 1) % cap + 1) * unit, unit

        block = es.enter_context(nc.Block())
        engmap = {"pe": block.tensor, "act": block.scalar, "dve": block.vector,
                  "pool": block.gpsimd, "sp": block.sync}
        for ename, deco in engmap.items():
            eops = [o for o in ops if o.eng == ename]

            def section(e, eops=eops):
                waited = {}
                for o in eops:
                    need = {}
                    for ai in o.deps:
                        a = ops[ai]
                        s_, v, _ = sem_for(a.chan, a.cnt)
                        if need.get(id(s_), (None, -1))[1] < v:
                            need[id(s_)] = (s_, v)
                    for sid, (s_, v) in need.items():
                        if waited.get(sid, -1) >= v:
                            continue
                        e.wait_ge(s_, v)
                        waited[sid] = v
                    if o.fn is None:
                        continue
                    ins = o.fn(e)
                    if o.needs_inc:
                        s_, v, unit = sem_for(o.chan, o.cnt)
                        ins.then_inc(s_, unit)

            deco(section)


class Builder:
    def __init__(self, stages, cfg):
        self.cfg = cfg
        self.stages = stages
        self.D, self.DFF, self.H, self.G, self.NQ = cfg["D"], cfg["DFF"], cfg["H"], cfg["G"], cfg["NQ"]
        self.KC = self.D // 128
        self.FC = self.DFF // 128
        self.FQ = self.FC // self.NQ
        self.NM = 9 * self.KC
        self.nc = bass.Bass("TRN2", target_bir_lowering=False)
        self.p = Prog()
        self.dram_in = {}
        self.dram_out = {}
        self.ring_i = 0
        self.uid = 0
        self._aps = {}
        self._sbs = {}
        self.okeys = {}
        self.fused = (len(stages) > 0 and stages[0][0] == "FUSED")

    def din(self, name, shape, dt=F32):
        if name in self._aps:
            return self._aps[name]
        t = self.nc.dram_tensor(name, list(shape), dt, kind="ExternalInput").ap()
        self.dram_in[name] = (tuple(shape), dt)
        self._aps[name] = t
        return t

    def dout(self, name, shape, dt=F32):
        if name in self._aps:
            return self._aps[name]
        t = self.nc.dram_tensor(name, list(shape), dt, kind="ExternalOutput").ap()
        self.dram_out[name] = (tuple(shape), dt)
        self._aps[name] = t
        return t

    def dscr(self, name, shape, dt=F32):
        if name in self._aps:
            return self._aps[name]
        t = self.nc.dram_tensor(name, list(shape), dt, kind="Internal").ap()
        self._aps[name] = t
        return t

    def sb(self, name, shape, dt):
        if name in self._sbs:
            return self._sbs[name]
        t = self.es.enter_context(self.nc.sbuf_tensor(name, list(shape), dt))
        self._sbs[name] = t
        return t

    def dma(self, q, out, in_, reads, writes, stream, **kw):
        return self.p.add(q, lambda e: e.dma_start(out=out, in_=in_, **kw), reads, list(writes) + [("strm", stream)], stream=stream)

    def wload(self, src_ap, ncols, tag):
        s = self.ring_i % NSLOT
        self.ring_i += 1
        dst = self.wring[:, s, 0:ncols]
        self.dma("pool", dst, src_ap, [], [("ring", s)], f"ring{s}", max_dma_last_dim=8192)
        return s

    def mm(self, out, lhsT, rhs, start, stop, reads, writes):
        return self.p.add("pe", lambda e: e.matmul(out, lhsT=lhsT, rhs=rhs, start=start, stop=stop), reads, writes)

    def act(self, out, in_, func, reads, writes, **kw):
        return self.p.add("act", lambda e: e.activation(out=out, in_=in_, func=func, **kw), reads, writes)

    def dve(self, fn, reads, writes):
        return self.p.add("dve", fn, reads, writes)

    def build(self):
        nc, cfg = self.nc, self.cfg
        KC, H, G, FC, NM = self.KC, self.H, self.G, self.FC, self.NM
        with ExitStack() as es:
            self.es = es
            self.xT = self.sb("xT", [128, KC, NT], F32)
            self.xm = self.sb("xm", [128, KC, NT], BF16)
            self.ab = self.sb("ab", [128, self.FQ, NT], BF16)
            self.wring = self.sb("wring", [128, NSLOT, SLOT], BF16)
            self.ones = self.sb("ones", [128, 128], BF16)
            self.modT_l = [self.sb(f"modTsb{i}", [128, NM, 2], F32) for i in range(2)]
            self.gs_l = [self.sb(f"gssb{i}", [128, 3, KC, 2], F32) for i in range(2)]
            self.gate_l = [self.sb(f"gatesb{i}", [128, 3, KC, 2], F32) for i in range(2)]
            self.nw_l = [self.sb(f"nwsb{i}", [128, 3, KC, 2], F32) for i in range(2)]
            self.set_par(0)
            self.rt = self.sb("rt", [128, 1, 384], F32)
            self.rstd = self.sb("rstd", [128, 1, 384], F32)
            self.sg = self.sb("sg", [128, 3, 384], F32)
            self.ps = [es.enter_context(nc.psum_tensor(f"ps{i}", [128, 512], F32)) for i in range(8)]
            self.p.add("dve", lambda e: e.memset(self.ones[:], 1.0), [], ["ones"])
            if self.fused:
                self.build_fused()
                self.p.emit(nc, es)
                return nc
            first = True
            for (kind, l) in self.stages:
                last_layer = (l == cfg["DEPTH"] - 1)
                if kind == "A":
                    self.stage_A(l, load_x=first)
                else:
                    self.stage_B(l, load_x=first, final=last_layer)
                first = False
            kind, l = self.stages[-1]
            if kind == "A":
                xo = self.dout("xT_out", [128, KC, NT])
                mo = self.dout("modT_out", [128, NM, 2])
                i1 = self.dma("sp", xo[:, :, :], self.xT[:], [("x", c, t) for c in range(KC) for t in range(3)], ["xo"], "fin0")
                i2 = self.dma("sp", mo[:, :, :], self.modT[:], [("modT", self.par)], ["mo"], "fin1")
                fin = [i1, i2] + self.out_dmas
            else:
                yo = self.dout("y_out", [128, KC, NLAT])
                i1 = self.dma("sp", yo[:, :, :], self.xT[:, :, 0:NLAT], [("x", c, t) for c in range(KC) for t in range(3)], ["yo"], "fin0")
                fin = [i1]
            self.p.add("sp", None, ["xo", "mo", "yo", "qkvf_out"], [])
            self.p.emit(nc, es)
        return nc

    def set_par(self, par):
        self.par = par
        self.modT, self.gs, self.gate, self.nw = self.modT_l[par], self.gs_l[par], self.gate_l[par], self.nw_l[par]

    def build_fused(self):
        KC, DEPTH = self.KC, self.cfg["DEPTH"]
        xin = self.din("xT_in", [2, 128, KC, NT])
        yo = self.dout("y_out", [2, 128, KC, NLAT])
        xs = self.dscr("xs", [2, 128, KC, NT])
        self.epsb = self.sb("epsb", [128, 1], F32)
        self.p.add("dve", lambda e: e.memset(self.epsb[:], EPS), [], ["epsb"])
        XK = [("x", c, t) for c in range(KC) for t in range(3)]

        def xload(src, key, q="sp"):
            for c in range(KC):
                self.dma(q, self.xT[:, c, :], src[:, c, :], [key], [("x", c, t) for t in range(3)], f"xin{c % 4}")

        def xstore(dst, key):
            self.dma("sp", dst, self.xT[:], XK, [key], "xst")

        def a_body(l, h, after_ffn=None, after_norm1=None):
            self.stage_norm(0)
            self.stage_ffn(l, 1, 0)
            if after_ffn:
                after_ffn()
            self.stage_norm(1)
            if after_norm1:
                after_norm1()
            self.stage_inproj(l, h)

        KC3 = 3 * KC
        self.mod_prep()
        for l in range(DEPTH):
            last = l == DEPTH - 1
            self.set_par(l % 2)
            if l == 0:
                xload(xin[0], "xin0")
                for it in self.mod_items(0, 0, KC3):
                    it()
                self.mod_finish(0, 0, KC3)
                self.derive_mod(0, subs=(0,))
                rest = self.mod_items(0, KC3, self.NM)
                self.stage_norm(0)
                self.stage_ffn(0, 1, 0, extra=rest)
                xstore(xs[0], ("xs", 0))
                self.mod_finish(0, KC3, self.NM)
                self.derive_mod(0, subs=(1, 2), load_nw=False)
                self.stage_norm(1)
                xload(xin[1], "xin1")
                self.stage_inproj(0, 0)
            else:
                a_body(l, 0, after_ffn=lambda: xstore(xs[0], ("xs", 0)), after_norm1=lambda: xload(xs[1], ("xs", 1)))
            a_body(l, 1)
            self.stage_B(l, load_x=False, final=last, h=1)
            if last:
                self.dma("sp", yo[1], self.xT[:, :, 0:NLAT], XK, ["yo"], "fin0")
            else:
                xstore(xs[1], ("xs", 1))
            xload(xs[0], ("xs", 0), q="pool")
            nxt = None if last else self.mod_items(l + 1, 0, self.NM)
            self.stage_B(l, load_x=False, final=last, h=0, ffn_extra=nxt)
            if last:
                self.dma("sp", yo[0], self.xT[:, :, 0:NLAT], XK, ["yo"], "fin1")
            else:
                self.set_par((l + 1) % 2)
                self.mod_finish(l + 1, 0, self.NM)
                self.derive_mod(l + 1)
        self.p.add("sp", None, ["yo"], [])

    def stage_A_body(self, l, h):
        self.stage_norm(0)
        self.stage_ffn(l, 1, 0)
        self.stage_norm(1)
        self.stage_inproj(l, h)

    def load_x(self):
        KC = self.KC
        xin = self.din("xT_in", [128, KC, NT])
        for c in range(KC):
            self.dma("sp", self.xT[:, c, :], xin[:, c, :], [], [("x", c, t) for t in range(3)], f"xin{c % 4}")

    def derive_mod(self, l, subs=(0, 1, 2), load_nw=True):
        KC = self.KC
        par = self.par
        modT, gs, gate, nw = self.modT, self.gs, self.gate, self.nw
        if load_nw:
            nwd = self.din(f"nw{l}", [128, 3, KC, 2])
            self.dma("sp", nw[:], nwd[:, :, :, :], [], [("nw", par)], "small")
        for sub in subs:
            sc = modT[:, (3 * sub + 1) * KC:(3 * sub + 2) * KC, :]
            gt = modT[:, (3 * sub + 2) * KC:(3 * sub + 3) * KC, :]
            self.dve(lambda e, sc=sc, sub=sub, gs=gs, nw=nw: e.scalar_tensor_tensor(
                out=gs[:, sub], in0=sc, scalar=1.0, in1=nw[:, sub], op0=ALU.add, op1=ALU.mult),
                [("modT", par), ("nw", par)], [("gs", par, sub)])
            f = 1.0 if sub == 1 else 0.5
            self.dve(lambda e, gt=gt, sub=sub, f=f, gate=gate: e.tensor_scalar(
                out=gate[:, sub], in0=gt, scalar1=f, scalar2=None, op0=ALU.mult),
                [("modT", par)], [("gate", par, sub)])

    def mod_prep(self):
        KC, NM = self.KC, self.NM
        cT = self.din("cT0", [128, KC, 2])
        cs = self.sb("cs", [128, KC, 2], F32)
        self.scT = self.sb("scT", [128, KC, 4], BF16)
        s32 = self.sb("s32", [128, KC, 2], F32)
        scT = self.scT
        self.dma("sp", cs[:], cT[:, :, :], [], ["cs"], "small")
        self.act(s32[:], cs[:], AF.Silu, ["cs"], ["s32"])
        self.act(scT[:, :, 0:2], s32[:], AF.Copy, ["s32"], ["scT"])
        self.dve(lambda e: e.tensor_tensor(out=scT[:, :, 2:4], in0=s32[:], in1=scT[:, :, 0:2], op=ALU.subtract),
                 ["s32", "scT"], ["scT"])

    def mod_items(self, l, j0, j1):
        KC, NM = self.KC, self.NM
        NH = NM // 2
        wm = self.din(f"wmod{l}", [NM, 128, KC * 128])
        scT = self.scT

        def item(j):
            s_ = self.wload(wm[j], KC * 128, "wmod")
            pbi = 7 if j < NH else 6
            jj = j if j < NH else j - NH
            for kc in range(KC):
                self.mm(self.ps[pbi][:, 4 * jj:4 * jj + 4], self.wring[:, s_, kc * 128:(kc + 1) * 128],
                        scT[:, kc, :], kc == 0, kc == KC - 1, [("ring", s_), "scT"], [("ps", pbi)])
        return [(lambda j=j: item(j)) for j in range(j0, j1)]

    def mod_finish(self, l, j0, j1):
        KC, NM = self.KC, self.NM
        NH = NM // 2
        par = self.par
        modT = self.modT
        bm = self.din(f"bmod{l}", [128, NM, 2])
        bms = self.sb("bms", [128, NM, 2], F32)
        self.dma("sp", bms[:], bm[:, :, :], [], ["bms"], "small2")
        for (a, b, pbi, off) in ((j0, min(j1, NH), 7, 0), (max(j0, NH), j1, 6, NH)):
            if b <= a:
                continue
            pv = self.ps[pbi][:, 4 * (a - off):4 * (b - off)].rearrange("p (a b) -> p a b", b=4)
            mo = modT[:, a:b, :]
            self.dve(lambda e, pv=pv, mo=mo, a=a, b=b: e.tensor_tensor(out=mo, in0=pv[:, :, 0:2], in1=bms[:, a:b, :], op=ALU.add),
                     [("ps", pbi), "bms"], [("modT", par)])
            self.dve(lambda e, pv=pv, mo=mo: e.tensor_tensor(out=mo, in0=pv[:, :, 2:4], in1=mo, op=ALU.add),
                     [("ps", pbi), ("modT", par)], [("modT", par)])

    def stage_mod(self, l):
        if not hasattr(self, "scT"):
            self.mod_prep()
        for it in self.mod_items(l, 0, self.NM):
            it()
        self.mod_finish(l, 0, self.NM)

    def stage_norm(self, sub):
        KC = self.KC
        D = self.D
        for t, (t0, t1) in enumerate(TT):
            n = t1 - t0
            xk = [("x", c, t) for c in range(KC)]
            mk = [("xm", c, t) for c in range(KC)]
            self.act(self.xm[:, :, t0:t1], self.xT[:, :, t0:t1], AF.Square, xk, mk)
            pst = self.ps[6]
            for c in range(KC):
                self.mm(pst[:, 0:n], self.ones[:], self.xm[:, c, t0:t1], c == 0, c == KC - 1,
                        ["ones", ("xm", c, t)], [("ps", 6)])
            self.act(self.rt[:, 0, 0:n], pst[:, 0:n], AF.Ln, [("ps", 6)], [("rt", 0)], scale=1.0 / D, bias=self.epsb[:])
            self.act(self.rstd[:, 0, 0:n], self.rt[:, 0, 0:n], AF.Exp, [("rt", 0)], [("rstd", 0)], scale=-0.5)
            for (s0, s1, r) in SEGS[t]:
                for c in range(KC):
                    b = c % 2
                    self.dve(lambda e, c=c, b=b, s0=s0, s1=s1, t0=t0: e.tensor_tensor(
                        out=self.sg[:, b, 0:s1 - s0], in0=self.xT[:, c, s0:s1], in1=self.rstd[:, 0, s0 - t0:s1 - t0], op=ALU.mult),
                        [("x", c, t), ("rstd", 0)], [("sg", b)])
                    sh = self.modT[:, 3 * sub * KC + c, r:r + 1]
                    self.act(self.xm[:, c, s0:s1], self.sg[:, b, 0:s1 - s0], AF.Identity,
                             [("sg", b), ("modT", self.par), ("gs", self.par, sub)], [("xm", c, t)],
                             scale=self.gs[:, sub, c, r:r + 1], bias=sh)

    def resid(self, bi, i, t, sub):
        psb = self.ps[bi]
        t0 = TT[t][0]
        gate, par = self.gate, self.par
        for (s0, s1, r) in SEGS[t]:
            self.dve(lambda e, s0=s0, s1=s1, r=r, gate=gate: e.scalar_tensor_tensor(
                out=self.xT[:, i, s0:s1], in0=psb[:, s0 - t0:s1 - t0], scalar=gate[:, sub, i, r:r + 1],
                in1=self.xT[:, i, s0:s1], op0=ALU.mult, op1=ALU.add),
                [("ps", bi), ("gate", par, sub), ("x", i, t)], [("x", i, t)])

    def stage_ffn(self, l, which, sub, extra=None):
        extra = list(extra) if extra else []
        nslots = 2 * self.FC
        per_slot = -(-len(extra) // nslots) if extra else 0
        KC, FQ, NQ = self.KC, self.FQ, self.NQ
        wi = self.din(f"wi{which}_{l}", [self.FC * 2, 128, KC * 128])
        wo = self.din(f"wo{which}_{l}", [NQ * KC, 128, FQ * 128])
        for qd in range(NQ):
            for jj in range(FQ):
                j = qd * FQ + jj
                for half in range(2):
                    s = self.wload(wi[2 * j + half], KC * 128, "wi")
                    for kc in range(KC):
                        for t, (t0, t1) in enumerate(TT):
                            b = half * 3 + t
                            self.mm(self.ps[b][:, 0:t1 - t0], self.wring[:, s, kc * 128:(kc + 1) * 128],
                                    self.xm[:, kc, t0:t1], kc == 0, kc == KC - 1, [("ring", s), ("xm", kc, t)], [("ps", b)])
                    for _ in range(per_slot):
                        if extra:
                            extra.pop(0)()
                for t, (t0, t1) in enumerate(TT):
                    n = t1 - t0
                    self.act(self.sg[:, t, 0:n], self.ps[t][:, 0:n], AF.Silu, [("ps", t)], [("sg", t)])
                    self.dve(lambda e, t=t, n=n, jj=jj, t0=t0, t1=t1: e.tensor_tensor(
                        out=self.ab[:, jj, t0:t1], in0=self.ps[3 + t][:, 0:n], in1=self.sg[:, t, 0:n], op=ALU.mult),
                        [("ps", 3 + t), ("sg", t)], [("ab", jj, t)])
            for i in range(KC):
                s = self.wload(wo[qd * KC + i], FQ * 128, "wo")
                bb = 3 * (i % 2)
                for kc in range(FQ):
                    for t, (t0, t1) in enumerate(TT):
                        self.mm(self.ps[bb + t][:, 0:t1 - t0], self.wring[:, s, kc * 128:(kc + 1) * 128],
                                self.ab[:, kc, t0:t1], kc == 0, kc == FQ - 1, [("ring", s), ("ab", kc, t)], [("ps", bb + t)])
                for t in range(3):
                    self.resid(bb + t, i, t, sub)
        while extra:
            extra.pop(0)()

    def okey(self, l, h, tag):
        if not self.fused:
            return "qkvf_out"
        k = ("qkvf", l, h, tag)
        self.okeys[(l, h)].append(k)
        return k

    def stage_inproj(self, l, h=None):
        KC, H, G = self.KC, self.H, self.G
        nqkf = 2 * H + G
        w1 = self.din(f"winqkf{l}", [nqkf, 128, KC * 128])
        nvg = H
        vw = 128
        w2 = self.din(f"winv{l}", [nvg, 128, KC * vw])
        qkg = self.din(f"qkg{l}", [128, 2])
        if self.fused:
            qo = self.dscr(f"q{l}_{h}", [128, H, NT], BF16)
            ko = self.dscr(f"k{l}_{h}", [128, H, NT], BF16)
            fo = self.dscr(f"f{l}_{h}", [128, G, NT], F32)
            vo = self.dscr(f"v{l}_{h}", [NT // 128, 128, H * 128], BF16)
            OK = ("qkvf", l, h)
            self.okeys.setdefault((l, h), [])
        else:
            qo = self.dout(f"q{l}", [128, H, NT], BF16)
            ko = self.dout(f"k{l}", [128, H, NT], BF16)
            fo = self.dout(f"f{l}", [128, G, NT], F32)
            vo = self.dout(f"v{l}", [NT // 128, 128, H * 128], BF16)
            OK = "qkvf_out"
        qkgs = self.sb("qkgs", [128, 2], F32)
        self.dma("sp", qkgs[:], qkg[:, :], [], ["qkg"], "small")
        stg = self.sb("stg", [128, 2, NT], BF16)
        sq = self.sb("sqq", [128, 1, 384], BF16)
        self.out_dmas = []
        def main(ch, hooks):
            s = self.wload(w1[ch], KC * 128, "winqkf")
            pb0 = 4 * (ch % 2)
            pts = {KC // 4: 0, KC // 2: 1, (3 * KC) // 4: 2} if KC >= 4 else {}
            for kc in range(KC):
                if kc in pts and hooks:
                    hooks[pts[kc]]()
                for t, (t0, t1) in enumerate(TT):
                    self.mm(self.ps[pb0 + t][:, 0:t1 - t0], self.wring[:, s, kc * 128:(kc + 1) * 128], self.xm[:, kc, t0:t1],
                            kc == 0, kc == KC - 1, [("ring", s), ("xm", kc, t)], [("ps", pb0 + t)])
            if hooks and not pts:
                for hk in hooks:
                    hk()

        def post_hooks(ch):
            sb_ = ch % 2
            pb0 = 4 * sb_
            if ch < 2 * H:
                which = 0 if ch < H else 1

                def sqr(t):
                    t0, t1 = TT[t]
                    n = t1 - t0
                    self.act(sq[:, 0, 0:n], self.ps[pb0 + t][:, 0:n], AF.Square, [("ps", pb0 + t)], [("sqq", 0)])

                def tile(t):
                    t0, t1 = TT[t]
                    n = t1 - t0
                    self.mm(self.ps[pb0 + 3][:, 0:n], self.ones[:], sq[:, 0, 0:n], True, True, ["ones", ("sqq", 0)], [("ps", pb0 + 3)])
                    self.act(self.rt[:, 0, 0:n], self.ps[pb0 + 3][:, 0:n], AF.Ln, [("ps", pb0 + 3)], [("rt", 0)], scale=1.0 / 128, bias=self.epsb[:])
                    self.act(self.rstd[:, 0, 0:n], self.rt[:, 0, 0:n], AF.Exp, [("rt", 0)], [("rstd", 0)], scale=-0.5)
                    self.dve(lambda e, t=t, n=n, t0=t0, t1=t1: e.scalar_tensor_tensor(
                        out=stg[:, sb_, t0:t1], in0=self.ps[pb0 + t][:, 0:n], scalar=qkgs[:, which:which + 1],
                        in1=self.rstd[:, 0, 0:n], op0=ALU.mult, op1=ALU.mult),
                        [("ps", pb0 + t), ("rstd", 0), "qkg"], [("stg", sb_)])
                    if t < 2:
                        sqr(t + 1)
                    else:
                        dst = qo[:, ch, :] if ch < H else ko[:, ch - H, :]
                        self.dma("sp", dst, stg[:, sb_, :], [("stg", sb_)], [self.okey(l, h, ("qk", ch))], f"qkf{sb_}")
                sqr(0)
                return [lambda t=t: tile(t) for t in range(3)]
            else:
                def ftile(t):
                    t0, t1 = TT[t]
                    n = t1 - t0
                    self.act(self.sg[:, t, 0:n], self.ps[pb0 + t][:, 0:n], AF.Copy, [("ps", pb0 + t)], [("sg", t)])
                    if t == 2:
                        self.dma("sp", fo[:, ch - 2 * H, :].rearrange("p (t n) -> p t n", t=3), self.sg[:],
                                 [("sg", 0), ("sg", 1), ("sg", 2)], [self.okey(l, h, ("f", ch))], "qkff")
                return [lambda t=t: ftile(t) for t in range(3)]

        hooks = []
        for ch in range(nqkf):
            main(ch, hooks)
            hooks = post_hooks(ch)
        for hk in hooks:
            hk()
        NTC = NT // 128
        for cg in range(nvg):
            s = self.wload(w2[cg], KC * vw, "winv")
            sb_ = cg % 2
            for tc in range(NTC):
                pb = 4 + (tc % 2)
                for kc in range(KC):
                    self.mm(self.ps[pb][:, 0:vw], self.xm[:, kc, tc * 128:(tc + 1) * 128], self.wring[:, s, kc * vw:(kc + 1) * vw],
                            kc == 0, kc == KC - 1, [("ring", s), ("xm", kc, tc // 3)], [("ps", pb)])
                self.act(stg[:, sb_, tc * 128:(tc + 1) * 128], self.ps[pb][:, 0:vw], AF.Copy, [("ps", pb)], [("stg", sb_)])
            self.dma("sp", vo[:, :, cg * vw:(cg + 1) * vw].rearrange("c p n -> p c n"),
                     stg[:, sb_, :].rearrange("p (c n) -> p c n", n=128), [("stg", sb_)], [self.okey(l, h, ("v", cg))], f"qkf{sb_}")

    def stage_A(self, l, load_x):
        if not hasattr(self, "epsb"):
            self.epsb = self.sb("epsb", [128, 1], F32)
            self.p.add("dve", lambda e: e.memset(self.epsb[:], EPS), [], ["epsb"])
        if load_x:
            self.load_x()
        self.stage_mod(l)
        self.derive_mod(l)
        self.stage_norm(0)
        self.stage_ffn(l, 1, 0)
        self.stage_norm(1)
        self.stage_inproj(l)

    def stage_B(self, l, load_x, final, h=None, ffn_extra=None):
        KC, H, G = self.KC, self.H, self.G
        nc = self.nc
        if not hasattr(self, "epsb"):
            self.epsb = self.sb("epsb", [128, 1], F32)
            self.p.add("dve", lambda e: e.memset(self.epsb[:], EPS), [], ["epsb"])
        if load_x:
            self.load_x()
            mi = self.din("modT_in", [128, self.NM, 2])
            self.dma("sp", self.modT[:], mi[:, :, :], [], [("modT", self.par)], "small")
            self.derive_mod(l)
        if self.fused:
            ksc = [self.dscr(f"k{l}_{r}", [128, H, NT], BF16) for r in range(2)]
            vsc = [self.dscr(f"v{l}_{r}", [NT // 128, 128, H * 128], BF16) for r in range(2)]
            fsc = [self.dscr(f"f{l}_{r}", [128, G, NT], F32) for r in range(2)]
            qd = self.dscr(f"q{l}_{h}", [128, H, NT], BF16)
            kown, vown = ksc[h], vsc[h]
            kp, vp, fp = ksc, vsc, fsc
            bias = self.din(f"bias{l}_{h}", [H, 5, 128, 6 * 128])
            RK = list(self.okeys[(l, 0)]) + list(self.okeys[(l, 1)])
        else:
            qd = self.din(f"qown{l}", [128, H, NT], BF16)
            kown = self.din(f"kown{l}", [128, H, NT], BF16)
            vown = self.din(f"vown{l}", [NT // 128, 128, H * 128], BF16)
            kp = self.din(f"kpair{l}", [2, 128, H, NT], BF16)
            vp = self.din(f"vpair{l}", [2, NT // 128, 128, H * 128], BF16)
            fp = self.din(f"fpair{l}", [2, 128, G, NT], F32)
            bias = self.din(f"bias{l}", [H, 5, 128, 6 * 128])
            kp = [kp[0], kp[1]]
            vp = [vp[0], vp[1]]
            fp = [fp[0], fp[1]]
            RK = []
        self.RK = RK
        self.bt = None
        Kt = self.sb("Kt", [128, 2, 18, 128], BF16)
        self.Kt, self.Qt = Kt, None
        Vt = self.sb("Vt", [128, 2, 14, 128], BF16)
        Qt = self.sb("Qt", [128, 2, NT], BF16)
        self.Qt = Qt
        bt = self.sb("bt", [128, 2, 6 * 128], F32)
        self.bt = bt
        tS = self.sb("tS", [128, 6 * 128], F32)
        Pm = self.sb("Pm", [128, 2, 8, 128], BF16)
        rD = self.sb("rD", [128, 128], F32)
        rL = self.sb("rL", [128, 128], F32)
        scale = 128.0 ** -0.5
        units = [(hh, jl) for hh in range(H) for jl in range(9)]
        state = {}

        def emit_S(ui):
            hh, jl = units[ui]
            hb = hh % 2
            kk, vk, qk = ("Kt", hb), ("Vt", hb), ("Qt", hb)
            if jl == 0:
                st = f"att{hb}"
                self.dma("sp", Kt[:, hb, 0:2, :], kp[0][:, hh, 768:1024].rearrange("p (c k) -> p c k", k=128), RK, [kk], st + "a")
                self.dma("sp", Kt[:, hb, 2:10, :], kown[:, hh, 0:1024].rearrange("p (c k) -> p c k", k=128), RK, [kk], st + "b")
                self.dma("sp", Kt[:, hb, 10:12, :], kp[1][:, hh, 0:256].rearrange("p (c k) -> p c k", k=128), RK, [kk], st + "c")
                self.dma("sp", Kt[:, hb, 12, :], kp[0][:, hh, 1024:1152], RK, [kk], st + "d")
                self.dma("sp", Kt[:, hb, 13, :], kp[1][:, hh, 1024:1152], RK, [kk], st + "e")
                hs = slice(hh * 128, (hh + 1) * 128)
                self.dma("sp", Vt[:, hb, 0:2, :], vp[0][6:8, :, hs].rearrange("c p d -> p c d"), RK, [vk], st + "f")
                self.dma("sp", Vt[:, hb, 2:10, :], vown[0:8, :, hs].rearrange("c p d -> p c d"), RK, [vk], st + "g")
                self.dma("sp", Vt[:, hb, 10:12, :], vp[1][0:2, :, hs].rearrange("c p d -> p c d"), RK, [vk], st + "h")
                self.dma("sp", Vt[:, hb, 12, :], vp[0][8, :, hs], RK, [vk], st + "i")
                self.dma("sp", Vt[:, hb, 13, :], vp[1][8, :, hs], RK, [vk], st + "j")
                self.dma("sp", Qt[:, hb, :], qd[:, hh, :], RK, [qk], st + "k")
            ib = ui % 2
            q0 = jl * 128
            if jl < 8:
                wb0, nw = _win(jl)
                slots = [wb0 + i for i in range(nw)] + [12, 13]
                bslot = 0 if jl == 0 else 1 if jl == 1 else 3 if jl == 6 else 4 if jl == 7 else 2
                self.dma("sp", bt[:, ib, :], bias[hh, bslot, :, :], [], [("bt", ib)], f"bt{ib}")
            else:
                slots = [12, 13]
            state[ui] = slots
            bA, bB = self.ps[4 * ib], self.ps[4 * ib + 1]
            iA, iB = 4 * ib, 4 * ib + 1
            for i, sl in enumerate(slots):
                pb, pi, off = (bA, iA, i) if i < 4 else (bB, iB, i - 4)
                self.mm(pb[:, off * 128:(off + 1) * 128], Kt[:, hb, sl, :], Qt[:, hb, q0:q0 + 128], True, True,
                        [kk, qk], [("ps", pi)])
            pk = ("Pm", ib)
            if jl < 8:
                self.dve(lambda e, ib=ib, bA=bA: e.scalar_tensor_tensor(
                    out=tS[:, 0:512], in0=bA[:, 0:512], scalar=scale, in1=bt[:, ib, 0:512], op0=ALU.mult, op1=ALU.add),
                    [("ps", iA), ("bt", ib)], ["tS0"])
                nB = nw - 4
                self.dve(lambda e, ib=ib, bB=bB, nB=nB: e.scalar_tensor_tensor(
                    out=tS[:, 512:512 + nB * 128], in0=bB[:, 0:nB * 128], scalar=scale, in1=bt[:, ib, 512:512 + nB * 128],
                    op0=ALU.mult, op1=ALU.add),
                    [("ps", iB), ("bt", ib)], ["tS1"])
                self.act(Pm[:, ib, 0:nw, :].rearrange("p a b -> p (a b)"), tS[:, 0:nw * 128], AF.Exp, ["tS0", "tS1"], [pk])
                self.act(Pm[:, ib, nw:nw + 2, :].rearrange("p a b -> p (a b)"), bB[:, nB * 128:(nB + 2) * 128], AF.Exp,
                         [("ps", iB)], [pk], scale=scale)
            else:
                self.act(Pm[:, ib, 0:2, :].rearrange("p a b -> p (a b)"), bA[:, 0:256], AF.Exp, [("ps", iA)], [pk], scale=scale)

        def emit_PV(ui):
            hh, jl = units[ui]
            hb = hh % 2
            vk = ("Vt", hb)
            ib = ui % 2
            q0 = jl * 128
            slots = state.pop(ui)
            ns = len(slots)
            bO, bD = self.ps[4 * ib + 2], self.ps[4 * ib + 3]
            iO, iD = 4 * ib + 2, 4 * ib + 3
            pk = ("Pm", ib)
            for i, sl in enumerate(slots):
                self.mm(bO[:, 0:128], Vt[:, hb, sl, :], Pm[:, ib, i, :], i == 0, i == ns - 1, [vk, pk], [("ps", iO)])
            for i, sl in enumerate(slots):
                self.mm(bD[:, 0:128], self.ones[:], Pm[:, ib, i, :], i == 0, i == ns - 1, ["ones", pk], [("ps", iD)])
            self.act(rL[:], bD[:, 0:128], AF.Ln, [("ps", iD)], ["rL"])
            self.act(rD[:], rL[:], AF.Exp, ["rL"], ["rD"], scale=-1.0)
            self.dve(lambda e, bO=bO, hh=hh, q0=q0: e.tensor_tensor(
                out=self.xm[:, hh, q0:q0 + 128], in0=bO[:, 0:128], in1=rD[:], op=ALU.mult),
                [("ps", iO), "rD"], [("xm", hh, q0 // 384)])

        for ui in range(len(units)):
            emit_S(ui)
            if ui > 0:
                emit_PV(ui - 1)
        emit_PV(len(units) - 1)
        self.stage_fourier(l, fp, h)
        wout = self.din(f"wout{l}", [KC, 128, KC * 128])
        for i in range(KC):
            s = self.wload(wout[i], KC * 128, "wout")
            bb = 3 * (i % 2)
            for kc in range(KC):
                for t, (t0, t1) in enumerate(TT):
                    self.mm(self.ps[bb + t][:, 0:t1 - t0], self.wring[:, s, kc * 128:(kc + 1) * 128], self.xm[:, kc, t0:t1],
                            kc == 0, kc == KC - 1, [("ring", s), ("xm", kc, t)], [("ps", bb + t)])
            for t in range(3):
                self.resid(bb + t, i, t, 1)
        self.stage_norm(2)
        self.stage_ffn(l, 2, 2, extra=ffn_extra)

    def stage_fourier(self, l, fp, h=None):
        KC, H, G = self.KC, self.H, self.G
        wf = self.din(f"wfour{l}", [G, 128, 128])
        ccs = self.din("dftc", [128, 2, 128])
        sfx = f"_{h}" if self.fused else ""
        dl = self.din("dftl" + sfx, [2, 4, 128, 4 * NLAT], BF16)
        dcx = self.din("dftx" + sfx, [2, 128, 2 * NCTX], BF16)
        RK = self.RK
        cc = self.sb("cc", [128, 2, 128], F32)
        wfs = self.sb("wfs", [128, G, 128], F32)
        AB = self.sb("AB", [128, 2, 128], F32)
        ff = self.bt[:].rearrange("p a n -> p (a n)")
        Y = self.Kt
        ALK = [("Kt", 0), ("Kt", 1)]
        ALQ = [("bt", 0), ("bt", 1)]
        dx = self.sb("dx", [128, 2, 2 * NCTX], BF16)
        self.dma("sp", cc[:], ccs[:, :, :], [], ["cc"], "small")
        self.dma("sp", wfs[:], wf.rearrange("g c d -> c g d"), [], ["wfs"], "small2")
        self.dma("sp", dx[:, 0, :], dcx[0], [], ["dx"], "dx0")
        self.dma("sp", dx[:, 1, :], dcx[1], [], ["dx"], "dx1")
        for g in range(G):
            for ab in range(2):
                self.mm(self.ps[6][:, ab * 128:(ab + 1) * 128], cc[:, ab, :], wfs[:, g, :], True, True, ["cc", "wfs"], [("ps", 6)])
            self.act(AB[:].rearrange("p a b -> p (a b)"), self.ps[6][:, 0:256], AF.Copy, [("ps", 6)], ["AB"])
            for r in range(2):
                self.dma("sp", ff[:, 0:NT], fp[r][:, g, :], RK, ["ff"] + ALQ, "ffa")
                for lc in range(9):
                    ci = r * 8 + lc if lc < 8 else 16 + r
                    c0 = lc * 128
                    pb = 4 + lc % 2
                    for ab in range(2):
                        self.mm(self.ps[pb][:, ab * 128:(ab + 1) * 128], ff[:, c0:c0 + 128], AB[:, ab, :], True, True, ["ff", "AB"], [("ps", pb)])
                    self.act(Y[:, :, ci, :], self.ps[pb][:, 0:256].rearrange("p (a b) -> p a b", a=2), AF.Copy, [("ps", pb)], [("Y", ci)] + ALK)
            LT = [(0, 384), (384, 768), (768, 1024)]
            for t, (t0, t1) in enumerate(LT):
                n = t1 - t0
                nmm = 0
                for cs in range(2):
                    for qtr in range(4):
                        s = self.ring_i % NSLOT
                        self.ring_i += 1
                        dst = self.wring[:, s, 0:4 * n].rearrange("p (k n) -> p k n", k=4)
                        src = dl[cs, qtr, :, :].rearrange("p (k n) -> p k n", k=4)[:, :, t0:t1]
                        self.dma("sp", dst, src, [], [("ring", s)], f"ring{s}")
                        for k4 in range(4):
                            kc = qtr * 4 + k4
                            self.mm(self.ps[t][:, 0:n], Y[:, cs, kc, :], self.wring[:, s, k4 * n:(k4 + 1) * n],
                                    nmm == 0, nmm == 31, [("Y", kc), ("ring", s)], [("ps", t)])
                            nmm += 1
                self.act(self.xm[:, H + g, t0:t1], self.ps[t][:, 0:n], AF.Copy, [("ps", t)], [("xm", H + g, t)])
            nmm = 0
            for cs in range(2):
                for k2 in range(2):
                    self.mm(self.ps[3][:, 0:128], Y[:, cs, 16 + k2, :], dx[:, cs, k2 * 128:(k2 + 1) * 128],
                            nmm == 0, nmm == 3, [("Y", 16 + k2), "dx"], [("ps", 3)])
                    nmm += 1
            self.act(self.xm[:, H + g, NLAT:NT], self.ps[3][:, 0:128], AF.Copy, [("ps", 3)], [("xm", H + g, 2)])


def _fm(a, kc):
    T = a.shape[0]
    return np.ascontiguousarray(a.reshape(T, kc, 128).transpose(2, 1, 0))


def _vec_fm(v, kc):
    return np.ascontiguousarray(v.reshape(kc, 128).T)


def _wtile(w, cols, kc):
    K, N = w.shape
    g = N // cols
    return np.ascontiguousarray(w.reshape(kc, 128, g, cols).transpose(2, 1, 0, 3)).reshape(g, 128, kc * cols)


def _win(jl):
    if jl == 0:
        return 0, 6
    if jl == 7:
        return 6, 6
    return jl, 5


def _bias_tables(rpb_l, h, H):
    out = np.full((H, 5, 128, 6, 128), NEG, np.float32)
    jl_of_slot = [0, 1, 2, 6, 7]
    kk = np.arange(128)
    qq = np.arange(128)
    for sl, jl in enumerate(jl_of_slot):
        J = 8 * h + jl
        r = 2 * J + qq // 64
        cq = qq % 64
        rs = np.clip(r - WIN_ROWS // 2, 0, ROWS - WIN_ROWS)
        cstart = np.clip(cq - WIN_COLS // 2, 0, GRID_W - WIN_COLS)
        wb0, nw = _win(jl)
        for i in range(nw):
            gc = wb0 + i + 8 * h - 2
            if gc < 0 or gc > 15:
                continue
            kr = 2 * gc + kk // 64
            ck = kk % 64
            valid = (kr[:, None] >= rs[None, :]) & (kr[:, None] < rs[None, :] + WIN_ROWS) & \
                    (ck[:, None] >= cstart[None, :]) & (ck[:, None] < cstart[None, :] + WIN_COLS)
            dr = np.clip(kr[:, None] - r[None, :] + WIN_ROWS - 1, 0, 2 * WIN_ROWS - 2)
            dc = np.clip(ck[:, None] - cq[None, :], -(WIN_COLS - 1), WIN_COLS - 1) + WIN_COLS - 1
            vals = rpb_l[:, dr, dc]
            out[:, sl, :, i, :] = np.where(valid[None], vals, NEG)
    return out.reshape(H, 5, 128, 6 * 128)


def _dft_consts(h):
    c = np.arange(128)
    ang = 2 * np.pi * np.outer(c, c) / 128
    dftc = (np.stack([np.cos(ang), -np.sin(ang)], 1) / np.sqrt(128)).astype(np.float32)
    lin = np.arange(SEQ)
    lout = np.arange(h * NLAT, (h + 1) * NLAT)
    ang = 2 * np.pi * (np.outer(lin, lout) % SEQ) / SEQ
    tabs = (np.stack([np.cos(ang), np.sin(ang)]) / np.sqrt(SEQ)).astype(np.float32)
    dftl = np.ascontiguousarray(tabs.reshape(2, 4, 4, 128, NLAT).transpose(0, 1, 3, 2, 4)).reshape(2, 4, 128, 4 * NLAT).astype(NPBF)
    lin = np.arange(CTX)
    lout = np.arange(h * NCTX, (h + 1) * NCTX)
    ang = 2 * np.pi * (np.outer(lin, lout) % CTX) / CTX
    tx = (np.stack([np.cos(ang), np.sin(ang)]) / np.sqrt(CTX)).astype(np.float32)
    dftx = np.ascontiguousarray(tx.reshape(2, 2, 128, NCTX).transpose(0, 2, 1, 3)).reshape(2, 128, 2 * NCTX).astype(NPBF)
    return dftc, dftl, dftx


_PROG_CACHE = {}


def _get_prog(stages, cfg):
    key = (tuple(stages), tuple(sorted(cfg.items())))
    if key not in _PROG_CACHE:
        b = Builder(list(stages), dict(cfg))
        nc = b.build()
        _PROG_CACHE[key] = (nc, b)
    return _PROG_CACHE[key]


def _run(stages, cfg, in_maps):
    nc, b = _get_prog(stages, cfg)
    maps = []
    for m in in_maps:
        mm = {}
        for name, (shape, dt) in b.dram_in.items():
            a = m[name]
            want = NPBF if dt == BF16 else np.float32
            a = np.ascontiguousarray(a)
            assert a.shape == shape, (name, a.shape, shape)
            assert a.dtype == want, (name, a.dtype)
            mm[name] = a
        maps.append(mm)
    res = run_bass_kernel_spmd(nc, maps, core_ids=list(range(8)))
    return res.results


def _layer_inputs_A(l, W, cfg, cores):
    KC, H, G = cfg["D"] // 128, cfg["H"], cfg["G"]
    D, DFF, NQ = cfg["D"], cfg["DFF"], cfg["NQ"]
    FQ = DFF // 128 // NQ
    sh = {}
    sh[f"wmod{l}"] = _wtile(W["w_mod"][l], 128, KC)
    sh[f"bmod{l}"] = np.ascontiguousarray(np.repeat(_vec_fm(W["b_mod"][l], 9 * KC)[:, :, None], 2, 2))
    sh.update(_ffn_w(l, 1, W, cfg))
    w_in = W["w_in"][l]
    NA = H * 128
    qkf = np.concatenate([w_in[:, :2 * NA], w_in[:, 3 * NA:]], 1)
    sh[f"winqkf{l}"] = _wtile(qkf, 128, KC)
    vw = 128
    sh[f"winv{l}"] = _wtile(w_in[:, 2 * NA:3 * NA], vw, KC)
    sh[f"qkg{l}"] = np.ascontiguousarray(np.stack([W["q_norm_w"][l], W["k_norm_w"][l]], 1))
    sh.update(_norm_w(l, W, cfg))
    per = []
    for (b, h) in cores:
        m = dict(sh)
        cc = np.stack([W["c"][b], W["c_ctx"]], 1)
        m[f"cT{l}"] = np.ascontiguousarray(cc.reshape(KC, 128, 2).transpose(1, 0, 2))
        per.append(m)
    return per


def _norm_w(l, W, cfg):
    KC = cfg["D"] // 128
    nw = np.stack([_vec_fm(W["norm_w"][l, s], KC) for s in range(3)], 1)
    return {f"nw{l}": np.ascontiguousarray(np.repeat(nw[:, :, :, None], 2, 3))}


def _ffn_w(l, which, W, cfg):
    KC, DFF, NQ = cfg["D"] // 128, cfg["DFF"], cfg["NQ"]
    FC = DFF // 128
    FQ = FC // NQ
    wi = W[f"ffn{which}_wi"][l]
    g = wi[:, :DFF].reshape(KC, 128, FC, 128)
    u = wi[:, DFF:].reshape(KC, 128, FC, 128)
    gu = np.stack([g, u], 3)
    wi_r = np.ascontiguousarray(gu.transpose(2, 3, 1, 0, 4)).reshape(FC * 2, 128, KC * 128)
    wo = W[f"ffn{which}_wo"][l]
    wo_r = np.ascontiguousarray(wo.reshape(NQ, FQ, 128, KC, 128).transpose(0, 3, 2, 1, 4)).reshape(NQ * KC, 128, FQ * 128)
    return {f"wi{which}_{l}": wi_r, f"wo{which}_{l}": wo_r}


def _layer_inputs_B(l, W, cfg, cores, dev):
    KC, H, G = cfg["D"] // 128, cfg["H"], cfg["G"]
    sh = {}
    sh[f"wfour{l}"] = np.ascontiguousarray(W["w_four"][l])
    sh[f"wout{l}"] = _wtile(W["w_out"][l], 128, KC)
    sh.update(_ffn_w(l, 2, W, cfg))
    per = []
    for ci, (b, h) in enumerate(cores):
        m = dict(sh)
        c0, c1 = 2 * b, 2 * b + 1
        m[f"qown{l}"] = dev[ci][f"q{l}"]
        m[f"kown{l}"] = dev[ci][f"k{l}"]
        m[f"vown{l}"] = dev[ci][f"v{l}"]
        m[f"kpair{l}"] = np.stack([dev[c0][f"k{l}"], dev[c1][f"k{l}"]])
        m[f"vpair{l}"] = np.stack([dev[c0][f"v{l}"], dev[c1][f"v{l}"]])
        m[f"fpair{l}"] = np.stack([dev[c0][f"f{l}"], dev[c1][f"f{l}"]])
        m[f"bias{l}"] = _bias_tables(W["rpb"][l], h, H)
        dftc, dftl, dftx = _dft_consts(h)
        m["dftc"], m["dftl"], m["dftx"] = dftc, dftl, dftx
        per.append(m)
    return per


def _run_n(stages, cfg, in_maps, n):
    nc, b = _get_prog(stages, cfg)
    maps = []
    for m in in_maps:
        mm = {}
        for name, (shape, dt) in b.dram_in.items():
            a = np.ascontiguousarray(m[name])
            want = NPBF if dt == BF16 else np.float32
            assert a.shape == shape, (name, a.shape, shape)
            assert a.dtype == want, (name, a.dtype)
            mm[name] = a
        maps.append(mm)
    res = run_bass_kernel_spmd(nc, maps, core_ids=list(range(n)))
    return res.results


def kernel(**inputs):
    cfg = dict(CFG)
    W = {k: np.asarray(v, np.float32) for k, v in inputs.items()}
    D, H, DEPTH = cfg["D"], cfg["H"], cfg["DEPTH"]
    KC = D // 128
    sh = {}
    for l in range(DEPTH):
        a = _layer_inputs_A(l, W, cfg, [(0, 0)])[0]
        ct = a.pop(f"cT{l}")
        sh.update(a)
        sh[f"wfour{l}"] = np.ascontiguousarray(W["w_four"][l])
        sh[f"wout{l}"] = _wtile(W["w_out"][l], 128, KC)
        sh.update(_ffn_w(l, 2, W, cfg))
        for h in range(2):
            sh[f"bias{l}_{h}"] = _bias_tables(W["rpb"][l], h, H)
    for h in range(2):
        dftc, dftl, dftx = _dft_consts(h)
        sh["dftc"], sh[f"dftl_{h}"], sh[f"dftx_{h}"] = dftc, dftl, dftx
    per = []
    for b in range(4):
        m = dict(sh)
        cc = np.stack([W["c"][b], W["c_ctx"]], 1)
        for l in range(DEPTH):
            m[f"cT{l}"] = np.ascontiguousarray(cc.reshape(KC, 128, 2).transpose(1, 0, 2))
        halves = []
        for h in range(2):
            tok = np.concatenate([W["x"][b, h * NLAT:(h + 1) * NLAT], W["ctx"][b, h * NCTX:(h + 1) * NCTX]], 0)
            halves.append(_fm(tok, KC))
        m["xT_in"] = np.stack(halves)
        per.append(m)
    real = [0, 1, 2, 3]
    r = _run_n((("FUSED", 0),), cfg, per, 4)
    out = np.zeros((4, SEQ, D), np.float32)
    for b in range(4):
        y = r[real[b]]["y_out"]
        for h in range(2):
            out[b, h * NLAT:(h + 1) * NLAT] = y[h].transpose(2, 1, 0).reshape(NLAT, D)
    return out


def kernel_unfused(**inputs):
    cfg = dict(CFG)
    W = {k: np.asarray(v, np.float32) for k, v in inputs.items()}
    D = cfg["D"]
    KC = D // 128
    cores = [(b, h) for b in range(4) for h in range(2)]
    per = _layer_inputs_A(0, W, cfg, cores)
    for ci, (b, h) in enumerate(cores):
        tok = np.concatenate([W["x"][b, h * NLAT:(h + 1) * NLAT], W["ctx"][b, h * NCTX:(h + 1) * NCTX]], 0)
        per[ci]["xT_in"] = _fm(tok, KC)
    r0 = _run((("A", 0),), cfg, per)
    perB = _layer_inputs_B(0, W, cfg, cores, r0)
    perA = _layer_inputs_A(1, W, cfg, cores)
    nw0 = _norm_w(0, W, cfg)
    for ci in range(8):
        perB[ci].update(perA[ci])
        perB[ci].update(nw0)
        perB[ci]["xT_in"] = r0[ci]["xT_out"]
        perB[ci]["modT_in"] = r0[ci]["modT_out"]
    r1 = _run((("B", 0), ("A", 1)), cfg, perB)
    del perB, perA
    perB = _layer_inputs_B(1, W, cfg, cores, r1)
    nw1 = _norm_w(1, W, cfg)
    for ci in range(8):
        perB[ci].update(nw1)
        perB[ci]["xT_in"] = r1[ci]["xT_out"]
        perB[ci]["modT_in"] = r1[ci]["modT_out"]
    r2 = _run((("B", 1),), cfg, perB)
    out = np.zeros((4, SEQ, D), np.float32)
    for ci, (b, h) in enumerate(cores):
        y = r2[ci]["y_out"]
        out[b, h * NLAT:(h + 1) * NLAT] = y.transpose(2, 1, 0).reshape(NLAT, D)
    return out
```

```python
import numpy as np
import ml_dtypes
import concourse.bass as bass
import concourse.mybir as mybir
from concourse.bass_utils import run_bass_kernel_spmd
from contextlib import ExitStack

F32, BF16 = mybir.dt.float32, mybir.dt.bfloat16
AF = mybir.ActivationFunctionType
ALU = mybir.AluOpType
NPBF = ml_dtypes.bfloat16

CFG = dict(D=2048, DFF=5632, H=12, G=4, NQ=4, DEPTH=2)
SEQ, CTX, GRID_W, ROWS = 2048, 256, 64, 32
WIN_ROWS, WIN_COLS = 8, 16
NLAT, NCTX = 1024, 128
NT = NLAT + NCTX
TT = [(0, 384), (384, 768), (768, 1152)]
SEGS = [[(0, 384, 0)], [(384, 768, 0)], [(768, 1024, 0), (1024, 1152, 1)]]
EPS = 1e-6
NEG = -30000.0
SLOT = 2048
NSLOT = 4


class Op:
    __slots__ = ("eng", "fn", "deps", "idx", "needs_inc", "chan", "cnt", "is_dma")


class Prog:
    def __init__(self):
        self.ops = []
        self.lastw = {}
        self.readers = {}
        self.chan_cnt = {}

    def add(self, eng, fn, reads=(), writes=(), stream=None):
        op = Op()
        op.eng, op.fn, op.idx = eng, fn, len(self.ops)
        op.needs_inc = False
        op.is_dma = stream is not None
        op.chan = ("dma", stream) if stream is not None else ("eng", eng)
        deps = set()
        for k in reads:
            w = self.lastw.get(k)
            if w is not None:
                deps.add(w)
        for k in writes:
            w = self.lastw.get(k)
            if w is not None:
                deps.add(w)
            for r in self.readers.get(k, {}).values():
                deps.add(r)
        for k in reads:
            self.readers.setdefault(k, {})[op.chan] = op.idx
        for k in writes:
            self.lastw[k] = op.idx
            self.readers[k] = {}
        deps.discard(op.idx)
        op.deps = deps
        self.ops.append(op)
        return op.idx

    def emit(self, nc, es):
        ops = self.ops
        for b in ops:
            keep = set()
            for ai in b.deps:
                a = ops[ai]
                if (not a.is_dma) and (not b.is_dma) and a.eng == "pe" and b.eng == "pe":
                    continue
                keep.add(ai)
                a.needs_inc = True
            b.deps = keep
        for o in ops:
            if o.needs_inc:
                self.chan_cnt[o.chan] = self.chan_cnt.get(o.chan, 0) + 1
                o.cnt = self.chan_cnt[o.chan]
            else:
                o.cnt = None
        sems = {}

        def sem_for(chan, cnt):
            unit = 16 if chan[0] == "dma" else 1
            cap = 1800 if chan[0] == "dma" else 30000
            ep = (cnt - 1) // cap
            key = (chan, ep)
            if key not in sems:
                sems[key] = es.enter_context(nc.semaphore(f"s{len(sems)}"))
            return sems[key], ((cnt - 1) % cap + 1) * unit, unit

        block = es.enter_context(nc.Block())
        engmap = {"pe": block.tensor, "act": block.scalar, "dve": block.vector,
                  "pool": block.gpsimd, "sp": block.sync}
        for ename, deco in engmap.items():
            eops = [o for o in ops if o.eng == ename]

            def section(e, eops=eops):
                waited = {}
                for o in eops:
                    need = {}
                    for ai in o.deps:
                        a = ops[ai]
                        s_, v, _ = sem_for(a.chan, a.cnt)
                        if need.get(id(s_), (None, -1))[1] < v:
                            need[id(s_)] = (s_, v)
                    for sid, (s_, v) in need.items():
                        if waited.get(sid, -1) >= v:
                            continue
                        e.wait_ge(s_, v)
                        waited[sid] = v
                    if o.fn is None:
                        continue
                    ins = o.fn(e)
                    if o.needs_inc:
                        s_, v, unit = sem_for(o.chan, o.cnt)
                        ins.then_inc(s_, unit)

            deco(section)


class Builder:
    def __init__(self, stages, cfg):
        self.cfg = cfg
        self.stages = stages
        self.D, self.DFF, self.H, self.G, self.NQ = cfg["D"], cfg["DFF"], cfg["H"], cfg["G"], cfg["NQ"]
        self.KC = self.D // 128
        self.FC = self.DFF // 128
        self.FQ = self.FC // self.NQ
        self.NM = 9 * self.KC
        self.nc = bass.Bass("TRN2", target_bir_lowering=False)
        self.p = Prog()
        self.dram_in = {}
        self.dram_out = {}
        self.ring_i = 0
        self.uid = 0
        self._aps = {}
        self._sbs = {}
        self.fused = (len(stages) > 0 and stages[0][0] == "FUSED")

    def din(self, name, shape, dt=F32):
        if name in self._aps:
            return self._aps[name]
        t = self.nc.dram_tensor(name, list(shape), dt, kind="ExternalInput").ap()
        self.dram_in[name] = (tuple(shape), dt)
        self._aps[name] = t
        return t

    def dout(self, name, shape, dt=F32):
        if name in self._aps:
            return self._aps[name]
        t = self.nc.dram_tensor(name, list(shape), dt, kind="ExternalOutput").ap()
        self.dram_out[name] = (tuple(shape), dt)
        self._aps[name] = t
        return t

    def dscr(self, name, shape, dt=F32):
        if name in self._aps:
            return self._aps[name]
        t = self.nc.dram_tensor(name, list(shape), dt, kind="Internal").ap()
        self._aps[name] = t
        return t

    def sb(self, name, shape, dt):
        if name in self._sbs:
            return self._sbs[name]
        t = self.es.enter_context(self.nc.sbuf_tensor(name, list(shape), dt))
        self._sbs[name] = t
        return t

    def dma(self, q, out, in_, reads, writes, stream, **kw):
        return self.p.add(q, lambda e: e.dma_start(out=out, in_=in_, **kw), reads, list(writes) + [("strm", stream)], stream=stream)

    def wload(self, src_ap, ncols, tag):
        s = self.ring_i % NSLOT
        self.ring_i += 1
        dst = self.wring[:, s, 0:ncols]
        self.dma("pool", dst, src_ap, [], [("ring", s)], f"ring{s}", max_dma_last_dim=8192)
        return s

    def mm(self, out, lhsT, rhs, start, stop, reads, writes):
        return self.p.add("pe", lambda e: e.matmul(out, lhsT=lhsT, rhs=rhs, start=start, stop=stop), reads, writes)

    def act(self, out, in_, func, reads, writes, **kw):
        return self.p.add("act", lambda e: e.activation(out=out, in_=in_, func=func, **kw), reads, writes)

    def dve(self, fn, reads, writes):
        return self.p.add("dve", fn, reads, writes)

    def build(self):
        nc, cfg = self.nc, self.cfg
        KC, H, G, FC, NM = self.KC, self.H, self.G, self.FC, self.NM
        with ExitStack() as es:
            self.es = es
            self.xT = self.sb("xT", [128, KC, NT], F32)
            self.xm = self.sb("xm", [128, KC, NT], BF16)
            self.ab = self.sb("ab", [128, self.FQ, NT], BF16)
            self.wring = self.sb("wring", [128, NSLOT, SLOT], BF16)
            self.ones = self.sb("ones", [128, 128], BF16)
            self.modT_l = [self.sb(f"modTsb{i}", [128, NM, 2], F32) for i in range(2)]
            self.gs_l = [self.sb(f"gssb{i}", [128, 3, KC, 2], F32) for i in range(2)]
            self.gate_l = [self.sb(f"gatesb{i}", [128, 3, KC, 2], F32) for i in range(2)]
            self.nw_l = [self.sb(f"nwsb{i}", [128, 3, KC, 2], F32) for i in range(2)]
            self.set_par(0)
            self.rt = self.sb("rt", [128, 1, 384], F32)
            self.rstd = self.sb("rstd", [128, 1, 384], F32)
            self.sg = self.sb("sg", [128, 3, 384], F32)
            self.ps = [es.enter_context(nc.psum_tensor(f"ps{i}", [128, 512], F32)) for i in range(8)]
            self.p.add("dve", lambda e: e.memset(self.ones[:], 1.0), [], ["ones"])
            if self.fused:
                self.build_fused()
                self.p.emit(nc, es)
                return nc
            first = True
            for (kind, l) in self.stages:
                last_layer = (l == cfg["DEPTH"] - 1)
                if kind == "A":
                    self.stage_A(l, load_x=first)
                else:
                    self.stage_B(l, load_x=first, final=last_layer)
                first = False
            kind, l = self.stages[-1]
            if kind == "A":
                xo = self.dout("xT_out", [128, KC, NT])
                mo = self.dout("modT_out", [128, NM, 2])
                i1 = self.dma("sp", xo[:, :, :], self.xT[:], [("x", c, t) for c in range(KC) for t in range(3)], ["xo"], "fin0")
                i2 = self.dma("sp", mo[:, :, :], self.modT[:], [("modT", self.par)], ["mo"], "fin1")
                fin = [i1, i2] + self.out_dmas
            else:
                yo = self.dout("y_out", [128, KC, NLAT])
                i1 = self.dma("sp", yo[:, :, :], self.xT[:, :, 0:NLAT], [("x", c, t) for c in range(KC) for t in range(3)], ["yo"], "fin0")
                fin = [i1]
            self.p.add("sp", None, ["xo", "mo", "yo", "qkvf_out"], [])
            self.p.emit(nc, es)
        return nc

    def set_par(self, par):
        self.par = par
        self.modT, self.gs, self.gate, self.nw = self.modT_l[par], self.gs_l[par], self.gate_l[par], self.nw_l[par]

    def build_fused(self):
        KC, DEPTH = self.KC, self.cfg["DEPTH"]
        xin = self.din("xT_in", [2, 128, KC, NT])
        yo = self.dout("y_out", [2, 128, KC, NLAT])
        xs = self.dscr("xs", [2, 128, KC, NT])
        self.epsb = self.sb("epsb", [128, 1], F32)
        self.p.add("dve", lambda e: e.memset(self.epsb[:], EPS), [], ["epsb"])
        XK = [("x", c, t) for c in range(KC) for t in range(3)]

        def xload(src, key):
            for c in range(KC):
                self.dma("sp", self.xT[:, c, :], src[:, c, :], [key], [("x", c, t) for t in range(3)], f"xin{c % 4}")

        def xstore(dst, key):
            self.dma("sp", dst, self.xT[:], XK, [key], "xst")

        def a_body(l, h, after_ffn=None, after_norm1=None):
            self.stage_norm(0)
            self.stage_ffn(l, 1, 0)
            if after_ffn:
                after_ffn()
            self.stage_norm(1)
            if after_norm1:
                after_norm1()
            self.stage_inproj(l, h)

        KC3 = 3 * KC
        self.mod_prep()
        for l in range(DEPTH):
            last = l == DEPTH - 1
            self.set_par(l % 2)
            if l == 0:
                xload(xin[0], "xin0")
                for it in self.mod_items(0, 0, KC3):
                    it()
                self.mod_finish(0, 0, KC3)
                self.derive_mod(0, subs=(0,))
                rest = self.mod_items(0, KC3, self.NM)
                self.stage_norm(0)
                self.stage_ffn(0, 1, 0, extra=rest)
                xstore(xs[0], ("xs", 0))
                self.mod_finish(0, KC3, self.NM)
                self.derive_mod(0, subs=(1, 2), load_nw=False)
                self.stage_norm(1)
                xload(xin[1], "xin1")
                self.stage_inproj(0, 0)
            else:
                a_body(l, 0, after_ffn=lambda: xstore(xs[0], ("xs", 0)), after_norm1=lambda: xload(xs[1], ("xs", 1)))
            a_body(l, 1)
            self.stage_B(l, load_x=False, final=last, h=1)
            if last:
                self.dma("sp", yo[1], self.xT[:, :, 0:NLAT], XK, ["yo"], "fin0")
            else:
                xstore(xs[1], ("xs", 1))
            xload(xs[0], ("xs", 0))
            nxt = None if last else self.mod_items(l + 1, 0, self.NM)
            self.stage_B(l, load_x=False, final=last, h=0, ffn_extra=nxt)
            if last:
                self.dma("sp", yo[0], self.xT[:, :, 0:NLAT], XK, ["yo"], "fin1")
            else:
                self.set_par((l + 1) % 2)
                self.mod_finish(l + 1, 0, self.NM)
                self.derive_mod(l + 1)
        self.p.add("sp", None, ["yo"], [])

    def stage_A_body(self, l, h):
        self.stage_norm(0)
        self.stage_ffn(l, 1, 0)
        self.stage_norm(1)
        self.stage_inproj(l, h)

    def load_x(self):
        KC = self.KC
        xin = self.din("xT_in", [128, KC, NT])
        for c in range(KC):
            self.dma("sp", self.xT[:, c, :], xin[:, c, :], [], [("x", c, t) for t in range(3)], f"xin{c % 4}")

    def derive_mod(self, l, subs=(0, 1, 2), load_nw=True):
        KC = self.KC
        par = self.par
        modT, gs, gate, nw = self.modT, self.gs, self.gate, self.nw
        if load_nw:
            nwd = self.din(f"nw{l}", [128, 3, KC, 2])
            self.dma("sp", nw[:], nwd[:, :, :, :], [], [("nw", par)], "small")
        for sub in subs:
            sc = modT[:, (3 * sub + 1) * KC:(3 * sub + 2) * KC, :]
            gt = modT[:, (3 * sub + 2) * KC:(3 * sub + 3) * KC, :]
            self.dve(lambda e, sc=sc, sub=sub, gs=gs, nw=nw: e.scalar_tensor_tensor(
                out=gs[:, sub], in0=sc, scalar=1.0, in1=nw[:, sub], op0=ALU.add, op1=ALU.mult),
                [("modT", par), ("nw", par)], [("gs", par, sub)])
            f = 1.0 if sub == 1 else 0.5
            self.dve(lambda e, gt=gt, sub=sub, f=f, gate=gate: e.tensor_scalar(
                out=gate[:, sub], in0=gt, scalar1=f, scalar2=None, op0=ALU.mult),
                [("modT", par)], [("gate", par, sub)])

    def mod_prep(self):
        KC, NM = self.KC, self.NM
        cT = self.din("cT0", [128, KC, 2])
        cs = self.sb("cs", [128, KC, 2], F32)
        self.scT = self.sb("scT", [128, KC, 4], BF16)
        s32 = self.sb("s32", [128, KC, 2], F32)
        scT = self.scT
        self.dma("sp", cs[:], cT[:, :, :], [], ["cs"], "small")
        self.act(s32[:], cs[:], AF.Silu, ["cs"], ["s32"])
        self.act(scT[:, :, 0:2], s32[:], AF.Copy, ["s32"], ["scT"])
        self.dve(lambda e: e.tensor_tensor(out=scT[:, :, 2:4], in0=s32[:], in1=scT[:, :, 0:2], op=ALU.subtract),
                 ["s32", "scT"], ["scT"])

    def mod_items(self, l, j0, j1):
        KC, NM = self.KC, self.NM
        NH = NM // 2
        wm = self.din(f"wmod{l}", [NM, 128, KC * 128])
        scT = self.scT

        def item(j):
            s_ = self.wload(wm[j], KC * 128, "wmod")
            pbi = 7 if j < NH else 6
            jj = j if j < NH else j - NH
            for kc in range(KC):
                self.mm(self.ps[pbi][:, 4 * jj:4 * jj + 4], self.wring[:, s_, kc * 128:(kc + 1) * 128],
                        scT[:, kc, :], kc == 0, kc == KC - 1, [("ring", s_), "scT"], [("ps", pbi)])
        return [(lambda j=j: item(j)) for j in range(j0, j1)]

    def mod_finish(self, l, j0, j1):
        KC, NM = self.KC, self.NM
        NH = NM // 2
        par = self.par
        modT = self.modT
        bm = self.din(f"bmod{l}", [128, NM, 2])
        bms = self.sb("bms", [128, NM, 2], F32)
        self.dma("sp", bms[:], bm[:, :, :], [], ["bms"], "small2")
        for (a, b, pbi, off) in ((j0, min(j1, NH), 7, 0), (max(j0, NH), j1, 6, NH)):
            if b <= a:
                continue
            pv = self.ps[pbi][:, 4 * (a - off):4 * (b - off)].rearrange("p (a b) -> p a b", b=4)
            mo = modT[:, a:b, :]
            self.dve(lambda e, pv=pv, mo=mo, a=a, b=b: e.tensor_tensor(out=mo, in0=pv[:, :, 0:2], in1=bms[:, a:b, :], op=ALU.add),
                     [("ps", pbi), "bms"], [("modT", par)])
            self.dve(lambda e, pv=pv, mo=mo: e.tensor_tensor(out=mo, in0=pv[:, :, 2:4], in1=mo, op=ALU.add),
                     [("ps", pbi), ("modT", par)], [("modT", par)])

    def stage_mod(self, l):
        if not hasattr(self, "scT"):
            self.mod_prep()
        for it in self.mod_items(l, 0, self.NM):
            it()
        self.mod_finish(l, 0, self.NM)

    def stage_norm(self, sub):
        KC = self.KC
        D = self.D
        for t, (t0, t1) in enumerate(TT):
            n = t1 - t0
            xk = [("x", c, t) for c in range(KC)]
            mk = [("xm", c, t) for c in range(KC)]
            self.act(self.xm[:, :, t0:t1], self.xT[:, :, t0:t1], AF.Square, xk, mk)
            pst = self.ps[6]
            for c in range(KC):
                self.mm(pst[:, 0:n], self.ones[:], self.xm[:, c, t0:t1], c == 0, c == KC - 1,
                        ["ones", ("xm", c, t)], [("ps", 6)])
            self.act(self.rt[:, 0, 0:n], pst[:, 0:n], AF.Ln, [("ps", 6)], [("rt", 0)], scale=1.0 / D, bias=self.epsb[:])
            self.act(self.rstd[:, 0, 0:n], self.rt[:, 0, 0:n], AF.Exp, [("rt", 0)], [("rstd", 0)], scale=-0.5)
            for (s0, s1, r) in SEGS[t]:
                for c in range(KC):
                    b = c % 2
                    self.dve(lambda e, c=c, b=b, s0=s0, s1=s1, t0=t0: e.tensor_tensor(
                        out=self.sg[:, b, 0:s1 - s0], in0=self.xT[:, c, s0:s1], in1=self.rstd[:, 0, s0 - t0:s1 - t0], op=ALU.mult),
                        [("x", c, t), ("rstd", 0)], [("sg", b)])
                    sh = self.modT[:, 3 * sub * KC + c, r:r + 1]
                    self.act(self.xm[:, c, s0:s1], self.sg[:, b, 0:s1 - s0], AF.Identity,
                             [("sg", b), ("modT", self.par), ("gs", self.par, sub)], [("xm", c, t)],
                             scale=self.gs[:, sub, c, r:r + 1], bias=sh)

    def resid(self, bi, i, t, sub):
        psb = self.ps[bi]
        t0 = TT[t][0]
        gate, par = self.gate, self.par
        for (s0, s1, r) in SEGS[t]:
            self.dve(lambda e, s0=s0, s1=s1, r=r, gate=gate: e.scalar_tensor_tensor(
                out=self.xT[:, i, s0:s1], in0=psb[:, s0 - t0:s1 - t0], scalar=gate[:, sub, i, r:r + 1],
                in1=self.xT[:, i, s0:s1], op0=ALU.mult, op1=ALU.add),
                [("ps", bi), ("gate", par, sub), ("x", i, t)], [("x", i, t)])

    def stage_ffn(self, l, which, sub, extra=None):
        extra = list(extra) if extra else []
        nslots = 2 * self.FC
        per_slot = -(-len(extra) // nslots) if extra else 0
        KC, FQ, NQ = self.KC, self.FQ, self.NQ
        wi = self.din(f"wi{which}_{l}", [self.FC * 2, 128, KC * 128])
        wo = self.din(f"wo{which}_{l}", [NQ * KC, 128, FQ * 128])
        for qd in range(NQ):
            for jj in range(FQ):
                j = qd * FQ + jj
                for half in range(2):
                    s = self.wload(wi[2 * j + half], KC * 128, "wi")
                    for kc in range(KC):
                        for t, (t0, t1) in enumerate(TT):
                            b = half * 3 + t
                            self.mm(self.ps[b][:, 0:t1 - t0], self.wring[:, s, kc * 128:(kc + 1) * 128],
                                    self.xm[:, kc, t0:t1], kc == 0, kc == KC - 1, [("ring", s), ("xm", kc, t)], [("ps", b)])
                    for _ in range(per_slot):
                        if extra:
                            extra.pop(0)()
                for t, (t0, t1) in enumerate(TT):
                    n = t1 - t0
                    self.act(self.sg[:, t, 0:n], self.ps[t][:, 0:n], AF.Silu, [("ps", t)], [("sg", t)])
                    self.dve(lambda e, t=t, n=n, jj=jj, t0=t0, t1=t1: e.tensor_tensor(
                        out=self.ab[:, jj, t0:t1], in0=self.ps[3 + t][:, 0:n], in1=self.sg[:, t, 0:n], op=ALU.mult),
                        [("ps", 3 + t), ("sg", t)], [("ab", jj, t)])
            for i in range(KC):
                s = self.wload(wo[qd * KC + i], FQ * 128, "wo")
                bb = 3 * (i % 2)
                for kc in range(FQ):
                    for t, (t0, t1) in enumerate(TT):
                        self.mm(self.ps[bb + t][:, 0:t1 - t0], self.wring[:, s, kc * 128:(kc + 1) * 128],
                                self.ab[:, kc, t0:t1], kc == 0, kc == FQ - 1, [("ring", s), ("ab", kc, t)], [("ps", bb + t)])
                for t in range(3):
                    self.resid(bb + t, i, t, sub)
        while extra:
            extra.pop(0)()

    def stage_inproj(self, l, h=None):
        KC, H, G = self.KC, self.H, self.G
        nqkf = 2 * H + G
        w1 = self.din(f"winqkf{l}", [nqkf, 128, KC * 128])
        nvg = H
        vw = 128
        w2 = self.din(f"winv{l}", [nvg, 128, KC * vw])
        qkg = self.din(f"qkg{l}", [128, 2])
        if self.fused:
            qo = self.dscr(f"q{l}_{h}", [128, H, NT], BF16)
            ko = self.dscr(f"k{l}_{h}", [128, H, NT], BF16)
            fo = self.dscr(f"f{l}_{h}", [128, G, NT], F32)
            vo = self.dscr(f"v{l}_{h}", [NT // 128, 128, H * 128], BF16)
            OK = ("qkvf", l, h)
        else:
            qo = self.dout(f"q{l}", [128, H, NT], BF16)
            ko = self.dout(f"k{l}", [128, H, NT], BF16)
            fo = self.dout(f"f{l}", [128, G, NT], F32)
            vo = self.dout(f"v{l}", [NT // 128, 128, H * 128], BF16)
            OK = "qkvf_out"
        qkgs = self.sb("qkgs", [128, 2], F32)
        self.dma("sp", qkgs[:], qkg[:, :], [], ["qkg"], "small")
        stg = self.sb("stg", [128, 2, NT], BF16)
        sq = self.sb("sqq", [128, 1, 384], BF16)
        vst = self.sb("vst", [128, 2, vw], BF16)
        self.out_dmas = []
        def main(ch, hooks):
            s = self.wload(w1[ch], KC * 128, "winqkf")
            pb0 = 4 * (ch % 2)
            pts = {KC // 4: 0, KC // 2: 1, (3 * KC) // 4: 2} if KC >= 4 else {}
            for kc in range(KC):
                if kc in pts and hooks:
                    hooks[pts[kc]]()
                for t, (t0, t1) in enumerate(TT):
                    self.mm(self.ps[pb0 + t][:, 0:t1 - t0], self.wring[:, s, kc * 128:(kc + 1) * 128], self.xm[:, kc, t0:t1],
                            kc == 0, kc == KC - 1, [("ring", s), ("xm", kc, t)], [("ps", pb0 + t)])
            if hooks and not pts:
                for hk in hooks:
                    hk()

        def post_hooks(ch):
            sb_ = ch % 2
            pb0 = 4 * sb_
            if ch < 2 * H:
                which = 0 if ch < H else 1

                def sqr(t):
                    t0, t1 = TT[t]
                    n = t1 - t0
                    self.act(sq[:, 0, 0:n], self.ps[pb0 + t][:, 0:n], AF.Square, [("ps", pb0 + t)], [("sqq", 0)])

                def tile(t):
                    t0, t1 = TT[t]
                    n = t1 - t0
                    self.mm(self.ps[pb0 + 3][:, 0:n], self.ones[:], sq[:, 0, 0:n], True, True, ["ones", ("sqq", 0)], [("ps", pb0 + 3)])
                    self.act(self.rt[:, 0, 0:n], self.ps[pb0 + 3][:, 0:n], AF.Ln, [("ps", pb0 + 3)], [("rt", 0)], scale=1.0 / 128, bias=self.epsb[:])
                    self.act(self.rstd[:, 0, 0:n], self.rt[:, 0, 0:n], AF.Exp, [("rt", 0)], [("rstd", 0)], scale=-0.5)
                    self.dve(lambda e, t=t, n=n, t0=t0, t1=t1: e.scalar_tensor_tensor(
                        out=stg[:, sb_, t0:t1], in0=self.ps[pb0 + t][:, 0:n], scalar=qkgs[:, which:which + 1],
                        in1=self.rstd[:, 0, 0:n], op0=ALU.mult, op1=ALU.mult),
                        [("ps", pb0 + t), ("rstd", 0), "qkg"], [("stg", sb_)])
                    if t < 2:
                        sqr(t + 1)
                    else:
                        dst = qo[:, ch, :] if ch < H else ko[:, ch - H, :]
                        self.dma("sp", dst, stg[:, sb_, :], [("stg", sb_)], [OK], f"qkf{sb_}")
                sqr(0)
                return [lambda t=t: tile(t) for t in range(3)]
            else:
                def ftile(t):
                    t0, t1 = TT[t]
                    n = t1 - t0
                    self.act(self.sg[:, t, 0:n], self.ps[pb0 + t][:, 0:n], AF.Copy, [("ps", pb0 + t)], [("sg", t)])
                    if t == 2:
                        self.dma("sp", fo[:, ch - 2 * H, :].rearrange("p (t n) -> p t n", t=3), self.sg[:],
                                 [("sg", 0), ("sg", 1), ("sg", 2)], [OK], "qkff")
                return [lambda t=t: ftile(t) for t in range(3)]

        hooks = []
        for ch in range(nqkf):
            main(ch, hooks)
            hooks = post_hooks(ch)
        for hk in hooks:
            hk()
        for cg in range(nvg):
            s = self.wload(w2[cg], KC * vw, "winv")
            for tc in range(NT // 128):
                pb = 4 + (tc % 2)
                for kc in range(KC):
                    self.mm(self.ps[pb][:, 0:vw], self.xm[:, kc, tc * 128:(tc + 1) * 128], self.wring[:, s, kc * vw:(kc + 1) * vw],
                            kc == 0, kc == KC - 1, [("ring", s), ("xm", kc, tc // 3)], [("ps", pb)])
                vb = tc % 2
                self.act(vst[:, vb, :], self.ps[pb][:, 0:vw], AF.Copy, [("ps", pb)], [("vst", vb)])
                self.dma("sp", vo[tc, :, cg * vw:(cg + 1) * vw], vst[:, vb, :], [("vst", vb)], [OK], f"vst{vb}")

    def stage_A(self, l, load_x):
        if not hasattr(self, "epsb"):
            self.epsb = self.sb("epsb", [128, 1], F32)
            self.p.add("dve", lambda e: e.memset(self.epsb[:], EPS), [], ["epsb"])
        if load_x:
            self.load_x()
        self.stage_mod(l)
        self.derive_mod(l)
        self.stage_norm(0)
        self.stage_ffn(l, 1, 0)
        self.stage_norm(1)
        self.stage_inproj(l)

    def stage_B(self, l, load_x, final, h=None, ffn_extra=None):
        KC, H, G = self.KC, self.H, self.G
        nc = self.nc
        if not hasattr(self, "epsb"):
            self.epsb = self.sb("epsb", [128, 1], F32)
            self.p.add("dve", lambda e: e.memset(self.epsb[:], EPS), [], ["epsb"])
        if load_x:
            self.load_x()
            mi = self.din("modT_in", [128, self.NM, 2])
            self.dma("sp", self.modT[:], mi[:, :, :], [], [("modT", self.par)], "small")
            self.derive_mod(l)
        if self.fused:
            ksc = [self.dscr(f"k{l}_{r}", [128, H, NT], BF16) for r in range(2)]
            vsc = [self.dscr(f"v{l}_{r}", [NT // 128, 128, H * 128], BF16) for r in range(2)]
            fsc = [self.dscr(f"f{l}_{r}", [128, G, NT], F32) for r in range(2)]
            qd = self.dscr(f"q{l}_{h}", [128, H, NT], BF16)
            kown, vown = ksc[h], vsc[h]
            kp, vp, fp = ksc, vsc, fsc
            bias = self.din(f"bias{l}_{h}", [H, 5, 128, 6 * 128])
            RK = [("qkvf", l, 0), ("qkvf", l, 1)]
        else:
            qd = self.din(f"qown{l}", [128, H, NT], BF16)
            kown = self.din(f"kown{l}", [128, H, NT], BF16)
            vown = self.din(f"vown{l}", [NT // 128, 128, H * 128], BF16)
            kp = self.din(f"kpair{l}", [2, 128, H, NT], BF16)
            vp = self.din(f"vpair{l}", [2, NT // 128, 128, H * 128], BF16)
            fp = self.din(f"fpair{l}", [2, 128, G, NT], F32)
            bias = self.din(f"bias{l}", [H, 5, 128, 6 * 128])
            kp = [kp[0], kp[1]]
            vp = [vp[0], vp[1]]
            fp = [fp[0], fp[1]]
            RK = []
        self.RK = RK
        self.bt = None
        Kt = self.sb("Kt", [128, 2, 18, 128], BF16)
        self.Kt, self.Qt = Kt, None
        Vt = self.sb("Vt", [128, 2, 14, 128], BF16)
        Qt = self.sb("Qt", [128, 2, NT], BF16)
        self.Qt = Qt
        bt = self.sb("bt", [128, 2, 6 * 128], F32)
        self.bt = bt
        tS = self.sb("tS", [128, 6 * 128], F32)
        Pm = self.sb("Pm", [128, 2, 8, 128], BF16)
        rD = self.sb("rD", [128, 128], F32)
        rL = self.sb("rL", [128, 128], F32)
        scale = 128.0 ** -0.5
        units = [(hh, jl) for hh in range(H) for jl in range(9)]
        state = {}

        def emit_S(ui):
            hh, jl = units[ui]
            hb = hh % 2
            kk, vk, qk = ("Kt", hb), ("Vt", hb), ("Qt", hb)
            if jl == 0:
                st = f"att{hb}"
                self.dma("sp", Kt[:, hb, 0:2, :], kp[0][:, hh, 768:1024].rearrange("p (c k) -> p c k", k=128), RK, [kk], st + "a")
                self.dma("sp", Kt[:, hb, 2:10, :], kown[:, hh, 0:1024].rearrange("p (c k) -> p c k", k=128), RK, [kk], st + "b")
                self.dma("sp", Kt[:, hb, 10:12, :], kp[1][:, hh, 0:256].rearrange("p (c k) -> p c k", k=128), RK, [kk], st + "c")
                self.dma("sp", Kt[:, hb, 12, :], kp[0][:, hh, 1024:1152], RK, [kk], st + "d")
                self.dma("sp", Kt[:, hb, 13, :], kp[1][:, hh, 1024:1152], RK, [kk], st + "e")
                hs = slice(hh * 128, (hh + 1) * 128)
                self.dma("sp", Vt[:, hb, 0:2, :], vp[0][6:8, :, hs].rearrange("c p d -> p c d"), RK, [vk], st + "f")
                self.dma("sp", Vt[:, hb, 2:10, :], vown[0:8, :, hs].rearrange("c p d -> p c d"), RK, [vk], st + "g")
                self.dma("sp", Vt[:, hb, 10:12, :], vp[1][0:2, :, hs].rearrange("c p d -> p c d"), RK, [vk], st + "h")
                self.dma("sp", Vt[:, hb, 12, :], vp[0][8, :, hs], RK, [vk], st + "i")
                self.dma("sp", Vt[:, hb, 13, :], vp[1][8, :, hs], RK, [vk], st + "j")
                self.dma("sp", Qt[:, hb, :], qd[:, hh, :], RK, [qk], st + "k")
            ib = ui % 2
            q0 = jl * 128
            if jl < 8:
                wb0, nw = _win(jl)
                slots = [wb0 + i for i in range(nw)] + [12, 13]
                bslot = 0 if jl == 0 else 1 if jl == 1 else 3 if jl == 6 else 4 if jl == 7 else 2
                self.dma("sp", bt[:, ib, :], bias[hh, bslot, :, :], [], [("bt", ib)], f"bt{ib}")
            else:
                slots = [12, 13]
            state[ui] = slots
            bA, bB = self.ps[4 * ib], self.ps[4 * ib + 1]
            iA, iB = 4 * ib, 4 * ib + 1
            for i, sl in enumerate(slots):
                pb, pi, off = (bA, iA, i) if i < 4 else (bB, iB, i - 4)
                self.mm(pb[:, off * 128:(off + 1) * 128], Kt[:, hb, sl, :], Qt[:, hb, q0:q0 + 128], True, True,
                        [kk, qk], [("ps", pi)])
            pk = ("Pm", ib)
            if jl < 8:
                self.dve(lambda e, ib=ib, bA=bA: e.scalar_tensor_tensor(
                    out=tS[:, 0:512], in0=bA[:, 0:512], scalar=scale, in1=bt[:, ib, 0:512], op0=ALU.mult, op1=ALU.add),
                    [("ps", iA), ("bt", ib)], ["tS0"])
                nB = nw - 4
                self.dve(lambda e, ib=ib, bB=bB, nB=nB: e.scalar_tensor_tensor(
                    out=tS[:, 512:512 + nB * 128], in0=bB[:, 0:nB * 128], scalar=scale, in1=bt[:, ib, 512:512 + nB * 128],
                    op0=ALU.mult, op1=ALU.add),
                    [("ps", iB), ("bt", ib)], ["tS1"])
                self.act(Pm[:, ib, 0:nw, :].rearrange("p a b -> p (a b)"), tS[:, 0:nw * 128], AF.Exp, ["tS0", "tS1"], [pk])
                self.act(Pm[:, ib, nw:nw + 2, :].rearrange("p a b -> p (a b)"), bB[:, nB * 128:(nB + 2) * 128], AF.Exp,
                         [("ps", iB)], [pk], scale=scale)
            else:
                self.act(Pm[:, ib, 0:2, :].rearrange("p a b -> p (a b)"), bA[:, 0:256], AF.Exp, [("ps", iA)], [pk], scale=scale)

        def emit_PV(ui):
            hh, jl = units[ui]
            hb = hh % 2
            vk = ("Vt", hb)
            ib = ui % 2
            q0 = jl * 128
            slots = state.pop(ui)
            ns = len(slots)
            bO, bD = self.ps[4 * ib + 2], self.ps[4 * ib + 3]
            iO, iD = 4 * ib + 2, 4 * ib + 3
            pk = ("Pm", ib)
            for i, sl in enumerate(slots):
                self.mm(bO[:, 0:128], Vt[:, hb, sl, :], Pm[:, ib, i, :], i == 0, i == ns - 1, [vk, pk], [("ps", iO)])
            for i, sl in enumerate(slots):
                self.mm(bD[:, 0:128], self.ones[:], Pm[:, ib, i, :], i == 0, i == ns - 1, ["ones", pk], [("ps", iD)])
            self.act(rL[:], bD[:, 0:128], AF.Ln, [("ps", iD)], ["rL"])
            self.act(rD[:], rL[:], AF.Exp, ["rL"], ["rD"], scale=-1.0)
            self.dve(lambda e, bO=bO, hh=hh, q0=q0: e.tensor_tensor(
                out=self.xm[:, hh, q0:q0 + 128], in0=bO[:, 0:128], in1=rD[:], op=ALU.mult),
                [("ps", iO), "rD"], [("xm", hh, q0 // 384)])

        for ui in range(len(units)):
            emit_S(ui)
            if ui > 0:
                emit_PV(ui - 1)
        emit_PV(len(units) - 1)
        self.stage_fourier(l, fp, h)
        wout = self.din(f"wout{l}", [KC, 128, KC * 128])
        for i in range(KC):
            s = self.wload(wout[i], KC * 128, "wout")
            bb = 3 * (i % 2)
            for kc in range(KC):
                for t, (t0, t1) in enumerate(TT):
                    self.mm(self.ps[bb + t][:, 0:t1 - t0], self.wring[:, s, kc * 128:(kc + 1) * 128], self.xm[:, kc, t0:t1],
                            kc == 0, kc == KC - 1, [("ring", s), ("xm", kc, t)], [("ps", bb + t)])
            for t in range(3):
                self.resid(bb + t, i, t, 1)
        self.stage_norm(2)
        self.stage_ffn(l, 2, 2, extra=ffn_extra)

    def stage_fourier(self, l, fp, h=None):
        KC, H, G = self.KC, self.H, self.G
        wf = self.din(f"wfour{l}", [G, 128, 128])
        ccs = self.din("dftc", [128, 2, 128])
        sfx = f"_{h}" if self.fused else ""
        dl = self.din("dftl" + sfx, [2, 4, 128, 4 * NLAT], BF16)
        dcx = self.din("dftx" + sfx, [2, 128, 2 * NCTX], BF16)
        RK = self.RK
        cc = self.sb("cc", [128, 2, 128], F32)
        wfs = self.sb("wfs", [128, G, 128], F32)
        AB = self.sb("AB", [128, 2, 128], F32)
        ff = self.bt[:].rearrange("p a n -> p (a n)")
        Y = self.Kt
        ALK = [("Kt", 0), ("Kt", 1)]
        ALQ = [("bt", 0), ("bt", 1)]
        dx = self.sb("dx", [128, 2, 2 * NCTX], BF16)
        self.dma("sp", cc[:], ccs[:, :, :], [], ["cc"], "small")
        self.dma("sp", wfs[:], wf.rearrange("g c d -> c g d"), [], ["wfs"], "small2")
        self.dma("sp", dx[:, 0, :], dcx[0], [], ["dx"], "dx0")
        self.dma("sp", dx[:, 1, :], dcx[1], [], ["dx"], "dx1")
        for g in range(G):
            for ab in range(2):
                self.mm(self.ps[6][:, ab * 128:(ab + 1) * 128], cc[:, ab, :], wfs[:, g, :], True, True, ["cc", "wfs"], [("ps", 6)])
            self.act(AB[:].rearrange("p a b -> p (a b)"), self.ps[6][:, 0:256], AF.Copy, [("ps", 6)], ["AB"])
            for r in range(2):
                self.dma("sp", ff[:, 0:NT], fp[r][:, g, :], RK, ["ff"] + ALQ, "ffa")
                for lc in range(9):
                    ci = r * 8 + lc if lc < 8 else 16 + r
                    c0 = lc * 128
                    pb = 4 + lc % 2
                    for ab in range(2):
                        self.mm(self.ps[pb][:, ab * 128:(ab + 1) * 128], ff[:, c0:c0 + 128], AB[:, ab, :], True, True, ["ff", "AB"], [("ps", pb)])
                    self.act(Y[:, :, ci, :], self.ps[pb][:, 0:256].rearrange("p (a b) -> p a b", a=2), AF.Copy, [("ps", pb)], [("Y", ci)] + ALK)
            LT = [(0, 384), (384, 768), (768, 1024)]
            for t, (t0, t1) in enumerate(LT):
                n = t1 - t0
                nmm = 0
                for cs in range(2):
                    for qtr in range(4):
                        s = self.ring_i % NSLOT
                        self.ring_i += 1
                        dst = self.wring[:, s, 0:4 * n].rearrange("p (k n) -> p k n", k=4)
                        src = dl[cs, qtr, :, :].rearrange("p (k n) -> p k n", k=4)[:, :, t0:t1]
                        self.dma("sp", dst, src, [], [("ring", s)], f"ring{s}")
                        for k4 in range(4):
                            kc = qtr * 4 + k4
                            self.mm(self.ps[t][:, 0:n], Y[:, cs, kc, :], self.wring[:, s, k4 * n:(k4 + 1) * n],
                                    nmm == 0, nmm == 31, [("Y", kc), ("ring", s)], [("ps", t)])
                            nmm += 1
                self.act(self.xm[:, H + g, t0:t1], self.ps[t][:, 0:n], AF.Copy, [("ps", t)], [("xm", H + g, t)])
            nmm = 0
            for cs in range(2):
                for k2 in range(2):
                    self.mm(self.ps[3][:, 0:128], Y[:, cs, 16 + k2, :], dx[:, cs, k2 * 128:(k2 + 1) * 128],
                            nmm == 0, nmm == 3, [("Y", 16 + k2), "dx"], [("ps", 3)])
                    nmm += 1
            self.act(self.xm[:, H + g, NLAT:NT], self.ps[3][:, 0:128], AF.Copy, [("ps", 3)], [("xm", H + g, 2)])


def _fm(a, kc):
    T = a.shape[0]
    return np.ascontiguousarray(a.reshape(T, kc, 128).transpose(2, 1, 0))


def _vec_fm(v, kc):
    return np.ascontiguousarray(v.reshape(kc, 128).T)


def _wtile(w, cols, kc):
    K, N = w.shape
    g = N // cols
    return np.ascontiguousarray(w.reshape(kc, 128, g, cols).transpose(2, 1, 0, 3)).reshape(g, 128, kc * cols)


def _win(jl):
    if jl == 0:
        return 0, 6
    if jl == 7:
        return 6, 6
    return jl, 5


def _bias_tables(rpb_l, h, H):
    out = np.full((H, 5, 128, 6, 128), NEG, np.float32)
    jl_of_slot = [0, 1, 2, 6, 7]
    kk = np.arange(128)
    qq = np.arange(128)
    for sl, jl in enumerate(jl_of_slot):
        J = 8 * h + jl
        r = 2 * J + qq // 64
        cq = qq % 64
        rs = np.clip(r - WIN_ROWS // 2, 0, ROWS - WIN_ROWS)
        cstart = np.clip(cq - WIN_COLS // 2, 0, GRID_W - WIN_COLS)
        wb0, nw = _win(jl)
        for i in range(nw):
            gc = wb0 + i + 8 * h - 2
            if gc < 0 or gc > 15:
                continue
            kr = 2 * gc + kk // 64
            ck = kk % 64
            valid = (kr[:, None] >= rs[None, :]) & (kr[:, None] < rs[None, :] + WIN_ROWS) & \
                    (ck[:, None] >= cstart[None, :]) & (ck[:, None] < cstart[None, :] + WIN_COLS)
            dr = np.clip(kr[:, None] - r[None, :] + WIN_ROWS - 1, 0, 2 * WIN_ROWS - 2)
            dc = np.clip(ck[:, None] - cq[None, :], -(WIN_COLS - 1), WIN_COLS - 1) + WIN_COLS - 1
            vals = rpb_l[:, dr, dc]
            out[:, sl, :, i, :] = np.where(valid[None], vals, NEG)
    return out.reshape(H, 5, 128, 6 * 128)


def _dft_consts(h):
    c = np.arange(128)
    ang = 2 * np.pi * np.outer(c, c) / 128
    dftc = (np.stack([np.cos(ang), -np.sin(ang)], 1) / np.sqrt(128)).astype(np.float32)
    lin = np.arange(SEQ)
    lout = np.arange(h * NLAT, (h + 1) * NLAT)
    ang = 2 * np.pi * (np.outer(lin, lout) % SEQ) / SEQ
    tabs = (np.stack([np.cos(ang), np.sin(ang)]) / np.sqrt(SEQ)).astype(np.float32)
    dftl = np.ascontiguousarray(tabs.reshape(2, 4, 4, 128, NLAT).transpose(0, 1, 3, 2, 4)).reshape(2, 4, 128, 4 * NLAT).astype(NPBF)
    lin = np.arange(CTX)
    lout = np.arange(h * NCTX, (h + 1) * NCTX)
    ang = 2 * np.pi * (np.outer(lin, lout) % CTX) / CTX
    tx = (np.stack([np.cos(ang), np.sin(ang)]) / np.sqrt(CTX)).astype(np.float32)
    dftx = np.ascontiguousarray(tx.reshape(2, 2, 128, NCTX).transpose(0, 2, 1, 3)).reshape(2, 128, 2 * NCTX).astype(NPBF)
    return dftc, dftl, dftx


_PROG_CACHE = {}


def _get_prog(stages, cfg):
    key = (tuple(stages), tuple(sorted(cfg.items())))
    if key not in _PROG_CACHE:
        b = Builder(list(stages), dict(cfg))
        nc = b.build()
        _PROG_CACHE[key] = (nc, b)
    return _PROG_CACHE[key]


def _run(stages, cfg, in_maps):
    nc, b = _get_prog(stages, cfg)
    maps = []
    for m in in_maps:
        mm = {}
        for name, (shape, dt) in b.dram_in.items():
            a = m[name]
            want = NPBF if dt == BF16 else np.float32
            a = np.ascontiguousarray(a)
            assert a.shape == shape, (name, a.shape, shape)
            assert a.dtype == want, (name, a.dtype)
            mm[name] = a
        maps.append(mm)
    res = run_bass_kernel_spmd(nc, maps, core_ids=list(range(8)))
    return res.results


def _layer_inputs_A(l, W, cfg, cores):
    KC, H, G = cfg["D"] // 128, cfg["H"], cfg["G"]
    D, DFF, NQ = cfg["D"], cfg["DFF"], cfg["NQ"]
    FQ = DFF // 128 // NQ
    sh = {}
    sh[f"wmod{l}"] = _wtile(W["w_mod"][l], 128, KC)
    sh[f"bmod{l}"] = np.ascontiguousarray(np.repeat(_vec_fm(W["b_mod"][l], 9 * KC)[:, :, None], 2, 2))
    sh.update(_ffn_w(l, 1, W, cfg))
    w_in = W["w_in"][l]
    NA = H * 128
    qkf = np.concatenate([w_in[:, :2 * NA], w_in[:, 3 * NA:]], 1)
    sh[f"winqkf{l}"] = _wtile(qkf, 128, KC)
    vw = 128
    sh[f"winv{l}"] = _wtile(w_in[:, 2 * NA:3 * NA], vw, KC)
    sh[f"qkg{l}"] = np.ascontiguousarray(np.stack([W["q_norm_w"][l], W["k_norm_w"][l]], 1))
    sh.update(_norm_w(l, W, cfg))
    per = []
    for (b, h) in cores:
        m = dict(sh)
        cc = np.stack([W["c"][b], W["c_ctx"]], 1)
        m[f"cT{l}"] = np.ascontiguousarray(cc.reshape(KC, 128, 2).transpose(1, 0, 2))
        per.append(m)
    return per


def _norm_w(l, W, cfg):
    KC = cfg["D"] // 128
    nw = np.stack([_vec_fm(W["norm_w"][l, s], KC) for s in range(3)], 1)
    return {f"nw{l}": np.ascontiguousarray(np.repeat(nw[:, :, :, None], 2, 3))}


def _ffn_w(l, which, W, cfg):
    KC, DFF, NQ = cfg["D"] // 128, cfg["DFF"], cfg["NQ"]
    FC = DFF // 128
    FQ = FC // NQ
    wi = W[f"ffn{which}_wi"][l]
    g = wi[:, :DFF].reshape(KC, 128, FC, 128)
    u = wi[:, DFF:].reshape(KC, 128, FC, 128)
    gu = np.stack([g, u], 3)
    wi_r = np.ascontiguousarray(gu.transpose(2, 3, 1, 0, 4)).reshape(FC * 2, 128, KC * 128)
    wo = W[f"ffn{which}_wo"][l]
    wo_r = np.ascontiguousarray(wo.reshape(NQ, FQ, 128, KC, 128).transpose(0, 3, 2, 1, 4)).reshape(NQ * KC, 128, FQ * 128)
    return {f"wi{which}_{l}": wi_r, f"wo{which}_{l}": wo_r}


def _layer_inputs_B(l, W, cfg, cores, dev):
    KC, H, G = cfg["D"] // 128, cfg["H"], cfg["G"]
    sh = {}
    sh[f"wfour{l}"] = np.ascontiguousarray(W["w_four"][l])
    sh[f"wout{l}"] = _wtile(W["w_out"][l], 128, KC)
    sh.update(_ffn_w(l, 2, W, cfg))
    per = []
    for ci, (b, h) in enumerate(cores):
        m = dict(sh)
        c0, c1 = 2 * b, 2 * b + 1
        m[f"qown{l}"] = dev[ci][f"q{l}"]
        m[f"kown{l}"] = dev[ci][f"k{l}"]
        m[f"vown{l}"] = dev[ci][f"v{l}"]
        m[f"kpair{l}"] = np.stack([dev[c0][f"k{l}"], dev[c1][f"k{l}"]])
        m[f"vpair{l}"] = np.stack([dev[c0][f"v{l}"], dev[c1][f"v{l}"]])
        m[f"fpair{l}"] = np.stack([dev[c0][f"f{l}"], dev[c1][f"f{l}"]])
        m[f"bias{l}"] = _bias_tables(W["rpb"][l], h, H)
        dftc, dftl, dftx = _dft_consts(h)
        m["dftc"], m["dftl"], m["dftx"] = dftc, dftl, dftx
        per.append(m)
    return per


def _run_n(stages, cfg, in_maps, n):
    nc, b = _get_prog(stages, cfg)
    maps = []
    for m in in_maps:
        mm = {}
        for name, (shape, dt) in b.dram_in.items():
            a = np.ascontiguousarray(m[name])
            want = NPBF if dt == BF16 else np.float32
            assert a.shape == shape, (name, a.shape, shape)
            assert a.dtype == want, (name, a.dtype)
            mm[name] = a
        maps.append(mm)
    res = run_bass_kernel_spmd(nc, maps, core_ids=list(range(n)))
    return res.results


def kernel(**inputs):
    cfg = dict(CFG)
    W = {k: np.asarray(v, np.float32) for k, v in inputs.items()}
    D, H, DEPTH = cfg["D"], cfg["H"], cfg["DEPTH"]
    KC = D // 128
    sh = {}
    for l in range(DEPTH):
        a = _layer_inputs_A(l, W, cfg, [(0, 0)])[0]
        ct = a.pop(f"cT{l}")
        sh.update(a)
        sh[f"wfour{l}"] = np.ascontiguousarray(W["w_four"][l])
        sh[f"wout{l}"] = _wtile(W["w_out"][l], 128, KC)
        sh.update(_ffn_w(l, 2, W, cfg))
        for h in range(2):
            sh[f"bias{l}_{h}"] = _bias_tables(W["rpb"][l], h, H)
    for h in range(2):
        dftc, dftl, dftx = _dft_consts(h)
        sh["dftc"], sh[f"dftl_{h}"], sh[f"dftx_{h}"] = dftc, dftl, dftx
    per = []
    for b in range(4):
        m = dict(sh)
        cc = np.stack([W["c"][b], W["c_ctx"]], 1)
        for l in range(DEPTH):
            m[f"cT{l}"] = np.ascontiguousarray(cc.reshape(KC, 128, 2).transpose(1, 0, 2))
        halves = []
        for h in range(2):
            tok = np.concatenate([W["x"][b, h * NLAT:(h + 1) * NLAT], W["ctx"][b, h * NCTX:(h + 1) * NCTX]], 0)
            halves.append(_fm(tok, KC))
        m["xT_in"] = np.stack(halves)
        per.append(m)
    real = [0, 1, 2, 3]
    r = _run_n((("FUSED", 0),), cfg, per, 4)
    out = np.zeros((4, SEQ, D), np.float32)
    for b in range(4):
        y = r[real[b]]["y_out"]
        for h in range(2):
            out[b, h * NLAT:(h + 1) * NLAT] = y[h].transpose(2, 1, 0).reshape(NLAT, D)
    return out


def kernel_unfused(**inputs):
    cfg = dict(CFG)
    W = {k: np.asarray(v, np.float32) for k, v in inputs.items()}
    D = cfg["D"]
    KC = D // 128
    cores = [(b, h) for b in range(4) for h in range(2)]
    per = _layer_inputs_A(0, W, cfg, cores)
    for ci, (b, h) in enumerate(cores):
        tok = np.concatenate([W["x"][b, h * NLAT:(h + 1) * NLAT], W["ctx"][b, h * NCTX:(h + 1) * NCTX]], 0)
        per[ci]["xT_in"] = _fm(tok, KC)
    r0 = _run((("A", 0),), cfg, per)
    perB = _layer_inputs_B(0, W, cfg, cores, r0)
    perA = _layer_inputs_A(1, W, cfg, cores)
    nw0 = _norm_w(0, W, cfg)
    for ci in range(8):
        perB[ci].update(perA[ci])
        perB[ci].update(nw0)
        perB[ci]["xT_in"] = r0[ci]["xT_out"]
        perB[ci]["modT_in"] = r0[ci]["modT_out"]
    r1 = _run((("B", 0), ("A", 1)), cfg, perB)
    del perB, perA
    perB = _layer_inputs_B(1, W, cfg, cores, r1)
    nw1 = _norm_w(1, W, cfg)
    for ci in range(8):
        perB[ci].update(nw1)
        perB[ci]["xT_in"] = r1[ci]["xT_out"]
        perB[ci]["modT_in"] = r1[ci]["modT_out"]
    r2 = _run((("B", 1),), cfg, perB)
    out = np.zeros((4, SEQ, D), np.float32)
    for ci, (b, h) in enumerate(cores):
        y = r2[ci]["y_out"]
        out[b, h * NLAT:(h + 1) * NLAT] = y.transpose(2, 1, 0).reshape(NLAT, D)
    return out
```
